# Optimizing a Trainium2 kernel written in Bass

```python
import jax, jax.numpy as jnp
from jax import lax
import numpy as np

D_MODEL = 1024
BATCH = 2
SEQ = 8192
DEPTH = 1
DEC_BATCH = 32
DEC_SEQ = 64
PAST_LEN = 1024

CHUNK = 64
N_HEADS = 8
N_KV_HEADS = 2
HEAD_DIM = 64
GROUP = N_HEADS // N_KV_HEADS
IDX_HEADS = 8
IDX_DIM = 64
TOPK_MAX = 256
Q_BLOCK = 128
CONV_CH = 512
CONV_WIDTH = 31
D_FF = 2816
ROPE_THETA = 10000.0
EPS = 1e-6

Q_COLS = N_HEADS * HEAD_DIM
KV_COLS = N_KV_HEADS * HEAD_DIM
IDXQ_COLS = IDX_HEADS * IDX_DIM
GATE_COLS = 2 * D_MODEL
SPLIT_SIZES = (Q_COLS, KV_COLS, KV_COLS, IDXQ_COLS, IDX_DIM, IDX_HEADS, 2 * CONV_CH, GATE_COLS)
IN_COLS = sum(SPLIT_SIZES)

kernel_name = "dsa_conformer_gated_streaming_step"


def _rms(x, g):
    x32 = x.astype(jnp.float32)
    y = x32 * lax.rsqrt(jnp.mean(x32 * x32, axis=-1, keepdims=True) + EPS)
    return (y * g.astype(jnp.float32)).astype(x.dtype)


def _layernorm(x, g, b):
    x32 = x.astype(jnp.float32)
    mu = jnp.mean(x32, axis=-1, keepdims=True)
    var = jnp.mean(jnp.square(x32 - mu), axis=-1, keepdims=True)
    y = (x32 - mu) * lax.rsqrt(var + EPS)
    return (y * g.astype(jnp.float32) + b.astype(jnp.float32)).astype(x.dtype)


def _rope(x, pos):
    d = x.shape[-1]
    inv = ROPE_THETA ** (-jnp.arange(0, d, 2, dtype=jnp.float32) / d)
    ang = pos.astype(jnp.float32)[:, None] * inv[None, :]
    cos = jnp.cos(ang)[None, :, None, :]
    sin = jnp.sin(ang)[None, :, None, :]
    x32 = x.astype(jnp.float32)
    x1, x2 = x32[..., : d // 2], x32[..., d // 2:]
    out = jnp.concatenate([x1 * cos - x2 * sin, x2 * cos + x1 * sin], axis=-1)
    return out.astype(x.dtype)


def _swiglu(x, g, w_in, w_out):
    a, b = jnp.split(_rms(x, g) @ w_in, 2, axis=-1)
    return (jax.nn.silu(a) * b) @ w_out


def _dsa_attention(q, k, v, q_idx, k_idx, w_idx, q_pos):
    B, T = q.shape[0], q.shape[1]
    L = k.shape[1]
    top = min(TOPK_MAX, L // 4)
    qb = min(Q_BLOCK, T)
    nb = T // qb
    key_pos = jnp.arange(L, dtype=jnp.int32)

    def blocks(a):
        return jnp.moveaxis(a.reshape((B, nb, qb) + a.shape[2:]), 1, 0)

    def one_block(args):
        qq, qi, wi, pos = args
        limit = (pos // CHUNK + 1) * CHUNK
        admissible = key_pos[None, :] < limit[:, None]
        s = jnp.einsum('bqhd,bsd->bqhs', qi, k_idx).astype(jnp.float32)
        score = jnp.einsum('bqh,bqhs->bqs', wi.astype(jnp.float32), jax.nn.relu(s))
        score = jnp.where(admissible[None], score, -jnp.inf)
        _, sel = lax.top_k(score, top)
        valid = sel < limit[None, :, None]
        kg = jax.vmap(lambda kb, ib: kb[ib])(k, sel)
        vg = jax.vmap(lambda vb, ib: vb[ib])(v, sel)
        qg = qq.reshape(B, qb, N_KV_HEADS, GROUP, HEAD_DIM)
        logits = jnp.einsum('bqcgd,bqncd->bqcgn', qg, kg).astype(jnp.float32) * (HEAD_DIM ** -0.5)
        logits = jnp.where(valid[:, :, None, None, :], logits, -jnp.inf)
        p = jax.nn.softmax(logits, axis=-1).astype(vg.dtype)
        o = jnp.einsum('bqcgn,bqncd->bqcgd', p, vg)
        return o.reshape(B, qb, N_HEADS * HEAD_DIM)

    out = lax.map(one_block, (blocks(q), blocks(q_idx), blocks(w_idx), q_pos.reshape(nb, qb)))
    return jnp.moveaxis(out, 0, 1).reshape(B, T, N_HEADS * HEAD_DIM)


def _layer(x, past_k, past_v, past_idx_k, past_conv,
           ffn1_norm, ffn1_w_in, ffn1_w_out, mix_norm, w_in, b_gate,
           conv_w, conv_b, conv_ln_g, conv_ln_b, conv_w_out, attn_w_out, w_out,
           ffn2_norm, ffn2_w_in, ffn2_w_out):
    B, T, _ = x.shape
    P = past_k.shape[1]
    pos = P + jnp.arange(T, dtype=jnp.int32)

    h = x + 0.5 * _swiglu(x, ffn1_norm, ffn1_w_in, ffn1_w_out)
    u = _rms(h, mix_norm)
    z = u @ w_in
    points = np.cumsum(SPLIT_SIZES)[:-1].tolist()
    q, k, v, qi, ki, wi, conv_in, gate = jnp.split(z, points, axis=-1)

    q = _rope(q.reshape(B, T, N_HEADS, HEAD_DIM), pos)
    k = _rope(k.reshape(B, T, N_KV_HEADS, HEAD_DIM), pos)
    v = v.reshape(B, T, N_KV_HEADS, HEAD_DIM)
    qi = _rope(qi.reshape(B, T, IDX_HEADS, IDX_DIM), pos)
    ki = _rope(ki.reshape(B, T, 1, IDX_DIM), pos)[:, :, 0]
    wi = wi * (IDX_HEADS ** -0.5)
    k_all = jnp.concatenate([past_k, k], axis=1)
    v_all = jnp.concatenate([past_v, v], axis=1)
    ki_all = jnp.concatenate([past_idx_k, ki], axis=1)
    attn = _dsa_attention(q, k_all, v_all, qi, ki_all, wi, pos)

    ca, cb = jnp.split(conv_in, 2, axis=-1)
    c = ca * jax.nn.sigmoid(cb)
    c_pad = jnp.concatenate([past_conv, c], axis=1)
    dc = lax.conv_general_dilated(c_pad, conv_w[:, None, :], window_strides=(1,), padding='VALID',
                                  dimension_numbers=('NWC', 'WIO', 'NWC'),
                                  feature_group_count=CONV_CH) + conv_b
    conv_out = jax.nn.silu(_layernorm(dc, conv_ln_g, conv_ln_b)) @ conv_w_out

    g_a, g_c = jnp.split(jax.nn.sigmoid(gate + b_gate), 2, axis=-1)
    merged = g_a * (attn @ attn_w_out) + g_c * conv_out
    h = h + merged @ w_out
    h = h + 0.5 * _swiglu(h, ffn2_norm, ffn2_w_in, ffn2_w_out)
    return h, k, v, ki, c_pad[:, -(CONV_WIDTH - 1):]


def setup_inputs(seed: int = 0) -> dict:
    key = jax.random.key(seed)
    ks = jax.random.split(key, 32)
    f32 = jnp.float32

    def nrm(i, shape, scale):
        return jax.random.normal(ks[i], shape, f32) * scale

    def gain(i, shape):
        return 1.0 + 0.02 * jax.random.normal(ks[i], shape, f32)

    return {
        "x_prompt": nrm(0, (BATCH, SEQ, D_MODEL), 1.0),
        "x_sample": nrm(1, (DEC_BATCH, DEC_SEQ, D_MODEL), 1.0),
        "cache_k": nrm(2, (DEPTH, DEC_BATCH, PAST_LEN, N_KV_HEADS, HEAD_DIM), 1.0),
        "cache_v": nrm(3, (DEPTH, DEC_BATCH, PAST_LEN, N_KV_HEADS, HEAD_DIM), 1.0),
        "cache_idx_k": nrm(4, (DEPTH, DEC_BATCH, PAST_LEN, IDX_DIM), 1.0),
        "state_conv": nrm(5, (DEPTH, DEC_BATCH, CONV_WIDTH - 1, CONV_CH), 0.5),
        "ffn1_norm": gain(6, (DEPTH, D_MODEL)),
        "ffn1_w_in": nrm(7, (DEPTH, D_MODEL, 2 * D_FF), D_MODEL ** -0.5),
        "ffn1_w_out": nrm(8, (DEPTH, D_FF, D_MODEL), D_FF ** -0.5),
        "mix_norm": gain(9, (DEPTH, D_MODEL)),
        "w_in": nrm(10, (DEPTH, D_MODEL, IN_COLS), D_MODEL ** -0.5),
        "b_gate": nrm(11, (DEPTH, GATE_COLS), 0.01),
        "conv_w": nrm(12, (DEPTH, CONV_WIDTH, CONV_CH), CONV_WIDTH ** -0.5),
        "conv_b": nrm(13, (DEPTH, CONV_CH), 0.01),
        "conv_ln_g": gain(14, (DEPTH, CONV_CH)),
        "conv_ln_b": nrm(15, (DEPTH, CONV_CH), 0.01),
        "conv_w_out": nrm(16, (DEPTH, CONV_CH, D_MODEL), CONV_CH ** -0.5),
        "attn_w_out": nrm(17, (DEPTH, Q_COLS, D_MODEL), Q_COLS ** -0.5),
        "w_out": nrm(18, (DEPTH, D_MODEL, D_MODEL), D_MODEL ** -0.5),
        "ffn2_norm": gain(19, (DEPTH, D_MODEL)),
        "ffn2_w_in": nrm(20, (DEPTH, D_MODEL, 2 * D_FF), D_MODEL ** -0.5),
        "ffn2_w_out": nrm(21, (DEPTH, D_FF, D_MODEL), D_FF ** -0.5),
        "final_norm": gain(22, (D_MODEL,)),
    }


def reference(x_prompt, x_sample, cache_k, cache_v, cache_idx_k, state_conv,
              ffn1_norm, ffn1_w_in, ffn1_w_out, mix_norm, w_in, b_gate,
              conv_w, conv_b, conv_ln_g, conv_ln_b, conv_w_out, attn_w_out, w_out,
              ffn2_norm, ffn2_w_in, ffn2_w_out, final_norm):
    hp, hs = x_prompt, x_sample
    Bp = x_prompt.shape[0]
    dt = x_prompt.dtype
    kp_l, vp_l, ip_l, cp_l = [], [], [], []
    ks_l, vs_l, is_l, cs_l = [], [], [], []
    for l in range(DEPTH):
        w = (ffn1_norm[l], ffn1_w_in[l], ffn1_w_out[l], mix_norm[l], w_in[l], b_gate[l],
             conv_w[l], conv_b[l], conv_ln_g[l], conv_ln_b[l], conv_w_out[l], attn_w_out[l], w_out[l],
             ffn2_norm[l], ffn2_w_in[l], ffn2_w_out[l])
        hp, kp, vp, ip, cp = _layer(
            hp,
            jnp.zeros((Bp, 0, N_KV_HEADS, HEAD_DIM), dt),
            jnp.zeros((Bp, 0, N_KV_HEADS, HEAD_DIM), dt),
            jnp.zeros((Bp, 0, IDX_DIM), dt),
            jnp.zeros((Bp, CONV_WIDTH - 1, CONV_CH), dt),
            *w)
        hs, ks_, vs_, is_, cs_ = _layer(hs, cache_k[l], cache_v[l], cache_idx_k[l], state_conv[l], *w)
        kp_l.append(kp); vp_l.append(vp); ip_l.append(ip); cp_l.append(cp)
        ks_l.append(ks_); vs_l.append(vs_); is_l.append(is_); cs_l.append(cs_)
    y_prompt = _rms(hp, final_norm)
    y_sample = _rms(hs, final_norm)
    return (y_prompt, y_sample,
            jnp.stack(kp_l), jnp.stack(vp_l), jnp.stack(ip_l), jnp.stack(cp_l),
            jnp.stack(ks_l), jnp.stack(vs_l), jnp.stack(is_l), jnp.stack(cs_l))
```

```python
import os
import numpy as np
from contextlib import ExitStack
import concourse.bass as bass
import concourse.mybir as mybir
from concourse.bass_utils import run_bass_kernel_spmd

F32 = mybir.dt.float32
BF16 = mybir.dt.bfloat16
I32 = mybir.dt.int32
AF = mybir.ActivationFunctionType
ALU = mybir.AluOpType
AX = mybir.AxisListType

D = 1024
DFF = 2816
NT = 18
NTOK = NT * 128
NPT = 16
BLK = 9
NSUB = 3
SUB = 384
TBK = BLK * 128
EPS = 1e-6
BIG = 1.0e30
NITER = 14
TOPK = 256.0
INCOLS = 4424


class TBuf:
    def __init__(self, name, t=None):
        self.name = name
        self.t = t
        self.w = {}
        self.r = {}
        self.dsem = None
        self.dcount = 0


class Prog:
    ENGS = ("tensor", "vector", "scalar", "gpsimd", "sync")

    def __init__(self, nc):
        self.nc = nc
        self.stacks = [ExitStack()]
        self.sems = {}
        self.count = {}
        self.waited = {}
        self.drams = {}
        self.dma_bufs = []
        self.uid = 0

    def __enter__(self):
        self.stacks[0].__enter__()
        for e in self.ENGS:
            self.sems[e] = self.stacks[0].enter_context(self.nc.semaphore("sem_" + e))
            self.count[e] = 0
            self.waited[e] = {}
        return self

    def __exit__(self, *a):
        while len(self.stacks) > 1:
            self.stacks.pop().close()
        return self.stacks[0].__exit__(*a)

    def push(self):
        st = ExitStack()
        st.__enter__()
        self.stacks.append(st)

    def pop(self):
        self.barrier()
        self.stacks.pop().close()

    def sb(self, name, shape, dtype):
        self.uid += 1
        return TBuf(name, self.stacks[-1].enter_context(self.nc.sbuf_tensor(f"{name}_{self.uid}", list(shape), dtype)))

    def ps(self, name, shape, dtype):
        return TBuf(name, self.stacks[0].enter_context(self.nc.psum_tensor(name, list(shape), dtype)))

    def dram(self, name):
        if name not in self.drams:
            self.drams[name] = TBuf("d_" + name)
        return self.drams[name]

    def _eng(self, name):
        return getattr(self.nc, name)

    def _wait(self, engname, key, val):
        if key == engname and engname == "tensor":
            return
        w = self.waited[engname]
        if w.get(key, 0) >= val:
            return
        if key in self.ENGS:
            assert self.count[key] >= val, f"wait on un-signalled instruction {key} {val} {self.count[key]}"
        w[key] = val
        self._eng(engname).wait_ge(self.sems[key], val)

    def _deps(self, engname, reads, writes):
        for b in reads:
            for k, v in b.w.items():
                self._wait(engname, k, v)
        for b in writes:
            for k, v in b.w.items():
                self._wait(engname, k, v)
            for k, v in b.r.items():
                self._wait(engname, k, v)

    @staticmethod
    def _rec(d, key, val):
        if d.get(key, 0) < val:
            d[key] = val

    def op(self, engname, fn, reads=(), writes=(), signal=True):
        self._deps(engname, reads, writes)
        inst = fn(self._eng(engname))
        if signal:
            self.count[engname] += 1
            inst.then_inc(self.sems[engname], 1)
            val = self.count[engname]
        else:
            val = self.count[engname] + 1
        for b in reads:
            self._rec(b.r, engname, val)
        for b in writes:
            self._rec(b.w, engname, val)
        return inst

    def _dsem(self, b0):
        if b0.dsem is None:
            key = "dma_" + b0.name + str(len(self.dma_bufs))
            b0.dsem = key
            self.sems[key] = self.stacks[0].enter_context(self.nc.semaphore("s" + str(len(self.dma_bufs))))
            self.dma_bufs.append(b0)
        return b0.dsem

    def dma(self, queue, out_ap, in_ap, reads=(), writes=(), **kw):
        self._deps(queue, reads, writes)
        b0 = writes[0]
        key = self._dsem(b0)
        inst = self._eng(queue).dma_start(out=out_ap, in_=in_ap, **kw)
        b0.dcount += 16
        inst.then_inc(self.sems[key], 16)
        for b in reads:
            self._rec(b.r, key, b0.dcount)
        for b in writes:
            self._rec(b.w, key, b0.dcount)
        return inst

    def collective(self, fn, reads, writes):
        self._deps("gpsimd", reads, writes)
        b0 = writes[0]
        key = self._dsem(b0)
        inst = fn(self.nc.gpsimd)
        b0.dcount += 1
        inst.then_inc(self.sems[key])
        for b in reads:
            self._rec(b.r, key, b0.dcount)
        for b in writes:
            self._rec(b.w, key, b0.dcount)

    def sync_to(self, engname, bufs):
        self._deps(engname, (), bufs)

    def barrier(self):
        for e in self.ENGS:
            for e2 in self.ENGS:
                if e2 != e and self.count[e2] > 0:
                    self._wait(e, e2, self.count[e2])
            for b in self.dma_bufs:
                if b.dcount:
                    self._wait(e, b.dsem, b.dcount)

    def finish(self):
        for b in self.dma_bufs:
            if b.dcount:
                self._wait("sync", b.dsem, b.dcount)
        for e in self.ENGS:
            if e != "sync" and self.count[e] > 0:
                self._wait("sync", e, self.count[e])


class Rot:
    def __init__(self, bufs):
        self.bufs = bufs
        self.i = 0

    def next(self):
        b = self.bufs[self.i % len(self.bufs)]
        self.i += 1
        return b


def build_program(stage=3):
    nc = bass.Bass("TRN2", target_bir_lowering=False)

    def din(name, shape, dt=F32):
        return nc.dram_tensor(name, list(shape), dt, kind="ExternalInput").ap()

    def dout(name, shape, dt=F32):
        return nc.dram_tensor(name, list(shape), dt, kind="ExternalOutput").ap()

    def dscr(name, shape, dt):
        return nc.dram_tensor(name, list(shape), dt).ap()

    x_d = din("x", [NTOK, D])
    ck_d = din("ck", [4, 1024, 128])
    cv_d = din("cv", [4, 1024, 128])
    cik_d = din("cik", [4, 1024, 64])
    sconv_d = din("sconv", [4, 30, 512])
    g1_d = din("g1", [128, D])
    gm_d = din("gm", [128, D])
    g2_d = din("g2", [128, D])
    gf_d = din("gf", [128, D])
    w1i_d = din("w1i", [D, 2 * DFF])
    w1o_d = din("w1o", [DFF, D])
    win_d = din("win", [D, INCOLS])
    bg_d = din("bgT", [128, 16])
    cw_d = din("cwT", [128, 4, 31])
    cb_d = din("cbT", [128, 4])
    lg_d = din("lngT", [128, 4])
    lb_d = din("lnbT", [128, 4])
    wco_d = din("wco", [512, D])
    wao_d = din("wao", [512, D])
    wo_d = din("wo", [D, D])
    w2i_d = din("w2i", [D, 2 * DFF])
    w2o_d = din("w2o", [DFF, D])
    cs_d = din("cs", [128, NT, 64])
    lim_d = din("limrel", [128, 1])
    sel_d = din("sel", [128, 5])

    y_d = dout("y", [NTOK, D])
    nk_d = dout("nk", [NTOK, 128])
    nv_d = dout("nv", [NTOK, 128])
    nki_d = dout("nki", [NTOK, 64])
    ncp_d = dout("ncp", [30, 512])
    ncs_d = dout("ncs", [4, 30, 512])

    h_s = dscr("h_s", [NTOK, D], F32)
    gate_s = dscr("gate_s", [128, 16, NTOK], BF16)
    qT_s = dscr("qT_s", [128, 4, NTOK], BF16)
    qiT_s = dscr("qiT_s", [64, 8, NTOK], BF16)
    wi_s = dscr("wi_s", [NTOK, 8], F32)
    cT_s = dscr("cT_s", [128, 4, NTOK], F32)
    mT_s = dscr("mT_s", [128, 8, NTOK], BF16)
    kvmy = dscr("kvmy", [192, NPT * 128], BF16)
    kvmyv = dscr("kvmyv", [128, NPT * 128], BF16)
    kvsm = dscr("kvsm", [320, 256], BF16)
    ctmy = dscr("ctmy", [128, 4 * NPT * 32], F32)
    kvg = dscr("kvg", [4 * 192, NPT * 128], BF16)
    kvgv = dscr("kvgv", [4 * 128, NPT * 128], BF16)
    ctg = dscr("ctg", [4 * 128, 4 * NPT * 32], F32)

    P = Prog(nc)
    with P:
        D_ = P.dram
        psb = [P.ps(f"ps{i}", [128, 512], F32) for i in range(8)]

        def bfv(i):
            return psb[i].t[:].bitcast(BF16)

        ident = P.sb("ident", [128, 128], BF16)
        identF = P.sb("identF", [128, 128], F32)
        it = P.sb("iota_tmp", [128, 128], I32)
        P.op("gpsimd", lambda e: e.iota(it.t[:], pattern=[[1, 128]], base=0, channel_multiplier=-1), writes=[it])
        P.op("vector", lambda e: e.tensor_scalar(out=ident.t[:], in0=it.t[:], scalar1=0.0, scalar2=None, op0=ALU.is_equal),
             reads=[it], writes=[ident])
        P.op("vector", lambda e: e.tensor_scalar(out=identF.t[:], in0=it.t[:], scalar1=0.0, scalar2=None, op0=ALU.is_equal),
             reads=[it], writes=[identF])
        cs = P.sb("cs", [128, NT, 64], F32)
        P.dma("sync", cs.t[:], cs_d, writes=[cs])
        bgT = P.sb("bgT", [128, 16], F32)
        P.dma("sync", bgT.t[:], bg_d, writes=[bgT])
        cwT = P.sb("cwT", [128, 4, 31], F32)
        P.dma("sync", cwT.t[:], cw_d, writes=[cwT])
        cbT = P.sb("cbT", [128, 4], F32)
        P.dma("sync", cbT.t[:], cb_d, writes=[cbT])
        lngT = P.sb("lngT", [128, 4], F32)
        P.dma("sync", lngT.t[:], lg_d, writes=[lngT])
        lnbT = P.sb("lnbT", [128, 4], F32)
        P.dma("sync", lnbT.t[:], lb_d, writes=[lnbT])
        limrel = P.sb("limrel", [128, 1], F32)
        P.dma("sync", limrel.t[:], lim_d, writes=[limrel])
        sel = P.sb("sel", [128, 5], F32)
        P.dma("sync", sel.t[:], sel_d, writes=[sel])
        epsc = P.sb("epsc", [128, 1], F32)
        P.op("vector", lambda e: e.memset(epsc.t[:], EPS), writes=[epsc])

        ssq_r = Rot([P.sb(f"ssq{i}", [128, 1], F32) for i in range(2)])
        rstd_r = Rot([P.sb(f"rstd{i}", [128, 1], F32) for i in range(2)])
        sh = {}

        def alloc_norm_tmp():
            sh["xs_r"] = Rot([P.sb(f"xs{i}", [128, D], BF16) for i in range(2)])
            sh["junk"] = P.sb("junk", [128, D], BF16)

        def rstd_of(src_ap, srcbuf):
            ssq = ssq_r.next()
            rstd = rstd_r.next()
            junk = sh["junk"]
            P.op("scalar", lambda e: e.activation(out=junk.t[:], in_=src_ap, func=AF.Square, accum_out=ssq.t[:, 0:1]),
                 reads=[srcbuf], writes=[junk, ssq])
            P.op("scalar", lambda e: e.activation(out=rstd.t[:], in_=ssq.t[:], func=AF.Sqrt, bias=epsc.t[:, 0:1], scale=1.0 / D),
                 reads=[ssq, epsc], writes=[rstd])
            P.op("vector", lambda e: e.reciprocal(out=rstd.t[:], in_=rstd.t[:]), reads=[rstd], writes=[rstd])
            return rstd

        def norm_transpose(src_ap, srcbuf, gB, dstT, col0, pbank):
            rstd = rstd_of(src_ap, srcbuf)
            xs = sh["xs_r"].next()
            P.op("vector", lambda e: e.scalar_tensor_tensor(out=xs.t[:], in0=src_ap, scalar=rstd.t[:, 0:1], in1=gB.t[:],
                                                            op0=ALU.mult, op1=ALU.mult),
                 reads=[srcbuf, rstd, gB], writes=[xs])
            pv = bfv(pbank)
            for kc in range(8):
                P.op("tensor", lambda e: e.transpose(out=pv[:, kc * 128:(kc + 1) * 128], in_=xs.t[:, kc * 128:(kc + 1) * 128],
                                                     identity=ident.t[:]),
                     reads=[xs, ident], writes=[psb[pbank]], signal=(kc == 7))
            P.op("scalar", lambda e: e.activation(out=dstT.t[:, :, col0:col0 + 128],
                                                  in_=pv.rearrange("p (k n) -> p k n", k=8), func=AF.Copy),
                 reads=[psb[pbank]], writes=[dstT])

        class Streamer:
            def __init__(self, maxel):
                self.stage = [P.sb(f"stg{i}", [128, maxel], F32) for i in range(2)]
                self.wb = [P.sb(f"wbf{i}", [128, maxel], BF16) for i in range(2)]
                self.n = 0
                self.nd = 0

            def load(self, pieces, KC, ncols, dst=None):
                s = self.n % 2
                self.n += 1
                stg = self.stage[s]
                sv = stg.t[:, 0:KC * ncols].rearrange("p (k n) -> p k n", k=KC)
                for (off, ap, w) in pieces:
                    P.dma("sync", sv[:, :, off:off + w], ap, writes=[stg])
                if dst is None:
                    wbuf = self.wb[self.nd % 2]
                    self.nd += 1
                    ov = wbuf.t[:, 0:KC * ncols].rearrange("p (k n) -> p k n", k=KC)
                else:
                    wbuf, ov = dst
                if self.n % 2 == 0:
                    P.op("scalar", lambda e: e.activation(out=ov, in_=sv, func=AF.Copy), reads=[stg], writes=[wbuf])
                else:
                    P.op("vector", lambda e: e.tensor_copy(out=ov, in_=sv), reads=[stg], writes=[wbuf])
                return wbuf, ov

        sA_r = None
        oT_r = None

        def ffn_in(ST, xnT, gT, w_in_ap):
            Wv = w_in_ap.rearrange("(kc p) n -> p kc n", p=128)

            def load(j):
                return ST.load([(0, Wv[:, :, j * 128:(j + 1) * 128], 128),
                                (128, Wv[:, :, DFF + j * 128:DFF + (j + 1) * 128], 128)], 8, 256)
            nxt = load(0)
            cnt = 0
            for j in range(22):
                wbuf, wv = nxt
                if j + 1 < 22:
                    nxt = load(j + 1)
                for sbk in range(NSUB):
                    pa = psb[cnt % 2]
                    pb = psb[2 + cnt % 2]
                    cnt += 1
                    cols = slice(sbk * SUB, (sbk + 1) * SUB)
                    for kc in range(8):
                        P.op("tensor", lambda e: e.matmul(pa.t[:, 0:SUB], lhsT=wv[:, kc, 0:128], rhs=xnT.t[:, kc, cols],
                                                          start=(kc == 0), stop=(kc == 7)),
                             reads=[wbuf, xnT], writes=[pa], signal=(kc == 7))
                    for kc in range(8):
                        P.op("tensor", lambda e: e.matmul(pb.t[:, 0:SUB], lhsT=wv[:, kc, 128:256], rhs=xnT.t[:, kc, cols],
                                                          start=(kc == 0), stop=(kc == 7)),
                             reads=[wbuf, xnT], writes=[pb], signal=(kc == 7))
                    sA = sA_r.next()
                    P.op("scalar", lambda e: e.activation(out=sA.t[:, 0:SUB], in_=pa.t[:, 0:SUB], func=AF.Silu),
                         reads=[pa], writes=[sA])
                    P.op("vector", lambda e: e.tensor_tensor(out=gT.t[:, j, cols], in0=pb.t[:, 0:SUB], in1=sA.t[:, 0:SUB], op=ALU.mult),
                         reads=[pb, sA], writes=[gT])

        def proj_back(ST, srcT, KC, w_ap, hblk, scale):
            Wv = w_ap.rearrange("(kc p) n -> p kc n", p=128)

            def load(n):
                return ST.load([(0, Wv[:, :, n * 128:(n + 1) * 128], 128)], KC, 128)
            nxt = load(0)
            cnt = 0
            for n in range(8):
                wbuf, wv = nxt
                if n + 1 < 8:
                    nxt = load(n + 1)
                for sbk in range(NSUB):
                    po = psb[4 + cnt % 2]
                    pt = psb[6 + cnt % 2]
                    cnt += 1
                    cols = slice(sbk * SUB, (sbk + 1) * SUB)
                    for kc in range(KC):
                        P.op("tensor", lambda e: e.matmul(po.t[:, 0:SUB], lhsT=wv[:, kc, :], rhs=srcT.t[:, kc, cols],
                                                          start=(kc == 0), stop=(kc == KC - 1)),
                             reads=[wbuf, srcT], writes=[po], signal=(kc == KC - 1))
                    oT = oT_r.next()
                    P.op("scalar", lambda e: e.activation(out=oT.t[:, 0:SUB], in_=po.t[:, 0:SUB], func=AF.Copy),
                         reads=[po], writes=[oT])
                    for t3 in range(3):
                        P.op("tensor", lambda e: e.transpose(out=pt.t[:, t3 * 128:(t3 + 1) * 128], in_=oT.t[:, t3 * 128:(t3 + 1) * 128],
                                                             identity=identF.t[:]),
                             reads=[oT, identF], writes=[pt], signal=(t3 == 2))
                    hv = hblk.t[:, sbk * 3:sbk * 3 + 3, n * 128:(n + 1) * 128]
                    P.op("vector", lambda e: e.scalar_tensor_tensor(out=hv, in0=pt.t[:, 0:SUB].rearrange("p (t n) -> p t n", t=3),
                                                                    scalar=float(scale), in1=hv, op0=ALU.mult, op1=ALU.add),
                         reads=[pt, hblk], writes=[hblk])

        P.push()
        alloc_norm_tmp()
        g1B = P.sb("g1B", [128, D], F32)
        P.dma("sync", g1B.t[:], g1_d, writes=[g1B])
        gmB = P.sb("gmB", [128, D], F32)
        P.dma("sync", gmB.t[:], gm_d, writes=[gmB])
        xnT = P.sb("xnT", [128, 8, TBK], BF16)
        gT = P.sb("gT", [128, 22, TBK], BF16)
        hblk = P.sb("hblk", [128, BLK, D], F32)
        ST = Streamer(2816)
        sA_r = Rot([P.sb(f"sA{i}", [128, SUB], F32) for i in range(2)])
        oT_r = Rot([P.sb(f"oT{i}", [128, SUB], F32) for i in range(2)])
        wtok = gT.t[:].rearrange("p a b -> p (a b)")[:, 0:8 * 1352].rearrange("p (k n) -> p k n", k=8)
        ropeT = Rot([P.sb(f"ropeT{i}", [128, 8, 32], F32) for i in range(8)])
        qb2_r = Rot([P.sb(f"qb2{i}", [128, 512], BF16) for i in range(2)])
        qib_r = Rot([P.sb(f"qib{i}", [128, 512], BF16) for i in range(2)])
        kf_r = Rot([P.sb(f"kf{i}", [128, 128], F32) for i in range(2)])
        vf_r = Rot([P.sb(f"vf{i}", [128, 128], F32) for i in range(2)])
        kif_r = Rot([P.sb(f"kif{i}", [128, 64], F32) for i in range(2)])
        kb_r = Rot([P.sb(f"kb{i}", [128, 128], BF16) for i in range(2)])
        vb_r = Rot([P.sb(f"vb{i}", [128, 128], BF16) for i in range(2)])
        kib_r = Rot([P.sb(f"kib{i}", [128, 64], BF16) for i in range(2)])
        wis_r = Rot([P.sb(f"wis{i}", [128, 8], F32) for i in range(2)])
        qTt_r = Rot([P.sb(f"qTt{i}", [128, 4, 128], BF16) for i in range(2)])
        kTt_r = Rot([P.sb(f"kTt{i}", [128, 128], BF16) for i in range(2)])
        qiTt_r = Rot([P.sb(f"qiTt{i}", [64, 8, 128], BF16) for i in range(2)])
        kiTt_r = Rot([P.sb(f"kiTt{i}", [64, 128], BF16) for i in range(2)])
        cbuf_r = Rot([P.sb(f"cbuf{i}", [128, SUB], F32) for i in range(2)])
        gbuf_r = Rot([P.sb(f"gbuf{i}", [128, SUB], BF16) for i in range(2)])

        def rope(src4, srcbuf, tg, o1, o2, obuf, shape):
            x1 = src4[:, :, :, 0:32]
            x2 = src4[:, :, :, 32:64]
            a, b = shape
            cosB = cs.t[:, tg, 0:32].unsqueeze(1).unsqueeze(1).broadcast_to([128, a, b, 32])
            sinB = cs.t[:, tg, 32:64].unsqueeze(1).unsqueeze(1).broadcast_to([128, a, b, 32])
            ts = [ropeT.next() for _ in range(4)]
            tv = [t.t[:, 0:a * b, :].rearrange("p (a b) d -> p a b d", a=a) for t in ts]
            for (tb, tvv, xx, cc) in ((ts[0], tv[0], x1, cosB), (ts[1], tv[1], x2, sinB), (ts[2], tv[2], x2, cosB), (ts[3], tv[3], x1, sinB)):
                P.op("vector", lambda e: e.tensor_tensor(out=tvv, in0=xx, in1=cc, op=ALU.mult), reads=[srcbuf, cs], writes=[tb])
            P.op("gpsimd", lambda e: e.tensor_tensor(out=o1, in0=tv[0], in1=tv[1], op=ALU.subtract), reads=[ts[0], ts[1]], writes=[obuf])
            P.op("gpsimd", lambda e: e.tensor_tensor(out=o2, in0=tv[2], in1=tv[3], op=ALU.add), reads=[ts[2], ts[3]], writes=[obuf])

        WinV = win_d.rearrange("(kc p) n -> p kc n", p=128)
        groups = [[0, 1, 2, 3], [4, 5, 6, 7]]
        for blk in range(2):
            tok0 = blk * TBK
            for tt in range(BLK):
                P.dma("sync", hblk.t[:, tt, :], x_d[tok0 + tt * 128: tok0 + (tt + 1) * 128, :], writes=[hblk])
            if stage == 0.05:
                P.barrier(); P.finish(); return nc
            for tt in range(BLK):
                norm_transpose(hblk.t[:, tt, :], hblk, g1B, xnT, tt * 128, 6 + tt % 2)
            if stage == 0.1:
                P.barrier(); P.finish(); return nc
            ffn_in(ST, xnT, gT, w1i_d)
            if stage == 0.2:
                P.barrier(); P.finish(); return nc
            proj_back(ST, gT, 22, w1o_d, hblk, 0.5)
            if stage == 0.3:
                P.barrier(); P.finish(); return nc
            for tt in range(BLK):
                norm_transpose(hblk.t[:, tt, :], hblk, gmB, xnT, tt * 128, 6 + tt % 2)
            for tt in range(BLK):
                P.dma("sync", h_s[tok0 + tt * 128: tok0 + (tt + 1) * 128, :], hblk.t[:, tt, :], reads=[hblk], writes=[D_("h_s")])
            wtok_pieces = []
            c0 = 0
            while c0 < 1352:
                w = min(256, 1352 - c0)
                wtok_pieces.append((c0, w))
                c0 += w

            def load_wtok(n):
                for _ in range(n):
                    if wtok_pieces:
                        c0_, w_ = wtok_pieces.pop(0)
                        ST.load([(0, WinV[:, :, c0_:c0_ + w_], w_)], 8, w_, dst=(gT, wtok[:, :, c0_:c0_ + w_]))
            if stage == 0.7:
                P.barrier(); P.finish(); return nc
            def loadc(cc):
                return ST.load([(0, WinV[:, :, 1352 + cc * 128:1352 + (cc + 1) * 128], 128),
                                (128, WinV[:, :, 1864 + cc * 128:1864 + (cc + 1) * 128], 128)], 8, 256)
            nxt = loadc(0)
            cnt = 0
            for cc in range(4):
                wbuf, wv = nxt
                if cc + 1 < 4:
                    nxt = loadc(cc + 1)
                load_wtok(2)
                for sbk in range(NSUB):
                    pa = psb[cnt % 2]
                    pb = psb[2 + cnt % 2]
                    cnt += 1
                    cols = slice(sbk * SUB, (sbk + 1) * SUB)
                    for kc in range(8):
                        P.op("tensor", lambda e: e.matmul(pa.t[:, 0:SUB], lhsT=wv[:, kc, 0:128], rhs=xnT.t[:, kc, cols],
                                                          start=(kc == 0), stop=(kc == 7)),
                             reads=[wbuf, xnT], writes=[pa], signal=(kc == 7))
                    for kc in range(8):
                        P.op("tensor", lambda e: e.matmul(pb.t[:, 0:SUB], lhsT=wv[:, kc, 128:256], rhs=xnT.t[:, kc, cols],
                                                          start=(kc == 0), stop=(kc == 7)),
                             reads=[wbuf, xnT], writes=[pb], signal=(kc == 7))
                    sA = sA_r.next()
                    P.op("scalar", lambda e: e.activation(out=sA.t[:, 0:SUB], in_=pb.t[:, 0:SUB], func=AF.Sigmoid), reads=[pb], writes=[sA])
                    cbuf = cbuf_r.next()
                    P.op("vector", lambda e: e.tensor_tensor(out=cbuf.t[:], in0=pa.t[:, 0:SUB], in1=sA.t[:, 0:SUB], op=ALU.mult),
                         reads=[pa, sA], writes=[cbuf])
                    P.dma("gpsimd", cT_s[:, cc, tok0 + sbk * SUB: tok0 + (sbk + 1) * SUB], cbuf.t[:], reads=[cbuf], writes=[D_("cT_s")])
                    for t3 in range(3):
                        tg = blk * BLK + sbk * 3 + t3
                        if tg < NPT:
                            o = (cc * NPT + tg) * 32
                            P.dma("gpsimd", ctmy[:, o:o + 32], cbuf.t[:, t3 * 128 + 96: t3 * 128 + 128], reads=[cbuf], writes=[D_("ctmy")])
            load_wtok(6)
            def zmm(tt):
                s3 = 3 * (tt % 2)
                for (bk, o0, a0, a1) in ((s3, 0, 0, 512), (s3 + 1, 0, 512, 768), (s3 + 1, 256, 1280, 1352), (s3 + 2, 0, 768, 1280)):
                    for kc in range(8):
                        P.op("tensor", lambda e: e.matmul(psb[bk].t[:, o0:o0 + a1 - a0], lhsT=xnT.t[:, kc, tt * 128:(tt + 1) * 128],
                                                          rhs=wtok[:, kc, a0:a1], start=(kc == 0), stop=(kc == 7)),
                             reads=[xnT, gT], writes=[psb[bk]], signal=(kc == 7))

            def post(tt):
                tg = blk * BLK + tt
                trow = slice(tok0 + tt * 128, tok0 + (tt + 1) * 128)
                tcol = trow
                s3 = 3 * (tt % 2)
                bq, bkv, bqi = psb[s3], psb[s3 + 1], psb[s3 + 2]
                qb2 = qb2_r.next()
                zq = bq.t[:, 0:512].rearrange("p (c g d) -> p c g d", c=2, g=4)
                oq = qb2.t[:].rearrange("p (g c d) -> p c g d", g=4, c=2)
                rope(zq, bq, tg, oq[:, :, :, 0:32], oq[:, :, :, 32:64], qb2, (2, 4))
                kf = kf_r.next()
                zk = bkv.t[:, 0:128].rearrange("p (a c d) -> p a c d", a=1, c=2)
                ok = kf.t[:].rearrange("p (a c d) -> p a c d", a=1, c=2)
                rope(zk, bkv, tg, ok[:, :, :, 0:32], ok[:, :, :, 32:64], kf, (1, 2))
                vf = vf_r.next()
                vb = vb_r.next()
                P.op("vector", lambda e: e.tensor_copy(out=vf.t[:], in_=bkv.t[:, 128:256]), reads=[bkv], writes=[vf])
                P.op("vector", lambda e: e.tensor_copy(out=vb.t[:], in_=bkv.t[:, 128:256]), reads=[bkv], writes=[vb])
                qib = qib_r.next()
                zqi = bqi.t[:, 0:512].rearrange("p (a h d) -> p a h d", a=1, h=8)
                oqi = qib.t[:].rearrange("p (a h d) -> p a h d", a=1, h=8)
                rope(zqi, bqi, tg, oqi[:, :, :, 0:32], oqi[:, :, :, 32:64], qib, (1, 8))
                kif = kif_r.next()
                zki = bkv.t[:, 256:320].rearrange("p (a h d) -> p a h d", a=1, h=1)
                oki = kif.t[:].rearrange("p (a h d) -> p a h d", a=1, h=1)
                rope(zki, bkv, tg, oki[:, :, :, 0:32], oki[:, :, :, 32:64], kif, (1, 1))
                wis = wis_r.next()
                P.op("vector", lambda e: e.tensor_scalar(out=wis.t[:], in0=bkv.t[:, 320:328], scalar1=float(8 ** -0.5), scalar2=None, op0=ALU.mult),
                     reads=[bkv], writes=[wis])
                kb = kb_r.next()
                kib = kib_r.next()
                P.op("scalar", lambda e: e.activation(out=kb.t[:], in_=kf.t[:], func=AF.Copy), reads=[kf], writes=[kb])
                P.op("scalar", lambda e: e.activation(out=kib.t[:], in_=kif.t[:], func=AF.Copy), reads=[kif], writes=[kib])
                p6 = bfv(6)
                for g in range(4):
                    P.op("tensor", lambda e: e.transpose(out=p6[:, g * 128:(g + 1) * 128], in_=qb2.t[:, g * 128:(g + 1) * 128], identity=ident.t[:]),
                         reads=[qb2, ident], writes=[psb[6]], signal=False)
                P.op("tensor", lambda e: e.transpose(out=p6[:, 512:640], in_=kb.t[:], identity=ident.t[:]),
                     reads=[kb, ident], writes=[psb[6]], signal=False)
                P.op("tensor", lambda e: e.transpose(out=p6[0:64, 640:768], in_=kib.t[:], identity=ident.t[:]),
                     reads=[kib, ident], writes=[psb[6]])
                p7 = bfv(7)
                for h in range(8):
                    P.op("tensor", lambda e: e.transpose(out=p7[0:64, h * 128:(h + 1) * 128], in_=qib.t[:, h * 64:(h + 1) * 64], identity=ident.t[:]),
                         reads=[qib, ident], writes=[psb[7]], signal=(h == 7))
                qTt = qTt_r.next()
                kTt = kTt_r.next()
                qiTt = qiTt_r.next()
                kiTt = kiTt_r.next()
                P.op("scalar", lambda e: e.activation(out=qTt.t[:], in_=p6[:, 0:512].rearrange("p (g n) -> p g n", g=4), func=AF.Copy),
                     reads=[psb[6]], writes=[qTt])
                P.op("scalar", lambda e: e.activation(out=kTt.t[:], in_=p6[:, 512:640], func=AF.Copy), reads=[psb[6]], writes=[kTt])
                P.op("scalar", lambda e: e.activation(out=kiTt.t[:], in_=p6[0:64, 640:768], func=AF.Copy), reads=[psb[6]], writes=[kiTt])
                P.op("vector", lambda e: e.tensor_copy(out=qiTt.t[:], in_=p7[0:64, :].rearrange("p (h n) -> p h n", h=8)),
                     reads=[psb[7]], writes=[qiTt])
                P.dma("gpsimd", nk_d[trow, :], kf.t[:], reads=[kf], writes=[D_("nk")])
                P.dma("gpsimd", nv_d[trow, :], vf.t[:], reads=[vf], writes=[D_("nv")])
                P.dma("gpsimd", nki_d[trow, :], kif.t[:], reads=[kif], writes=[D_("nki")])
                P.dma("gpsimd", wi_s[trow, :], wis.t[:], reads=[wis], writes=[D_("wi_s")])
                P.dma("scalar", qT_s[:, :, tcol], qTt.t[:], reads=[qTt], writes=[D_("qT_s")])
                P.dma("gpsimd", qiT_s[:, :, tcol], qiTt.t[:], reads=[qiTt], writes=[D_("qiT_s")])
                if tg < NPT:
                    kcol = slice(tg * 128, (tg + 1) * 128)
                    P.dma("scalar", kvmy[0:64, kcol], kiTt.t[:], reads=[kiTt], writes=[D_("kvmy")])
                    P.dma("scalar", kvmy[64:192, kcol], kTt.t[:], reads=[kTt], writes=[D_("kvmy")])
                    P.dma("gpsimd", kvmyv[:, kcol], vb.t[:], reads=[vb], writes=[D_("kvmyv")])
                else:
                    kcol = slice((tg - NPT) * 128, (tg - NPT + 1) * 128)
                    P.dma("scalar", kvsm[0:64, kcol], kiTt.t[:], reads=[kiTt], writes=[D_("kvsm")])
                    P.dma("scalar", kvsm[64:192, kcol], kTt.t[:], reads=[kTt], writes=[D_("kvsm")])
                    P.dma("scalar", kvsm[192:320, kcol], vb.t[:], reads=[vb], writes=[D_("kvsm")])

            if os.environ.get("K_PIPE", "1") == "1":
                zmm(0)
                for tt in range(BLK):
                    if tt + 1 < BLK:
                        zmm(tt + 1)
                    post(tt)
            else:
                for tt in range(BLK):
                    zmm(tt)
                    post(tt)
            if blk == 1:
                P.collective(lambda e: e.collective_compute("AllGather", ALU.bypass, replica_groups=groups,
                                                            ins=[kvmy.opt()], outs=[kvg.opt()]),
                             reads=[D_("kvmy")], writes=[D_("kvg")])
                P.collective(lambda e: e.collective_compute("AllGather", ALU.bypass, replica_groups=groups,
                                                            ins=[kvmyv.opt()], outs=[kvgv.opt()]),
                             reads=[D_("kvmyv")], writes=[D_("kvgv")])
                P.collective(lambda e: e.collective_compute("AllGather", ALU.bypass, replica_groups=groups,
                                                            ins=[ctmy.opt()], outs=[ctg.opt()]),
                             reads=[D_("ctmy")], writes=[D_("ctg")])
            if stage == 0.8:
                P.barrier(); P.finish(); return nc
            def loadg(pp):
                return ST.load([(0, WinV[:, :, 2376 + pp * 256:2376 + (pp + 1) * 256], 256)], 8, 256)
            nxt = loadg(0)
            cnt = 0
            for pp in range(8):
                wbuf, wv = nxt
                if pp + 1 < 8:
                    nxt = loadg(pp + 1)
                for hh in range(2):
                    gc = pp * 2 + hh
                    for sbk in range(NSUB):
                        pa = psb[cnt % 4]
                        cnt += 1
                        cols = slice(sbk * SUB, (sbk + 1) * SUB)
                        for kc in range(8):
                            P.op("tensor", lambda e: e.matmul(pa.t[:, 0:SUB], lhsT=wv[:, kc, hh * 128:(hh + 1) * 128], rhs=xnT.t[:, kc, cols],
                                                              start=(kc == 0), stop=(kc == 7)),
                                 reads=[wbuf, xnT], writes=[pa], signal=(kc == 7))
                        gbuf = gbuf_r.next()
                        P.op("scalar", lambda e: e.activation(out=gbuf.t[:], in_=pa.t[:, 0:SUB], func=AF.Sigmoid, bias=bgT.t[:, gc:gc + 1], scale=1.0),
                             reads=[pa, bgT], writes=[gbuf])
                        P.dma("gpsimd", gate_s[:, gc, tok0 + sbk * SUB: tok0 + (sbk + 1) * SUB], gbuf.t[:], reads=[gbuf], writes=[D_("gate_s")])
        P.pop()
        if stage == 1:
            P.finish()
            return nc

        if stage == 1.1:
            P.barrier(); P.finish(); return nc
        P.push()
        wao = P.sb("wao", [128, 4, D], BF16)
        wco = P.sb("wco", [128, 4, D], BF16)
        amneg = P.sb("amneg", [128, 512], F32)
        ampos = P.sb("ampos", [128, 512], F32)
        P.push()
        ST = Streamer(2048)
        for (wt_, wd_) in ((wao, wao_d), (wco, wco_d)):
            Wv = wd_.rearrange("(kc p) n -> p kc n", p=128)
            for hf in range(2):
                ST.load([(0, Wv[:, :, hf * 512:(hf + 1) * 512], 512)], 4, 512, dst=(wt_, wt_.t[:, :, hf * 512:(hf + 1) * 512]))
        iot_i = P.sb("iot_i", [128, 512], I32)
        P.op("gpsimd", lambda e: e.iota(iot_i.t[:], pattern=[[1, 512]], base=0, channel_multiplier=0), writes=[iot_i])
        am = P.sb("am", [128, 512], F32)
        P.op("vector", lambda e: e.tensor_scalar(out=am.t[:], in0=iot_i.t[:], scalar1=limrel.t[:, 0:1], scalar2=None, op0=ALU.is_ge),
             reads=[iot_i, limrel], writes=[am])
        P.op("vector", lambda e: e.tensor_scalar(out=amneg.t[:], in0=am.t[:], scalar1=-BIG, scalar2=None, op0=ALU.mult), reads=[am], writes=[amneg])
        P.op("vector", lambda e: e.tensor_scalar(out=ampos.t[:], in0=am.t[:], scalar1=BIG, scalar2=None, op0=ALU.mult), reads=[am], writes=[ampos])
        P.pop()
        if stage == 1.2:
            P.barrier(); P.finish(); return nc
        kiT_all = P.sb("kiT_all", [128, 8192], BF16)
        P.op("vector", lambda e: e.memset(kiT_all.t[64:128, :], 0.0), writes=[kiT_all])
        kT_all = P.sb("kT_all", [128, 8192], BF16)
        v1_all = P.sb("v1_all", [128, 64, 2, 65], BF16)
        score = P.sb("score", [128, 8192], F32)
        score_b = [TBuf(f"score_b{i}") for i in range(16)]
        msk2 = [P.sb(f"msk{i}", [128, 8192], BF16) for i in range(2)]
        zeros = P.sb("zeros", [1, 512], BF16)
        P.op("vector", lambda e: e.memset(zeros.t[:], 0.0), writes=[zeros])
        onesM = P.sb("onesM", [128, 128], F32)
        P.op("vector", lambda e: e.memset(onesM.t[:], 1.0 / 512.0), writes=[onesM])

        qiTu_r = Rot([P.sb(f"qiTu{i}", [128, 8, 128], BF16) for i in range(2)])
        for b_ in qiTu_r.bufs:
            P.op("gpsimd", lambda e: e.memset(b_.t[64:128, :, :], 0.0), writes=[b_])
        wiu_r = Rot([P.sb(f"wiu{i}", [128, 8], F32) for i in range(2)])
        qTz_r = Rot([[P.sb(f"qTz{i}_{c}", [128, 4, 128], BF16) for c in range(2)] for i in range(2)])
        for pr in qTz_r.bufs:
            P.op("gpsimd", lambda e: e.memset(pr[0].t[64:128, :, :], 0.0), writes=[pr[0]])
            P.op("gpsimd", lambda e: e.memset(pr[1].t[0:64, :, :], 0.0), writes=[pr[1]])
        ident4 = P.sb("ident4", [128, 4, 128], BF16)
        for g_ in range(4):
            P.op("gpsimd", lambda e: e.tensor_copy(out=ident4.t[:, g_, :], in_=ident.t[:]), reads=[ident], writes=[ident4])
        gateu_r = Rot([P.sb(f"gateu{i}", [128, 16, 128], BF16) for i in range(3)])
        cpad_r = Rot([P.sb(f"cpad{i}", [128, 4, 158], F32) for i in range(3)])
        cand_r = Rot([P.sb(f"cand{i}", [128, 5, 4, 32], F32) for i in range(2)])
        E_r = Rot([P.sb(f"E{i}", [128, 512], BF16) for i in range(3)])
        fvec = P.sb("fvec", [128, NITER + 1], F32)
        for k_ in range(NITER + 1):
            P.op("vector", lambda e: e.memset(fvec.t[:, k_:k_ + 1], float(2.0 ** -(k_ + 1))), writes=[fvec])
        frng = P.sb("frng", [128, NITER + 1], F32)
        nfrng = P.sb("nfrng", [128, NITER + 1], F32)
        sm = {n: P.sb("sm_" + n, [128, 1], F32) for n in ("rmax", "rmin", "mind", "lo", "rng", "mid", "cnt", "ind", "nmid", "cntA", "tcn")}
        rec = P.sb("rec", [128, 8], F32)
        attn_tok = P.sb("attn_tok", [128, 512], BF16)
        attnT = P.sb("attnT", [128, 4, 128], BF16)
        t1 = P.sb("t1", [128, 8, 128], F32)
        t2 = P.sb("t2", [128, 8, 128], F32)
        mTu_r = Rot([P.sb(f"mTu{i}", [128, 8, 128], BF16) for i in range(2)])
        cacc_r = Rot([P.sb(f"cacc{i}", [128, 4, 128], F32) for i in range(3)])
        tmpc = P.sb("tmpc", [128, 4, 128], F32)
        dcen = P.sb("dcen", [128, 4, 128], F32)
        dsq = P.sb("dsq", [128, 4, 128], F32)
        rsb = P.sb("rsb", [128, 128], F32)
        actT = P.sb("actT", [128, 4, 128], BF16)
        ncv = P.sb("ncv", [30, 512], F32)
        scst = ncv

        P.op("vector", lambda e: e.memset(v1_all.t[:], 1.0), writes=[v1_all])
        kiv = kiT_all.t[0:64, :].rearrange("p (i j n) -> p i j n", i=16, j=4)
        ktv = kT_all.t[:].rearrange("p (i j n) -> p i j n", i=16, j=4)
        v1v = v1_all.t[:].rearrange("p (i j) c d -> p i j c d", i=16, j=4)
        for jj in range(4):
            r0 = jj * 192
            P.dma("sync", kiv[:, :, jj, :], kvg[r0:r0 + 64, :].rearrange("p (i n) -> p i n", i=16), reads=[D_("kvg")], writes=[kiT_all])
            P.dma("sync", ktv[:, :, jj, :], kvg[r0 + 64:r0 + 192, :].rearrange("p (i n) -> p i n", i=16), reads=[D_("kvg")], writes=[kT_all])
            for c in range(2):
                P.dma("sync", v1v[:, :, jj, c, 0:64],
                      kvgv[jj * 128:(jj + 1) * 128, :].rearrange("p (i c d) -> p i c d", i=16, c=2)[:, :, c, :],
                      reads=[D_("kvgv")], writes=[v1_all])

        if stage == 1.3:
            P.barrier(); P.finish(); return nc
        kst_v = score.t[:, 4096:5120].rearrange("p (t d) -> p t d", t=8)
        kstb_v = score.t[:, 5120:5632].bitcast(BF16).rearrange("p (t d) -> p t d", t=8)
        kstT = TBuf("kstT")
        kstbT = TBuf("kstbT")

        def sample_prep(s, cx):
            kiB, kTB, v1B, kc0, kt0 = cx["kiB"], cx["kTB"], cx["v1B"], cx["kc0"], cx["kt0"]
            p7 = bfv(7)
            P.dma("sync", kst_v[:, :, 0:64], cik_d[s].rearrange("(t p) d -> p t d", p=128), writes=[kstT])
            P.op("gpsimd", lambda e: e.tensor_copy(out=kstb_v[:, :, 0:64], in_=kst_v[:, :, 0:64]), reads=[kstT], writes=[kstbT])
            for t8 in range(8):
                P.op("tensor", lambda e: e.transpose(out=p7[0:64, t8 * 128:(t8 + 1) * 128], in_=kstb_v[:, t8, 0:64], identity=ident.t[:]),
                     reads=[kstbT, ident], writes=[psb[7]], signal=(t8 == 7))
            P.op("scalar", lambda e: e.activation(out=kiT_all.t[0:64, kc0:kc0 + 1024], in_=p7[0:64, :], func=AF.Copy), reads=[psb[7]], writes=[kiB])
            P.dma("sync", kst_v, ck_d[s].rearrange("(t p) d -> p t d", p=128), writes=[kstT])
            P.op("gpsimd", lambda e: e.tensor_copy(out=kstb_v, in_=kst_v), reads=[kstT], writes=[kstbT])
            for t8 in range(8):
                P.op("tensor", lambda e: e.transpose(out=p7[:, t8 * 128:(t8 + 1) * 128], in_=kstb_v[:, t8, :], identity=ident.t[:]),
                     reads=[kstbT, ident], writes=[psb[7]], signal=(t8 == 7))
            P.op("scalar", lambda e: e.activation(out=kT_all.t[:, kc0:kc0 + 1024], in_=p7[:, :], func=AF.Copy), reads=[psb[7]], writes=[kTB])
            P.dma("sync", kst_v, cv_d[s].rearrange("(t p) d -> p t d", p=128), writes=[kstT])
            P.op("gpsimd", lambda e: e.tensor_copy(out=v1_all.t[:, kt0:kt0 + 8, :, 0:64], in_=kst_v.rearrange("p t (c d) -> p t c d", c=2)),
                 reads=[kstT], writes=[v1B])
            sc = slice(s * 64, (s + 1) * 64)
            P.dma("sync", kiT_all.t[0:64, kc0 + 1024:kc0 + 1088], kvsm[0:64, sc], reads=[D_("kvsm")], writes=[kiB])
            P.dma("sync", kT_all.t[:, kc0 + 1024:kc0 + 1088], kvsm[64:192, sc], reads=[D_("kvsm")], writes=[kTB])
            vr = 192 + (s % 2) * 64
            vc = slice((s // 2) * 128, (s // 2 + 1) * 128)
            P.dma("sync", v1_all.t[0:64, kt0 + 8, :, 0:64], kvsm[vr:vr + 64, vc].rearrange("p (c d) -> p c d", c=2),
                  reads=[D_("kvsm")], writes=[v1B])

        def front(cx):
            tile, col0, nq, L, prompt_slot, seq = cx["args"]
            tok0 = tile * 128 + col0
            tcol = slice(tok0, tok0 + nq)
            nblk = (L + 511) // 512
            qiTu = qiTu_r.next()
            wiu = wiu_r.next()
            qTz = qTz_r.next()
            gateu = gateu_r.next()
            cpad = cpad_r.next()
            cacc = cacc_r.next()
            mk = msk2[cx["idx"] % 2]
            kiB, kc0 = cx.get("kiB", kiT_all), cx.get("kc0", 0)
            if seq is not None:
                sample_prep(seq, cx)
            cx.update(qTz=qTz, gateu=gateu, cpad=cpad, msk=mk, tcol=tcol, cacc=cacc)
            P.dma("sync", qiTu.t[0:64, :, 0:nq], qiT_s[:, :, tcol], reads=[D_("qiT_s")], writes=[qiTu])
            P.dma("sync", wiu.t[0:nq, :], wi_s[tcol, :], reads=[D_("wi_s")], writes=[wiu])
            P.dma("sync", qTz[0].t[0:64, :, 0:nq], qT_s[0:64, :, tcol], reads=[D_("qT_s")], writes=[qTz[0]])
            P.dma("sync", qTz[1].t[64:128, :, 0:nq], qT_s[64:128, :, tcol], reads=[D_("qT_s")], writes=[qTz[1]])
            P.dma("sync", gateu.t[:, :, 0:nq], gate_s[:, :, tcol], reads=[D_("gate_s")], writes=[gateu])
            P.dma("sync", cpad.t[:, :, 30:30 + nq], cT_s[:, :, tcol], reads=[D_("cT_s")], writes=[cpad])
            if prompt_slot is not None:
                i = prompt_slot
                cand = cand_r.next()
                cx["cand"] = cand
                ctv = ctg.rearrange("(r p) (c i n) -> p r c i n", r=4, c=4, i=NPT)
                for r in range(4):
                    P.dma("sync", cand.t[:, r, :, :], ctv[:, r, :, i, :], reads=[D_("ctg")], writes=[cand])
                if i > 0:
                    P.dma("sync", cand.t[:, 4, :, :], ctv[:, 3, :, i - 1, :], reads=[D_("ctg")], writes=[cand])
            else:
                P.dma("sync", scst.t[:], sconv_d[seq], writes=[scst])
            if prompt_slot is not None:
                i = prompt_slot
                pass
                nk_ = 5 if i > 0 else 4
                P.op("vector", lambda e: e.tensor_scalar(out=cpad.t[:, :, 0:30], in0=cand.t[:, 0, :, 2:32], scalar1=sel.t[:, 0:1], scalar2=None,
                                                         op0=ALU.mult), reads=[cand, sel], writes=[cpad])
                for k in range(1, nk_):
                    P.op("vector", lambda e: e.scalar_tensor_tensor(out=cpad.t[:, :, 0:30], in0=cand.t[:, k, :, 2:32], scalar=sel.t[:, k:k + 1],
                                                                    in1=cpad.t[:, :, 0:30], op0=ALU.mult, op1=ALU.add),
                         reads=[cand, sel, cpad], writes=[cpad])
            else:
                for cc in range(4):
                    P.op("tensor", lambda e: e.transpose(out=psb[7].t[:, cc * 32:cc * 32 + 30], in_=scst.t[:, cc * 128:(cc + 1) * 128],
                                                         identity=identF.t[0:30, 0:30]),
                         reads=[scst, identF], writes=[psb[7]], signal=(cc == 3))
                P.op("vector", lambda e: e.tensor_copy(out=cpad.t[:, :, 0:30],
                                                       in_=psb[7].t[:, 0:128].rearrange("p (c n) -> p c n", c=4)[:, :, 0:30]),
                     reads=[psb[7]], writes=[cpad])
            cav = cacc.t[:, :, 0:nq]
            tmv = tmpc.t[:, :, 0:nq]
            P.op("gpsimd", lambda e: e.tensor_tensor(out=cav, in0=cpad.t[:, :, 0:nq], in1=cwT.t[:, :, 0:1].broadcast_to([128, 4, nq]), op=ALU.mult),
                 reads=[cpad, cwT], writes=[cacc])
            P.op("gpsimd", lambda e: e.tensor_tensor(out=cav, in0=cav, in1=cbT.t[:, :].unsqueeze(2).broadcast_to([128, 4, nq]), op=ALU.add),
                 reads=[cacc, cbT], writes=[cacc])
            for k in range(1, 31):
                P.op("gpsimd", lambda e: e.tensor_tensor(out=tmv, in0=cpad.t[:, :, k:k + nq], in1=cwT.t[:, :, k:k + 1].broadcast_to([128, 4, nq]), op=ALU.mult),
                     reads=[cpad, cwT], writes=[tmpc])
                P.op("gpsimd", lambda e: e.tensor_tensor(out=cav, in0=cav, in1=tmv, op=ALU.add), reads=[cacc, tmpc], writes=[cacc])
            yield 0.5
            hb = 0
            for bk in range(nblk):
                cols = min(512, L - bk * 512)
                sc_ap = score.t[0:nq, bk * 512: bk * 512 + cols]
                for h in range(8):
                    pbk = psb[hb % 3]
                    hb += 1
                    P.op("tensor", lambda e: e.matmul(pbk.t[0:nq, 0:cols], lhsT=qiTu.t[:, h, 0:nq], rhs=kiT_all.t[:, kc0 + bk * 512: kc0 + bk * 512 + cols],
                                                      start=True, stop=True),
                         reads=[qiTu, kiB], writes=[pbk])
                    P.op("scalar", lambda e: e.activation(out=pbk.t[0:nq, 0:cols], in_=pbk.t[0:nq, 0:cols], func=AF.Relu),
                         reads=[pbk], writes=[pbk])
                    if h == 0:
                        P.op("scalar", lambda e: e.activation(out=sc_ap, in_=pbk.t[0:nq, 0:cols], func=AF.Copy, scale=wiu.t[0:nq, 0:1]),
                             reads=[pbk, wiu], writes=[score_b[bk]])
                    else:
                        P.op("vector", lambda e: e.scalar_tensor_tensor(out=sc_ap, in0=pbk.t[0:nq, 0:cols], scalar=wiu.t[0:nq, h:h + 1],
                                                                        in1=sc_ap, op0=ALU.mult, op1=ALU.add),
                             reads=[pbk, wiu, score_b[bk]], writes=[score_b[bk]])
                    yield 0.6 * cols / 512
            sbs = score_b[0:nblk]
            cx["sbs"] = sbs
            if prompt_slot is not None:
                i = prompt_slot
                dg = score.t[0:nq, i * 512:(i + 1) * 512]
                tmpd_v = mk.t[:, 0:1024].bitcast(F32)
                P.sync_to("vector", [mk])
                P.op("vector", lambda e: e.tensor_tensor(out=tmpd_v[0:nq, :], in0=dg, in1=ampos.t[0:nq, :], op=ALU.add),
                     reads=[score_b[i], ampos], writes=[mk])
                P.op("vector", lambda e: e.tensor_reduce(out=sm["mind"].t[0:nq, :], in_=tmpd_v[0:nq, :], axis=AX.X, op=ALU.min),
                     reads=[mk], writes=[sm["mind"]])
                P.op("vector", lambda e: e.tensor_tensor(out=dg, in0=dg, in1=amneg.t[0:nq, :], op=ALU.add),
                     reads=[score_b[i], amneg], writes=[score_b[i]])
                if i > 0:
                    mn_in = score.t[0:nq, 0:i * 512].rearrange("p (a b) -> p a b", b=4)[:, :, 0] if i >= 3 else score.t[0:nq, 0:i * 512]
                    P.op("vector", lambda e: e.tensor_reduce(out=sm["rmin"].t[0:nq, :], in_=mn_in, axis=AX.X, op=ALU.min),
                         reads=sbs, writes=[sm["rmin"]])
                    P.op("vector", lambda e: e.tensor_tensor(out=sm["rmin"].t[0:nq, :], in0=sm["rmin"].t[0:nq, :], in1=sm["mind"].t[0:nq, :], op=ALU.min),
                         reads=[sm["rmin"], sm["mind"]], writes=[sm["rmin"]])
                else:
                    P.op("vector", lambda e: e.tensor_copy(out=sm["rmin"].t[0:nq, :], in_=sm["mind"].t[0:nq, :]), reads=[sm["mind"]], writes=[sm["rmin"]])
            else:
                P.op("vector", lambda e: e.tensor_reduce(out=sm["rmin"].t[0:nq, :], in_=score.t[0:nq, 0:L], axis=AX.X, op=ALU.min),
                     reads=sbs, writes=[sm["rmin"]])
            yield 0.55 * nblk
            mx_in = score.t[0:nq, 0:L].rearrange("p (a b) -> p a b", b=4)[:, :, 0] if (prompt_slot is not None and L >= 2048) else score.t[0:nq, 0:L]
            P.op("vector", lambda e: e.tensor_reduce(out=sm["rmax"].t[0:nq, :], in_=mx_in, axis=AX.X, op=ALU.max),
                 reads=sbs, writes=[sm["rmax"]])
            lo, rng, mid, cnt, ind = (sm[n] for n in ("lo", "rng", "mid", "cnt", "ind"))
            P.op("vector", lambda e: e.tensor_scalar(out=lo.t[0:nq, :], in0=sm["rmin"].t[0:nq, :], scalar1=-0.01, scalar2=None, op0=ALU.add),
                 reads=[sm["rmin"]], writes=[lo])
            P.op("vector", lambda e: e.scalar_tensor_tensor(out=rng.t[0:nq, :], in0=sm["rmax"].t[0:nq, :], scalar=0.01, in1=lo.t[0:nq, :],
                                                            op0=ALU.add, op1=ALU.subtract), reads=[sm["rmax"], lo], writes=[rng])
            yield 0.55 * nblk
            yield "PHASE_B"
            La = 0
            Lh = L - La
            mkD, mkA = TBuf("mkD"), TBuf("mkA")
            P.sync_to("vector", [mk])
            if La:
                P.sync_to("scalar", [mk])
            cntA, tcn, dstp = sm["cntA"], sm["tcn"], sm["ind"]
            P.op("vector", lambda e: e.tensor_scalar(out=frng.t[0:nq, :], in0=fvec.t[0:nq, :], scalar1=rng.t[0:nq, 0:1], scalar2=None, op0=ALU.mult),
                 reads=[fvec, rng], writes=[frng])
            P.op("vector", lambda e: e.tensor_scalar(out=nfrng.t[0:nq, :], in0=fvec.t[0:nq, :], scalar1=rng.t[0:nq, 0:1], scalar2=-1.0, op0=ALU.mult, op1=ALU.mult),
                 reads=[fvec, rng], writes=[nfrng])
            P.op("vector", lambda e: e.tensor_tensor(out=mid.t[0:nq, :], in0=lo.t[0:nq, :], in1=frng.t[0:nq, 0:1], op=ALU.add), reads=[lo, frng], writes=[mid])
            for k in range(1, NITER + 1):
                if La:
                    P.op("scalar", lambda e: e.activation(out=mk.t[0:nq, Lh:L], in_=score.t[0:nq, Lh:L], func=AF.Sign, bias=mid.t[0:nq, 0:1], scale=-1.0,
                                                          accum_out=cntA.t[0:nq, 0:1]),
                         reads=sbs + [mid], writes=[mkA, cntA])
                P.op("vector", lambda e: e.tensor_scalar(out=mk.t[0:nq, 0:Lh], in0=score.t[0:nq, 0:Lh], scalar1=mid.t[0:nq, 0:1], scalar2=None,
                                                         op0=ALU.is_ge, op1=ALU.add, accum_out=cnt.t[0:nq, 0:1]),
                     reads=sbs + [mid], writes=[mkD, cnt])
                if La:
                    P.op("vector", lambda e: e.scalar_tensor_tensor(out=tcn.t[0:nq, :], in0=cnt.t[0:nq, :], scalar=2.0, in1=cntA.t[0:nq, :],
                                                                    op0=ALU.mult, op1=ALU.subtract), reads=[cnt, cntA], writes=[tcn])
                    csrc, thr_ = tcn, float(2 * TOPK - 1 - La)
                else:
                    csrc, thr_ = cnt, TOPK - 0.5
                P.op("vector", lambda e: e.scalar_tensor_tensor(out=dstp.t[0:nq, :], in0=csrc.t[0:nq, :], scalar=thr_, in1=frng.t[0:nq, k - 1:k],
                                                                op0=ALU.is_ge, op1=ALU.mult), reads=[csrc, frng], writes=[dstp])
                if k < NITER:
                    P.op("vector", lambda e: e.scalar_tensor_tensor(out=mid.t[0:nq, :], in0=dstp.t[0:nq, :], scalar=nfrng.t[0:nq, k:k + 1], in1=mid.t[0:nq, :],
                                                                    op0=ALU.add, op1=ALU.add), reads=[dstp, nfrng, mid], writes=[mid])
                else:
                    P.op("vector", lambda e: e.scalar_tensor_tensor(out=lo.t[0:nq, :], in0=dstp.t[0:nq, :], scalar=nfrng.t[0:nq, k - 1:k], in1=mid.t[0:nq, :],
                                                                    op0=ALU.add, op1=ALU.add), reads=[dstp, nfrng, mid], writes=[lo])
                yield 0.45 * nblk + 0.5
            P.op("vector", lambda e: e.tensor_scalar(out=mk.t[0:nq, 0:L], in0=score.t[0:nq, 0:L], scalar1=lo.t[0:nq, 0:1], scalar2=-30000.0,
                                                     op0=ALU.is_lt, op1=ALU.mult), reads=sbs + [lo], writes=[mk, mkD, mkA])
            yield 0.55 * nblk

        def back(cx):
            tile, col0, nq, L, prompt_slot, seq = cx["args"]
            qTz, gateu, cpad, mk, tcol, cacc = cx["qTz"], cx["gateu"], cx["cpad"], cx["msk"], cx["tcol"], cx["cacc"]
            KT = (L + 127) // 128
            kTB, v1B, kc0, kt0 = cx.get("kTB", kT_all), cx.get("v1B", v1_all), cx.get("kc0", 0), cx.get("kt0", 0)
            yield 0.5
            accv = [psb[5], psb[7]]
            for c in range(2):
                P.op("tensor", lambda e: e.matmul(accv[c].t[0:nq, 0:260], lhsT=zeros.t[0:1, 0:nq], rhs=zeros.t[0:1, 0:260], start=True, stop=False,
                                                  skip_group_check=True),
                     reads=[zeros], writes=[accv[c]])
            p6 = bfv(7)
            steps = [(kt, c) for kt in range(KT) for c in range(2)]

            def qk(si):
                kt, c = steps[si]
                ks = min(128, L - kt * 128)
                lgp = psb[3 + si % 2]
                P.op("tensor", lambda e: e.matmul(lgp.t[0:ks, 0:4 * nq], lhsT=kT_all.t[:, kc0 + kt * 128: kc0 + kt * 128 + ks],
                                                  rhs=qTz[c].t[:, :, 0:nq], start=True, stop=False),
                     reads=[kTB, qTz[c]], writes=[lgp], signal=False)
                P.op("tensor", lambda e: e.matmul(lgp.t[0:ks, 0:4 * nq], lhsT=mk.t[0:nq, kt * 128: kt * 128 + ks],
                                                  rhs=ident4.t[0:nq, :, 0:nq], start=False, stop=True),
                     reads=[mk, ident4], writes=[lgp])
            qk(0)
            for si, (kt, c) in enumerate(steps):
                ks = min(128, L - kt * 128)
                if si + 1 < len(steps):
                    qk(si + 1)
                lgp = psb[3 + si % 2]
                Eb = E_r.next()
                P.op("scalar", lambda e: e.activation(out=Eb.t[0:ks, 0:4 * nq], in_=lgp.t[0:ks, 0:4 * nq], func=AF.Exp, scale=0.125),
                     reads=[lgp], writes=[Eb])
                for g in range(4):
                    P.op("tensor", lambda e: e.matmul(accv[c].t[0:nq, g * 65:(g + 1) * 65], lhsT=Eb.t[0:ks, g * nq:(g + 1) * nq],
                                                      rhs=v1_all.t[0:ks, kt0 + kt, c, :], start=False, stop=False, skip_group_check=True),
                         reads=[Eb, v1B], writes=[accv[c]], signal=(g == 3))
                if c == 1:
                    yield 2.9 * (nq / 128.0)
            yield "TAIL"
            for c in range(2):
                P.op("tensor", lambda e: e.matmul(accv[c].t[0:nq, 0:260], lhsT=zeros.t[0:1, 0:nq], rhs=zeros.t[0:1, 0:260], start=False, stop=True,
                                                  skip_group_check=True),
                     reads=[zeros], writes=[accv[c]])
            for c in range(2):
                av = accv[c].t[0:nq, 0:260].rearrange("p (g d) -> p g d", g=4)
                P.op("vector", lambda e: e.reciprocal(out=rec.t[0:nq, c * 4:(c + 1) * 4], in_=av[:, :, 64]), reads=[accv[c]], writes=[rec])
                for g in range(4):
                    hh = c * 4 + g
                    P.op("vector", lambda e: e.tensor_scalar(out=attn_tok.t[0:nq, hh * 64:(hh + 1) * 64], in0=av[:, g, 0:64],
                                                             scalar1=rec.t[0:nq, hh:hh + 1], scalar2=None, op0=ALU.mult),
                         reads=[accv[c], rec], writes=[attn_tok])
            yield 1.0
            for kc in range(4):
                P.op("tensor", lambda e: e.transpose(out=p6[:, kc * 128:kc * 128 + nq], in_=attn_tok.t[0:nq, kc * 128:(kc + 1) * 128],
                                                     identity=ident.t[0:nq, 0:nq]),
                     reads=[attn_tok, ident], writes=[psb[7]], signal=(kc == 3))
            P.op("scalar", lambda e: e.activation(out=attnT.t[:, :, 0:nq], in_=p6[:, 0:512].rearrange("p (k n) -> p k n", k=4)[:, :, 0:nq], func=AF.Copy),
                 reads=[psb[7]], writes=[attnT])
            for b4 in range(2):
                for n in range(b4 * 4, b4 * 4 + 4):
                    for kc in range(4):
                        P.op("tensor", lambda e: e.matmul(psb[7].t[:, (n % 4) * 128:(n % 4) * 128 + nq], lhsT=wao.t[:, kc, n * 128:(n + 1) * 128],
                                                          rhs=attnT.t[:, kc, 0:nq], start=(kc == 0), stop=(kc == 3)),
                             reads=[wao, attnT], writes=[psb[7]], signal=(kc == 3))
                P.op("vector", lambda e: e.tensor_tensor(out=t1.t[:, b4 * 4:(b4 + 1) * 4, 0:nq],
                                                         in0=psb[7].t[:, 0:512].rearrange("p (k n) -> p k n", k=4)[:, :, 0:nq],
                                                         in1=gateu.t[:, b4 * 4:(b4 + 1) * 4, 0:nq], op=ALU.mult),
                     reads=[psb[7], gateu], writes=[t1])
                yield 0.8
            pst = psb[7]
            for cc in range(4):
                P.op("tensor", lambda e: e.matmul(pst.t[:, 0:nq], lhsT=onesM.t[:], rhs=cacc.t[:, cc, 0:nq], start=(cc == 0), stop=(cc == 3)),
                     reads=[onesM, cacc], writes=[pst], signal=(cc == 3))
            for cc in range(4):
                P.op("vector", lambda e: e.tensor_tensor(out=dcen.t[:, cc, 0:nq], in0=cacc.t[:, cc, 0:nq], in1=pst.t[:, 0:nq], op=ALU.subtract),
                     reads=[cacc, pst], writes=[dcen])
            P.op("scalar", lambda e: e.activation(out=dsq.t[:, :, 0:nq], in_=dcen.t[:, :, 0:nq], func=AF.Square), reads=[dcen], writes=[dsq])
            for cc in range(4):
                P.op("tensor", lambda e: e.matmul(pst.t[:, 128:128 + nq], lhsT=onesM.t[:], rhs=dsq.t[:, cc, 0:nq], start=(cc == 0), stop=(cc == 3)),
                     reads=[onesM, dsq], writes=[pst], signal=(cc == 3))
            P.op("scalar", lambda e: e.activation(out=rsb.t[:, 0:nq], in_=pst.t[:, 128:128 + nq], func=AF.Sqrt, bias=epsc.t[:, 0:1], scale=1.0),
                 reads=[pst, epsc], writes=[rsb])
            P.op("vector", lambda e: e.reciprocal(out=rsb.t[:, 0:nq], in_=rsb.t[:, 0:nq]), reads=[rsb], writes=[rsb])
            P.op("vector", lambda e: e.tensor_tensor(out=dcen.t[:, :, 0:nq], in0=dcen.t[:, :, 0:nq],
                                                     in1=rsb.t[:, 0:nq].unsqueeze(1).broadcast_to([128, 4, nq]), op=ALU.mult),
                 reads=[dcen, rsb], writes=[dcen])
            for cc in range(4):
                P.op("scalar", lambda e: e.activation(out=actT.t[:, cc, 0:nq], in_=dcen.t[:, cc, 0:nq], func=AF.Silu,
                                                      bias=lnbT.t[:, cc:cc + 1], scale=lngT.t[:, cc:cc + 1]),
                     reads=[dcen, lnbT, lngT], writes=[actT])
            yield 2.0
            mTu = mTu_r.next()
            for b4 in range(2):
                for n in range(b4 * 4, b4 * 4 + 4):
                    for kc in range(4):
                        P.op("tensor", lambda e: e.matmul(psb[7].t[:, (n % 4) * 128:(n % 4) * 128 + nq], lhsT=wco.t[:, kc, n * 128:(n + 1) * 128],
                                                          rhs=actT.t[:, kc, 0:nq], start=(kc == 0), stop=(kc == 3)),
                             reads=[wco, actT], writes=[psb[7]], signal=(kc == 3))
                P.op("vector", lambda e: e.tensor_tensor(out=t2.t[:, b4 * 4:(b4 + 1) * 4, 0:nq],
                                                         in0=psb[7].t[:, 0:512].rearrange("p (k n) -> p k n", k=4)[:, :, 0:nq],
                                                         in1=gateu.t[:, 8 + b4 * 4:8 + (b4 + 1) * 4, 0:nq], op=ALU.mult),
                     reads=[psb[7], gateu], writes=[t2])
                yield 0.8
            P.op("gpsimd", lambda e: e.tensor_tensor(out=mTu.t[:, :, 0:nq], in0=t1.t[:, :, 0:nq], in1=t2.t[:, :, 0:nq], op=ALU.add),
                 reads=[t1, t2], writes=[mTu])
            P.dma("sync", mT_s[:, :, tcol], mTu.t[:, :, 0:nq], reads=[mTu], writes=[D_("mT_s")])
            if seq is not None or prompt_slot == NPT - 1:
                for cc in range(4):
                    P.op("tensor", lambda e: e.transpose(out=psb[7].t[0:30, cc * 128:(cc + 1) * 128], in_=cpad.t[:, cc, nq:nq + 30], identity=identF.t[:]),
                         reads=[cpad, identF], writes=[psb[7]], signal=(cc == 3))
                P.op("vector", lambda e: e.tensor_copy(out=ncv.t[:], in_=psb[7].t[0:30, :]), reads=[psb[7]], writes=[ncv])
                if seq is not None:
                    P.dma("sync", ncs_d[seq], ncv.t[:], reads=[ncv], writes=[D_("ncs")])
                else:
                    P.dma("sync", ncp_d, ncv.t[:], reads=[ncv], writes=[D_("ncp")])
            yield 0.5

        def run_until(gen, marker):
            for v in gen:
                if v == marker:
                    return

        def drain(gen):
            for _ in gen:
                pass

        def interleave(ga, gb):
            ta = tb = 0.0
            a_alive = b_alive = True
            while a_alive or b_alive:
                if a_alive and (not b_alive or ta <= tb):
                    try:
                        ta += next(ga)
                    except StopIteration:
                        a_alive = False
                else:
                    try:
                        tb += next(gb)
                    except StopIteration:
                        b_alive = False

        def until(gen, marker):
            for v in gen:
                if v == marker:
                    return
                yield v if not isinstance(v, str) else 0.0

        def run_pipeline(units, overlap_tail=True):
            cxs = []
            for i, a in enumerate(units):
                cx_ = {"args": a[0:6], "idx": i}
                if len(a) > 6:
                    cx_.update(a[6])
                cxs.append(cx_)
            n = len(cxs)
            fgs = {0: front(cxs[0])}
            drain(until(fgs[0], "PHASE_B"))
            drain(fgs[0])
            tail = None
            for u in range(n):
                bg = back(cxs[u])
                if u + 1 < n:
                    fg = front(cxs[u + 1])
                    s_phase = until(fg, "PHASE_B")
                    if tail is not None and overlap_tail:
                        interleave(s_phase, tail)
                    else:
                        if tail is not None:
                            drain(tail)
                        drain(s_phase)
                    interleave(fg, until(bg, "TAIL"))
                else:
                    if tail is not None:
                        drain(tail)
                    drain(until(bg, "TAIL"))
                tail = bg
            drain(tail)

        run_pipeline([(i, 0, 128, 512 * (i + 1), i, None) for i in range(NPT)])
        P.barrier()
        sunits = []
        regs = [{"kiB": TBuf(f"kiS{r_}"), "kTB": TBuf(f"kTS{r_}"), "v1B": TBuf(f"v1S{r_}"), "kc0": r_ * 2048, "kt0": r_ * 16} for r_ in range(2)]
        for s_ in range(4):
            sunits.append((NPT + s_ // 2, (s_ % 2) * 64, 64, 1088, None, s_, regs[s_ % 2]))
        run_pipeline(sunits, overlap_tail=False)
        P.pop()
        if stage == 2:
            P.finish()
            return nc

        P.push()
        alloc_norm_tmp()
        g2B = P.sb("g2B", [128, D], F32)
        P.dma("sync", g2B.t[:], g2_d, writes=[g2B])
        gfB = P.sb("gfB", [128, D], F32)
        P.dma("sync", gfB.t[:], gf_d, writes=[gfB])
        xnT = P.sb("xnT", [128, 8, TBK], BF16)
        gT = P.sb("gT", [128, 22, TBK], BF16)
        hblk = P.sb("hblk", [128, BLK, D], F32)
        ST = Streamer(2816)
        sA_r = Rot([P.sb(f"sA{i}", [128, SUB], F32) for i in range(2)])
        oT_r = Rot([P.sb(f"oT{i}", [128, SUB], F32) for i in range(2)])
        yt_r = Rot([P.sb(f"yt{i}", [128, D], F32) for i in range(2)])
        for blk in range(2):
            tok0 = blk * TBK
            P.dma("sync", xnT.t[:], mT_s[:, :, tok0:tok0 + TBK], reads=[D_("mT_s")], writes=[xnT])
            for tt in range(BLK):
                P.dma("sync", hblk.t[:, tt, :], h_s[tok0 + tt * 128: tok0 + (tt + 1) * 128, :], reads=[D_("h_s")], writes=[hblk])
            proj_back(ST, xnT, 8, wo_d, hblk, 1.0)
            for tt in range(BLK):
                norm_transpose(hblk.t[:, tt, :], hblk, g2B, xnT, tt * 128, 6 + tt % 2)
            ffn_in(ST, xnT, gT, w2i_d)
            proj_back(ST, gT, 22, w2o_d, hblk, 0.5)
            for tt in range(BLK):
                rstd = rstd_of(hblk.t[:, tt, :], hblk)
                yt = yt_r.next()
                P.op("vector", lambda e: e.scalar_tensor_tensor(out=yt.t[:], in0=hblk.t[:, tt, :], scalar=rstd.t[:, 0:1], in1=gfB.t[:],
                                                                op0=ALU.mult, op1=ALU.mult), reads=[hblk, rstd, gfB], writes=[yt])
                P.dma("sync", y_d[tok0 + tt * 128: tok0 + (tt + 1) * 128, :], yt.t[:], reads=[yt], writes=[D_("y")])
        P.pop()
        P.finish()
    return nc


_NC_CACHE = {}


def _rope_table(pos):
    inv = (10000.0 ** (-np.arange(0, 64, 2, dtype=np.float32) / np.float32(64))).astype(np.float32)
    ang = pos.astype(np.float32)[:, None] * inv[None, :]
    return np.concatenate([np.cos(ang), np.sin(ang)], axis=-1).astype(np.float32)


def kernel(x_prompt, x_sample, cache_k, cache_v, cache_idx_k, state_conv,
           ffn1_norm, ffn1_w_in, ffn1_w_out, mix_norm, w_in, b_gate,
           conv_w, conv_b, conv_ln_g, conv_ln_b, conv_w_out, attn_w_out, w_out,
           ffn2_norm, ffn2_w_in, ffn2_w_out, final_norm):
    f = lambda a: np.ascontiguousarray(np.asarray(a, dtype=np.float32))
    x_prompt, x_sample = f(x_prompt), f(x_sample)
    cache_k, cache_v, cache_idx_k, state_conv = f(cache_k), f(cache_v), f(cache_idx_k), f(state_conv)
    if "nc" not in _NC_CACHE:
        _NC_CACHE["nc"] = build_program()
    nc = _NC_CACHE["nc"]

    def bc(v):
        return np.ascontiguousarray(np.broadcast_to(f(v).reshape(1, D), (128, D)))

    def colT(v, n):
        return np.ascontiguousarray(f(v).reshape(n, 128).T)

    shared = {
        "g1": bc(ffn1_norm[0]), "gm": bc(mix_norm[0]), "g2": bc(ffn2_norm[0]), "gf": bc(final_norm),
        "w1i": f(ffn1_w_in[0]), "w1o": f(ffn1_w_out[0]), "win": f(w_in[0]),
        "bgT": colT(b_gate[0], 16),
        "cwT": np.ascontiguousarray(f(conv_w[0]).reshape(31, 4, 128).transpose(2, 1, 0)),
        "cbT": colT(conv_b[0], 4), "lngT": colT(conv_ln_g[0], 4), "lnbT": colT(conv_ln_b[0], 4),
        "wco": f(conv_w_out[0]), "wao": f(attn_w_out[0]), "wo": f(w_out[0]),
        "w2i": f(ffn2_w_in[0]), "w2o": f(ffn2_w_out[0]),
    }
    p = np.arange(128)
    in_maps = []
    for c in range(8):
        b, j = c // 4, c % 4
        xs = [x_prompt[b, (4 * i + j) * 128:(4 * i + j + 1) * 128] for i in range(NPT)]
        xs += [x_sample[4 * c + s] for s in range(4)]
        cs = np.zeros((128, NT, 64), np.float32)
        for i in range(NPT):
            cs[:, i, :] = _rope_table((4 * i + j) * 128 + p)
        for t in range(2):
            cs[:, NPT + t, :] = _rope_table(1024 + (p % 64))
        lim = (128 * j + 64 + 64 * (p >= 64)).astype(np.float32).reshape(128, 1)
        sel = np.zeros((128, 5), np.float32)
        if j >= 1:
            sel[:, j - 1] = 1.0
        else:
            sel[:, 4] = 1.0
        m = dict(shared)
        m.update({
            "x": np.ascontiguousarray(np.concatenate(xs, axis=0)),
            "ck": np.ascontiguousarray(cache_k[0, 4 * c:4 * c + 4].reshape(4, 1024, 128)),
            "cv": np.ascontiguousarray(cache_v[0, 4 * c:4 * c + 4].reshape(4, 1024, 128)),
            "cik": np.ascontiguousarray(cache_idx_k[0, 4 * c:4 * c + 4]),
            "sconv": np.ascontiguousarray(state_conv[0, 4 * c:4 * c + 4]),
            "cs": cs, "limrel": lim, "sel": sel,
        })
        in_maps.append(m)

    res = run_bass_kernel_spmd(nc, in_maps, core_ids=list(range(8)))
    R = res.results
    y_p = np.zeros((2, 8192, D), np.float32)
    y_s = np.zeros((32, 64, D), np.float32)
    nk_p = np.zeros((1, 2, 8192, 2, 64), np.float32)
    nv_p = np.zeros((1, 2, 8192, 2, 64), np.float32)
    ni_p = np.zeros((1, 2, 8192, 64), np.float32)
    nc_p = np.zeros((1, 2, 30, 512), np.float32)
    nk_s = np.zeros((1, 32, 64, 2, 64), np.float32)
    nv_s = np.zeros((1, 32, 64, 2, 64), np.float32)
    ni_s = np.zeros((1, 32, 64, 64), np.float32)
    nc_s = np.zeros((1, 32, 30, 512), np.float32)
    for c in range(8):
        b, j = c // 4, c % 4
        r = R[c]
        for i in range(NPT):
            g = slice((4 * i + j) * 128, (4 * i + j + 1) * 128)
            l = slice(i * 128, (i + 1) * 128)
            y_p[b, g] = r["y"][l]
            nk_p[0, b, g] = r["nk"][l].reshape(128, 2, 64)
            nv_p[0, b, g] = r["nv"][l].reshape(128, 2, 64)
            ni_p[0, b, g] = r["nki"][l]
        for s in range(4):
            l = slice(NPT * 128 + s * 64, NPT * 128 + (s + 1) * 64)
            y_s[4 * c + s] = r["y"][l]
            nk_s[0, 4 * c + s] = r["nk"][l].reshape(64, 2, 64)
            nv_s[0, 4 * c + s] = r["nv"][l].reshape(64, 2, 64)
            ni_s[0, 4 * c + s] = r["nki"][l]
            nc_s[0, 4 * c + s] = r["ncs"][s]
        if j == 3:
            nc_p[0, b] = r["ncp"]
    return (y_p, y_s, nk_p, nv_p, ni_p, nc_p, nk_s, nv_s, ni_s, nc_s)
```

```python
import os
import numpy as np
from contextlib import ExitStack
import concourse.bass as bass
import concourse.mybir as mybir
from concourse.bass_utils import run_bass_kernel_spmd

F32 = mybir.dt.float32
BF16 = mybir.dt.bfloat16
I32 = mybir.dt.int32
AF = mybir.ActivationFunctionType
ALU = mybir.AluOpType
AX = mybir.AxisListType

D = 1024
DFF = 2816
NT = 18
NTOK = NT * 128
NPT = 16
BLK = 9
NSUB = 3
SUB = 384
TBK = BLK * 128
EPS = 1e-6
BIG = 1.0e30
NITER = 14
TOPK = 256.0
INCOLS = 4424


class TBuf:
    def __init__(self, name, t=None):
        self.name = name
        self.t = t
        self.w = {}
        self.r = {}
        self.dsem = None
        self.dcount = 0


class Prog:
    ENGS = ("tensor", "vector", "scalar", "gpsimd", "sync")

    def __init__(self, nc):
        self.nc = nc
        self.stacks = [ExitStack()]
        self.sems = {}
        self.count = {}
        self.waited = {}
        self.drams = {}
        self.dma_bufs = []
        self.uid = 0

    def __enter__(self):
        self.stacks[0].__enter__()
        for e in self.ENGS:
            self.sems[e] = self.stacks[0].enter_context(self.nc.semaphore("sem_" + e))
            self.count[e] = 0
            self.waited[e] = {}
        return self

    def __exit__(self, *a):
        while len(self.stacks) > 1:
            self.stacks.pop().close()
        return self.stacks[0].__exit__(*a)

    def push(self):
        st = ExitStack()
        st.__enter__()
        self.stacks.append(st)

    def pop(self):
        self.barrier()
        self.stacks.pop().close()

    def sb(self, name, shape, dtype):
        self.uid += 1
        return TBuf(name, self.stacks[-1].enter_context(self.nc.sbuf_tensor(f"{name}_{self.uid}", list(shape), dtype)))

    def ps(self, name, shape, dtype):
        return TBuf(name, self.stacks[0].enter_context(self.nc.psum_tensor(name, list(shape), dtype)))

    def dram(self, name):
        if name not in self.drams:
            self.drams[name] = TBuf("d_" + name)
        return self.drams[name]

    def _eng(self, name):
        return getattr(self.nc, name)

    def _wait(self, engname, key, val):
        if key == engname and engname == "tensor":
            return
        w = self.waited[engname]
        if w.get(key, 0) >= val:
            return
        if key in self.ENGS:
            assert self.count[key] >= val, f"wait on un-signalled instruction {key} {val} {self.count[key]}"
        w[key] = val
        self._eng(engname).wait_ge(self.sems[key], val)

    def _deps(self, engname, reads, writes):
        for b in reads:
            for k, v in b.w.items():
                self._wait(engname, k, v)
        for b in writes:
            for k, v in b.w.items():
                self._wait(engname, k, v)
            for k, v in b.r.items():
                self._wait(engname, k, v)

    @staticmethod
    def _rec(d, key, val):
        if d.get(key, 0) < val:
            d[key] = val

    def op(self, engname, fn, reads=(), writes=(), signal=True):
        self._deps(engname, reads, writes)
        inst = fn(self._eng(engname))
        if signal:
            self.count[engname] += 1
            inst.then_inc(self.sems[engname], 1)
            val = self.count[engname]
        else:
            val = self.count[engname] + 1
        for b in reads:
            self._rec(b.r, engname, val)
        for b in writes:
            self._rec(b.w, engname, val)
        return inst

    def _dsem(self, b0):
        if b0.dsem is None:
            key = "dma_" + b0.name + str(len(self.dma_bufs))
            b0.dsem = key
            self.sems[key] = self.stacks[0].enter_context(self.nc.semaphore("s" + str(len(self.dma_bufs))))
            self.dma_bufs.append(b0)
        return b0.dsem

    def dma(self, queue, out_ap, in_ap, reads=(), writes=(), **kw):
        self._deps(queue, reads, writes)
        b0 = writes[0]
        key = self._dsem(b0)
        inst = self._eng(queue).dma_start(out=out_ap, in_=in_ap, **kw)
        b0.dcount += 16
        inst.then_inc(self.sems[key], 16)
        for b in reads:
            self._rec(b.r, key, b0.dcount)
        for b in writes:
            self._rec(b.w, key, b0.dcount)
        return inst

    def collective(self, fn, reads, writes):
        self._deps("gpsimd", reads, writes)
        b0 = writes[0]
        key = self._dsem(b0)
        inst = fn(self.nc.gpsimd)
        b0.dcount += 1
        inst.then_inc(self.sems[key])
        for b in reads:
            self._rec(b.r, key, b0.dcount)
        for b in writes:
            self._rec(b.w, key, b0.dcount)

    def sync_to(self, engname, bufs):
        self._deps(engname, (), bufs)

    def barrier(self):
        for e in self.ENGS:
            for e2 in self.ENGS:
                if e2 != e and self.count[e2] > 0:
                    self._wait(e, e2, self.count[e2])
            for b in self.dma_bufs:
                if b.dcount:
                    self._wait(e, b.dsem, b.dcount)

    def finish(self):
        for b in self.dma_bufs:
            if b.dcount:
                self._wait("sync", b.dsem, b.dcount)
        for e in self.ENGS:
            if e != "sync" and self.count[e] > 0:
                self._wait("sync", e, self.count[e])


class Rot:
    def __init__(self, bufs):
        self.bufs = bufs
        self.i = 0

    def next(self):
        b = self.bufs[self.i % len(self.bufs)]
        self.i += 1
        return b


def build_program(stage=3):
    nc = bass.Bass("TRN2", target_bir_lowering=False)

    def din(name, shape, dt=F32):
        return nc.dram_tensor(name, list(shape), dt, kind="ExternalInput").ap()

    def dout(name, shape, dt=F32):
        return nc.dram_tensor(name, list(shape), dt, kind="ExternalOutput").ap()

    def dscr(name, shape, dt):
        return nc.dram_tensor(name, list(shape), dt).ap()

    x_d = din("x", [NTOK, D])
    ck_d = din("ck", [4, 1024, 128])
    cv_d = din("cv", [4, 1024, 128])
    cik_d = din("cik", [4, 1024, 64])
    sconv_d = din("sconv", [4, 30, 512])
    g1_d = din("g1", [128, D])
    gm_d = din("gm", [128, D])
    g2_d = din("g2", [128, D])
    gf_d = din("gf", [128, D])
    w1i_d = din("w1i", [D, 2 * DFF])
    w1o_d = din("w1o", [DFF, D])
    win_d = din("win", [D, INCOLS])
    bg_d = din("bgT", [128, 16])
    cw_d = din("cwT", [128, 4, 31])
    cb_d = din("cbT", [128, 4])
    lg_d = din("lngT", [128, 4])
    lb_d = din("lnbT", [128, 4])
    wco_d = din("wco", [512, D])
    wao_d = din("wao", [512, D])
    wo_d = din("wo", [D, D])
    w2i_d = din("w2i", [D, 2 * DFF])
    w2o_d = din("w2o", [DFF, D])
    cs_d = din("cs", [128, NT, 64])
    lim_d = din("limrel", [128, 1])
    sel_d = din("sel", [128, 5])

    y_d = dout("y", [NTOK, D])
    nk_d = dout("nk", [NTOK, 128])
    nv_d = dout("nv", [NTOK, 128])
    nki_d = dout("nki", [NTOK, 64])
    ncp_d = dout("ncp", [30, 512])
    ncs_d = dout("ncs", [4, 30, 512])

    h_s = dscr("h_s", [NTOK, D], F32)
    gate_s = dscr("gate_s", [128, 16, NTOK], BF16)
    qT_s = dscr("qT_s", [128, 4, NTOK], BF16)
    qiT_s = dscr("qiT_s", [64, 8, NTOK], BF16)
    wi_s = dscr("wi_s", [NTOK, 8], F32)
    cT_s = dscr("cT_s", [128, 4, NTOK], F32)
    mT_s = dscr("mT_s", [128, 8, NTOK], BF16)
    kvmy = dscr("kvmy", [192, NPT * 128], BF16)
    kvmyv = dscr("kvmyv", [128, NPT * 128], BF16)
    kvsm = dscr("kvsm", [320, 256], BF16)
    ctmy = dscr("ctmy", [128, 4 * NPT * 32], F32)
    kvg = dscr("kvg", [4 * 192, NPT * 128], BF16)
    kvgv = dscr("kvgv", [4 * 128, NPT * 128], BF16)
    ctg = dscr("ctg", [4 * 128, 4 * NPT * 32], F32)

    P = Prog(nc)
    with P:
        D_ = P.dram
        psb = [P.ps(f"ps{i}", [128, 512], F32) for i in range(8)]

        def bfv(i):
            return psb[i].t[:].bitcast(BF16)

        ident = P.sb("ident", [128, 128], BF16)
        identF = P.sb("identF", [128, 128], F32)
        it = P.sb("iota_tmp", [128, 128], I32)
        P.op("gpsimd", lambda e: e.iota(it.t[:], pattern=[[1, 128]], base=0, channel_multiplier=-1), writes=[it])
        P.op("vector", lambda e: e.tensor_scalar(out=ident.t[:], in0=it.t[:], scalar1=0.0, scalar2=None, op0=ALU.is_equal),
             reads=[it], writes=[ident])
        P.op("vector", lambda e: e.tensor_scalar(out=identF.t[:], in0=it.t[:], scalar1=0.0, scalar2=None, op0=ALU.is_equal),
             reads=[it], writes=[identF])
        cs = P.sb("cs", [128, NT, 64], F32)
        P.dma("sync", cs.t[:], cs_d, writes=[cs])
        bgT = P.sb("bgT", [128, 16], F32)
        P.dma("sync", bgT.t[:], bg_d, writes=[bgT])
        cwT = P.sb("cwT", [128, 4, 31], F32)
        P.dma("sync", cwT.t[:], cw_d, writes=[cwT])
        cbT = P.sb("cbT", [128, 4], F32)
        P.dma("sync", cbT.t[:], cb_d, writes=[cbT])
        lngT = P.sb("lngT", [128, 4], F32)
        P.dma("sync", lngT.t[:], lg_d, writes=[lngT])
        lnbT = P.sb("lnbT", [128, 4], F32)
        P.dma("sync", lnbT.t[:], lb_d, writes=[lnbT])
        limrel = P.sb("limrel", [128, 1], F32)
        P.dma("sync", limrel.t[:], lim_d, writes=[limrel])
        sel = P.sb("sel", [128, 5], F32)
        P.dma("sync", sel.t[:], sel_d, writes=[sel])
        epsc = P.sb("epsc", [128, 1], F32)
        P.op("vector", lambda e: e.memset(epsc.t[:], EPS), writes=[epsc])

        ssq_r = Rot([P.sb(f"ssq{i}", [128, 1], F32) for i in range(2)])
        rstd_r = Rot([P.sb(f"rstd{i}", [128, 1], F32) for i in range(2)])
        sh = {}

        def alloc_norm_tmp():
            sh["xs_r"] = Rot([P.sb(f"xs{i}", [128, D], BF16) for i in range(2)])
            sh["junk"] = P.sb("junk", [128, D], BF16)

        def rstd_of(src_ap, srcbuf):
            ssq = ssq_r.next()
            rstd = rstd_r.next()
            junk = sh["junk"]
            P.op("scalar", lambda e: e.activation(out=junk.t[:], in_=src_ap, func=AF.Square, accum_out=ssq.t[:, 0:1]),
                 reads=[srcbuf], writes=[junk, ssq])
            P.op("scalar", lambda e: e.activation(out=rstd.t[:], in_=ssq.t[:], func=AF.Sqrt, bias=epsc.t[:, 0:1], scale=1.0 / D),
                 reads=[ssq, epsc], writes=[rstd])
            P.op("vector", lambda e: e.reciprocal(out=rstd.t[:], in_=rstd.t[:]), reads=[rstd], writes=[rstd])
            return rstd

        def norm_transpose(src_ap, srcbuf, gB, dstT, col0, pbank):
            rstd = rstd_of(src_ap, srcbuf)
            xs = sh["xs_r"].next()
            P.op("vector", lambda e: e.scalar_tensor_tensor(out=xs.t[:], in0=src_ap, scalar=rstd.t[:, 0:1], in1=gB.t[:],
                                                            op0=ALU.mult, op1=ALU.mult),
                 reads=[srcbuf, rstd, gB], writes=[xs])
            pv = bfv(pbank)
            for kc in range(8):
                P.op("tensor", lambda e: e.transpose(out=pv[:, kc * 128:(kc + 1) * 128], in_=xs.t[:, kc * 128:(kc + 1) * 128],
                                                     identity=ident.t[:]),
                     reads=[xs, ident], writes=[psb[pbank]], signal=(kc == 7))
            P.op("scalar", lambda e: e.activation(out=dstT.t[:, :, col0:col0 + 128],
                                                  in_=pv.rearrange("p (k n) -> p k n", k=8), func=AF.Copy),
                 reads=[psb[pbank]], writes=[dstT])

        class Streamer:
            def __init__(self, maxel):
                self.stage = [P.sb(f"stg{i}", [128, maxel], F32) for i in range(2)]
                self.wb = [P.sb(f"wbf{i}", [128, maxel], BF16) for i in range(2)]
                self.n = 0
                self.nd = 0

            def load(self, pieces, KC, ncols, dst=None):
                s = self.n % 2
                self.n += 1
                stg = self.stage[s]
                sv = stg.t[:, 0:KC * ncols].rearrange("p (k n) -> p k n", k=KC)
                for (off, ap, w) in pieces:
                    P.dma("sync", sv[:, :, off:off + w], ap, writes=[stg])
                if dst is None:
                    wbuf = self.wb[self.nd % 2]
                    self.nd += 1
                    ov = wbuf.t[:, 0:KC * ncols].rearrange("p (k n) -> p k n", k=KC)
                else:
                    wbuf, ov = dst
                if self.n % 2 == 0:
                    P.op("scalar", lambda e: e.activation(out=ov, in_=sv, func=AF.Copy), reads=[stg], writes=[wbuf])
                else:
                    P.op("vector", lambda e: e.tensor_copy(out=ov, in_=sv), reads=[stg], writes=[wbuf])
                return wbuf, ov

        sA_r = None
        oT_r = None

        def ffn_in(ST, xnT, gT, w_in_ap):
            Wv = w_in_ap.rearrange("(kc p) n -> p kc n", p=128)

            def load(j):
                return ST.load([(0, Wv[:, :, j * 128:(j + 1) * 128], 128),
                                (128, Wv[:, :, DFF + j * 128:DFF + (j + 1) * 128], 128)], 8, 256)
            nxt = load(0)
            cnt = 0
            for j in range(22):
                wbuf, wv = nxt
                if j + 1 < 22:
                    nxt = load(j + 1)
                for sbk in range(NSUB):
                    pa = psb[cnt % 2]
                    pb = psb[2 + cnt % 2]
                    cnt += 1
                    cols = slice(sbk * SUB, (sbk + 1) * SUB)
                    for kc in range(8):
                        P.op("tensor", lambda e: e.matmul(pa.t[:, 0:SUB], lhsT=wv[:, kc, 0:128], rhs=xnT.t[:, kc, cols],
                                                          start=(kc == 0), stop=(kc == 7)),
                             reads=[wbuf, xnT], writes=[pa], signal=(kc == 7))
                    for kc in range(8):
                        P.op("tensor", lambda e: e.matmul(pb.t[:, 0:SUB], lhsT=wv[:, kc, 128:256], rhs=xnT.t[:, kc, cols],
                                                          start=(kc == 0), stop=(kc == 7)),
                             reads=[wbuf, xnT], writes=[pb], signal=(kc == 7))
                    sA = sA_r.next()
                    P.op("scalar", lambda e: e.activation(out=sA.t[:, 0:SUB], in_=pa.t[:, 0:SUB], func=AF.Silu),
                         reads=[pa], writes=[sA])
                    P.op("vector", lambda e: e.tensor_tensor(out=gT.t[:, j, cols], in0=pb.t[:, 0:SUB], in1=sA.t[:, 0:SUB], op=ALU.mult),
                         reads=[pb, sA], writes=[gT])

        def proj_back(ST, srcT, KC, w_ap, hblk, scale):
            Wv = w_ap.rearrange("(kc p) n -> p kc n", p=128)

            def load(n):
                return ST.load([(0, Wv[:, :, n * 128:(n + 1) * 128], 128)], KC, 128)
            nxt = load(0)
            cnt = 0
            for n in range(8):
                wbuf, wv = nxt
                if n + 1 < 8:
                    nxt = load(n + 1)
                for sbk in range(NSUB):
                    po = psb[4 + cnt % 2]
                    pt = psb[6 + cnt % 2]
                    cnt += 1
                    cols = slice(sbk * SUB, (sbk + 1) * SUB)
                    for kc in range(KC):
                        P.op("tensor", lambda e: e.matmul(po.t[:, 0:SUB], lhsT=wv[:, kc, :], rhs=srcT.t[:, kc, cols],
                                                          start=(kc == 0), stop=(kc == KC - 1)),
                             reads=[wbuf, srcT], writes=[po], signal=(kc == KC - 1))
                    oT = oT_r.next()
                    P.op("scalar", lambda e: e.activation(out=oT.t[:, 0:SUB], in_=po.t[:, 0:SUB], func=AF.Copy),
                         reads=[po], writes=[oT])
                    for t3 in range(3):
                        P.op("tensor", lambda e: e.transpose(out=pt.t[:, t3 * 128:(t3 + 1) * 128], in_=oT.t[:, t3 * 128:(t3 + 1) * 128],
                                                             identity=identF.t[:]),
                             reads=[oT, identF], writes=[pt], signal=(t3 == 2))
                    hv = hblk.t[:, sbk * 3:sbk * 3 + 3, n * 128:(n + 1) * 128]
                    P.op("vector", lambda e: e.scalar_tensor_tensor(out=hv, in0=pt.t[:, 0:SUB].rearrange("p (t n) -> p t n", t=3),
                                                                    scalar=float(scale), in1=hv, op0=ALU.mult, op1=ALU.add),
                         reads=[pt, hblk], writes=[hblk])

        P.push()
        alloc_norm_tmp()
        g1B = P.sb("g1B", [128, D], F32)
        P.dma("sync", g1B.t[:], g1_d, writes=[g1B])
        gmB = P.sb("gmB", [128, D], F32)
        P.dma("sync", gmB.t[:], gm_d, writes=[gmB])
        xnT = P.sb("xnT", [128, 8, TBK], BF16)
        gT = P.sb("gT", [128, 22, TBK], BF16)
        hblk = P.sb("hblk", [128, BLK, D], F32)
        ST = Streamer(2816)
        sA_r = Rot([P.sb(f"sA{i}", [128, SUB], F32) for i in range(2)])
        oT_r = Rot([P.sb(f"oT{i}", [128, SUB], F32) for i in range(2)])
        wtok = gT.t[:].rearrange("p a b -> p (a b)")[:, 0:8 * 1352].rearrange("p (k n) -> p k n", k=8)
        ropeT = Rot([P.sb(f"ropeT{i}", [128, 8, 32], F32) for i in range(8)])
        qb2_r = Rot([P.sb(f"qb2{i}", [128, 512], BF16) for i in range(2)])
        qib_r = Rot([P.sb(f"qib{i}", [128, 512], BF16) for i in range(2)])
        kf_r = Rot([P.sb(f"kf{i}", [128, 128], F32) for i in range(2)])
        vf_r = Rot([P.sb(f"vf{i}", [128, 128], F32) for i in range(2)])
        kif_r = Rot([P.sb(f"kif{i}", [128, 64], F32) for i in range(2)])
        kb_r = Rot([P.sb(f"kb{i}", [128, 128], BF16) for i in range(2)])
        vb_r = Rot([P.sb(f"vb{i}", [128, 128], BF16) for i in range(2)])
        kib_r = Rot([P.sb(f"kib{i}", [128, 64], BF16) for i in range(2)])
        wis_r = Rot([P.sb(f"wis{i}", [128, 8], F32) for i in range(2)])
        qTt_r = Rot([P.sb(f"qTt{i}", [128, 4, 128], BF16) for i in range(2)])
        kTt_r = Rot([P.sb(f"kTt{i}", [128, 128], BF16) for i in range(2)])
        qiTt_r = Rot([P.sb(f"qiTt{i}", [64, 8, 128], BF16) for i in range(2)])
        kiTt_r = Rot([P.sb(f"kiTt{i}", [64, 128], BF16) for i in range(2)])
        cbuf_r = Rot([P.sb(f"cbuf{i}", [128, SUB], F32) for i in range(2)])
        gbuf_r = Rot([P.sb(f"gbuf{i}", [128, SUB], BF16) for i in range(2)])

        def rope(src4, srcbuf, tg, o1, o2, obuf, shape):
            x1 = src4[:, :, :, 0:32]
            x2 = src4[:, :, :, 32:64]
            a, b = shape
            cosB = cs.t[:, tg, 0:32].unsqueeze(1).unsqueeze(1).broadcast_to([128, a, b, 32])
            sinB = cs.t[:, tg, 32:64].unsqueeze(1).unsqueeze(1).broadcast_to([128, a, b, 32])
            ts = [ropeT.next() for _ in range(4)]
            tv = [t.t[:, 0:a * b, :].rearrange("p (a b) d -> p a b d", a=a) for t in ts]
            for (tb, tvv, xx, cc) in ((ts[0], tv[0], x1, cosB), (ts[1], tv[1], x2, sinB), (ts[2], tv[2], x2, cosB), (ts[3], tv[3], x1, sinB)):
                P.op("vector", lambda e: e.tensor_tensor(out=tvv, in0=xx, in1=cc, op=ALU.mult), reads=[srcbuf, cs], writes=[tb])
            P.op("gpsimd", lambda e: e.tensor_tensor(out=o1, in0=tv[0], in1=tv[1], op=ALU.subtract), reads=[ts[0], ts[1]], writes=[obuf])
            P.op("gpsimd", lambda e: e.tensor_tensor(out=o2, in0=tv[2], in1=tv[3], op=ALU.add), reads=[ts[2], ts[3]], writes=[obuf])

        WinV = win_d.rearrange("(kc p) n -> p kc n", p=128)
        groups = [[0, 1, 2, 3], [4, 5, 6, 7]]
        for blk in range(2):
            tok0 = blk * TBK
            for tt in range(BLK):
                P.dma("sync", hblk.t[:, tt, :], x_d[tok0 + tt * 128: tok0 + (tt + 1) * 128, :], writes=[hblk])
            if stage == 0.05:
                P.barrier(); P.finish(); return nc
            for tt in range(BLK):
                norm_transpose(hblk.t[:, tt, :], hblk, g1B, xnT, tt * 128, 6 + tt % 2)
            if stage == 0.1:
                P.barrier(); P.finish(); return nc
            ffn_in(ST, xnT, gT, w1i_d)
            if stage == 0.2:
                P.barrier(); P.finish(); return nc
            proj_back(ST, gT, 22, w1o_d, hblk, 0.5)
            if stage == 0.3:
                P.barrier(); P.finish(); return nc
            for tt in range(BLK):
                norm_transpose(hblk.t[:, tt, :], hblk, gmB, xnT, tt * 128, 6 + tt % 2)
            for tt in range(BLK):
                P.dma("sync", h_s[tok0 + tt * 128: tok0 + (tt + 1) * 128, :], hblk.t[:, tt, :], reads=[hblk], writes=[D_("h_s")])
            wtok_pieces = []
            c0 = 0
            while c0 < 1352:
                w = min(256, 1352 - c0)
                wtok_pieces.append((c0, w))
                c0 += w

            def load_wtok(n):
                for _ in range(n):
                    if wtok_pieces:
                        c0_, w_ = wtok_pieces.pop(0)
                        ST.load([(0, WinV[:, :, c0_:c0_ + w_], w_)], 8, w_, dst=(gT, wtok[:, :, c0_:c0_ + w_]))
            if stage == 0.7:
                P.barrier(); P.finish(); return nc
            def loadc(cc):
                return ST.load([(0, WinV[:, :, 1352 + cc * 128:1352 + (cc + 1) * 128], 128),
                                (128, WinV[:, :, 1864 + cc * 128:1864 + (cc + 1) * 128], 128)], 8, 256)
            nxt = loadc(0)
            cnt = 0
            for cc in range(4):
                wbuf, wv = nxt
                if cc + 1 < 4:
                    nxt = loadc(cc + 1)
                load_wtok(2)
                for sbk in range(NSUB):
                    pa = psb[cnt % 2]
                    pb = psb[2 + cnt % 2]
                    cnt += 1
                    cols = slice(sbk * SUB, (sbk + 1) * SUB)
                    for kc in range(8):
                        P.op("tensor", lambda e: e.matmul(pa.t[:, 0:SUB], lhsT=wv[:, kc, 0:128], rhs=xnT.t[:, kc, cols],
                                                          start=(kc == 0), stop=(kc == 7)),
                             reads=[wbuf, xnT], writes=[pa], signal=(kc == 7))
                    for kc in range(8):
                        P.op("tensor", lambda e: e.matmul(pb.t[:, 0:SUB], lhsT=wv[:, kc, 128:256], rhs=xnT.t[:, kc, cols],
                                                          start=(kc == 0), stop=(kc == 7)),
                             reads=[wbuf, xnT], writes=[pb], signal=(kc == 7))
                    sA = sA_r.next()
                    P.op("scalar", lambda e: e.activation(out=sA.t[:, 0:SUB], in_=pb.t[:, 0:SUB], func=AF.Sigmoid), reads=[pb], writes=[sA])
                    cbuf = cbuf_r.next()
                    P.op("vector", lambda e: e.tensor_tensor(out=cbuf.t[:], in0=pa.t[:, 0:SUB], in1=sA.t[:, 0:SUB], op=ALU.mult),
                         reads=[pa, sA], writes=[cbuf])
                    P.dma("gpsimd", cT_s[:, cc, tok0 + sbk * SUB: tok0 + (sbk + 1) * SUB], cbuf.t[:], reads=[cbuf], writes=[D_("cT_s")])
                    for t3 in range(3):
                        tg = blk * BLK + sbk * 3 + t3
                        if tg < NPT:
                            o = (cc * NPT + tg) * 32
                            P.dma("gpsimd", ctmy[:, o:o + 32], cbuf.t[:, t3 * 128 + 96: t3 * 128 + 128], reads=[cbuf], writes=[D_("ctmy")])
            load_wtok(6)
            def zmm(tt):
                s3 = 3 * (tt % 2)
                for (bk, o0, a0, a1) in ((s3, 0, 0, 512), (s3 + 1, 0, 512, 768), (s3 + 1, 256, 1280, 1352), (s3 + 2, 0, 768, 1280)):
                    for kc in range(8):
                        P.op("tensor", lambda e: e.matmul(psb[bk].t[:, o0:o0 + a1 - a0], lhsT=xnT.t[:, kc, tt * 128:(tt + 1) * 128],
                                                          rhs=wtok[:, kc, a0:a1], start=(kc == 0), stop=(kc == 7)),
                             reads=[xnT, gT], writes=[psb[bk]], signal=(kc == 7))

            def post(tt):
                tg = blk * BLK + tt
                trow = slice(tok0 + tt * 128, tok0 + (tt + 1) * 128)
                tcol = trow
                s3 = 3 * (tt % 2)
                bq, bkv, bqi = psb[s3], psb[s3 + 1], psb[s3 + 2]
                qb2 = qb2_r.next()
                zq = bq.t[:, 0:512].rearrange("p (c g d) -> p c g d", c=2, g=4)
                oq = qb2.t[:].rearrange("p (g c d) -> p c g d", g=4, c=2)
                rope(zq, bq, tg, oq[:, :, :, 0:32], oq[:, :, :, 32:64], qb2, (2, 4))
                kf = kf_r.next()
                zk = bkv.t[:, 0:128].rearrange("p (a c d) -> p a c d", a=1, c=2)
                ok = kf.t[:].rearrange("p (a c d) -> p a c d", a=1, c=2)
                rope(zk, bkv, tg, ok[:, :, :, 0:32], ok[:, :, :, 32:64], kf, (1, 2))
                vf = vf_r.next()
                vb = vb_r.next()
                P.op("vector", lambda e: e.tensor_copy(out=vf.t[:], in_=bkv.t[:, 128:256]), reads=[bkv], writes=[vf])
                P.op("vector", lambda e: e.tensor_copy(out=vb.t[:], in_=bkv.t[:, 128:256]), reads=[bkv], writes=[vb])
                qib = qib_r.next()
                zqi = bqi.t[:, 0:512].rearrange("p (a h d) -> p a h d", a=1, h=8)
                oqi = qib.t[:].rearrange("p (a h d) -> p a h d", a=1, h=8)
                rope(zqi, bqi, tg, oqi[:, :, :, 0:32], oqi[:, :, :, 32:64], qib, (1, 8))
                kif = kif_r.next()
                zki = bkv.t[:, 256:320].rearrange("p (a h d) -> p a h d", a=1, h=1)
                oki = kif.t[:].rearrange("p (a h d) -> p a h d", a=1, h=1)
                rope(zki, bkv, tg, oki[:, :, :, 0:32], oki[:, :, :, 32:64], kif, (1, 1))
                wis = wis_r.next()
                P.op("vector", lambda e: e.tensor_scalar(out=wis.t[:], in0=bkv.t[:, 320:328], scalar1=float(8 ** -0.5), scalar2=None, op0=ALU.mult),
                     reads=[bkv], writes=[wis])
                kb = kb_r.next()
                kib = kib_r.next()
                P.op("scalar", lambda e: e.activation(out=kb.t[:], in_=kf.t[:], func=AF.Copy), reads=[kf], writes=[kb])
                P.op("scalar", lambda e: e.activation(out=kib.t[:], in_=kif.t[:], func=AF.Copy), reads=[kif], writes=[kib])
                p6 = bfv(6)
                for g in range(4):
                    P.op("tensor", lambda e: e.transpose(out=p6[:, g * 128:(g + 1) * 128], in_=qb2.t[:, g * 128:(g + 1) * 128], identity=ident.t[:]),
                         reads=[qb2, ident], writes=[psb[6]], signal=False)
                P.op("tensor", lambda e: e.transpose(out=p6[:, 512:640], in_=kb.t[:], identity=ident.t[:]),
                     reads=[kb, ident], writes=[psb[6]], signal=False)
                P.op("tensor", lambda e: e.transpose(out=p6[0:64, 640:768], in_=kib.t[:], identity=ident.t[:]),
                     reads=[kib, ident], writes=[psb[6]])
                p7 = bfv(7)
                for h in range(8):
                    P.op("tensor", lambda e: e.transpose(out=p7[0:64, h * 128:(h + 1) * 128], in_=qib.t[:, h * 64:(h + 1) * 64], identity=ident.t[:]),
                         reads=[qib, ident], writes=[psb[7]], signal=(h == 7))
                qTt = qTt_r.next()
                kTt = kTt_r.next()
                qiTt = qiTt_r.next()
                kiTt = kiTt_r.next()
                P.op("scalar", lambda e: e.activation(out=qTt.t[:], in_=p6[:, 0:512].rearrange("p (g n) -> p g n", g=4), func=AF.Copy),
                     reads=[psb[6]], writes=[qTt])
                P.op("scalar", lambda e: e.activation(out=kTt.t[:], in_=p6[:, 512:640], func=AF.Copy), reads=[psb[6]], writes=[kTt])
                P.op("scalar", lambda e: e.activation(out=kiTt.t[:], in_=p6[0:64, 640:768], func=AF.Copy), reads=[psb[6]], writes=[kiTt])
                P.op("vector", lambda e: e.tensor_copy(out=qiTt.t[:], in_=p7[0:64, :].rearrange("p (h n) -> p h n", h=8)),
                     reads=[psb[7]], writes=[qiTt])
                P.dma("gpsimd", nk_d[trow, :], kf.t[:], reads=[kf], writes=[D_("nk")])
                P.dma("gpsimd", nv_d[trow, :], vf.t[:], reads=[vf], writes=[D_("nv")])
                P.dma("gpsimd", nki_d[trow, :], kif.t[:], reads=[kif], writes=[D_("nki")])
                P.dma("gpsimd", wi_s[trow, :], wis.t[:], reads=[wis], writes=[D_("wi_s")])
                P.dma("scalar", qT_s[:, :, tcol], qTt.t[:], reads=[qTt], writes=[D_("qT_s")])
                P.dma("gpsimd", qiT_s[:, :, tcol], qiTt.t[:], reads=[qiTt], writes=[D_("qiT_s")])
                if tg < NPT:
                    kcol = slice(tg * 128, (tg + 1) * 128)
                    P.dma("scalar", kvmy[0:64, kcol], kiTt.t[:], reads=[kiTt], writes=[D_("kvmy")])
                    P.dma("scalar", kvmy[64:192, kcol], kTt.t[:], reads=[kTt], writes=[D_("kvmy")])
                    P.dma("gpsimd", kvmyv[:, kcol], vb.t[:], reads=[vb], writes=[D_("kvmyv")])
                else:
                    kcol = slice((tg - NPT) * 128, (tg - NPT + 1) * 128)
                    P.dma("scalar", kvsm[0:64, kcol], kiTt.t[:], reads=[kiTt], writes=[D_("kvsm")])
                    P.dma("scalar", kvsm[64:192, kcol], kTt.t[:], reads=[kTt], writes=[D_("kvsm")])
                    P.dma("scalar", kvsm[192:320, kcol], vb.t[:], reads=[vb], writes=[D_("kvsm")])

            if os.environ.get("K_PIPE", "1") == "1":
                zmm(0)
                for tt in range(BLK):
                    if tt + 1 < BLK:
                        zmm(tt + 1)
                    post(tt)
            else:
                for tt in range(BLK):
                    zmm(tt)
                    post(tt)
            if blk == 1:
                P.collective(lambda e: e.collective_compute("AllGather", ALU.bypass, replica_groups=groups,
                                                            ins=[kvmy.opt()], outs=[kvg.opt()]),
                             reads=[D_("kvmy")], writes=[D_("kvg")])
                P.collective(lambda e: e.collective_compute("AllGather", ALU.bypass, replica_groups=groups,
                                                            ins=[kvmyv.opt()], outs=[kvgv.opt()]),
                             reads=[D_("kvmyv")], writes=[D_("kvgv")])
                P.collective(lambda e: e.collective_compute("AllGather", ALU.bypass, replica_groups=groups,
                                                            ins=[ctmy.opt()], outs=[ctg.opt()]),
                             reads=[D_("ctmy")], writes=[D_("ctg")])
            if stage == 0.8:
                P.barrier(); P.finish(); return nc
            def loadg(pp):
                return ST.load([(0, WinV[:, :, 2376 + pp * 256:2376 + (pp + 1) * 256], 256)], 8, 256)
            nxt = loadg(0)
            cnt = 0
            for pp in range(8):
                wbuf, wv = nxt
                if pp + 1 < 8:
                    nxt = loadg(pp + 1)
                for hh in range(2):
                    gc = pp * 2 + hh
                    for sbk in range(NSUB):
                        pa = psb[cnt % 4]
                        cnt += 1
                        cols = slice(sbk * SUB, (sbk + 1) * SUB)
                        for kc in range(8):
                            P.op("tensor", lambda e: e.matmul(pa.t[:, 0:SUB], lhsT=wv[:, kc, hh * 128:(hh + 1) * 128], rhs=xnT.t[:, kc, cols],
                                                              start=(kc == 0), stop=(kc == 7)),
                                 reads=[wbuf, xnT], writes=[pa], signal=(kc == 7))
                        gbuf = gbuf_r.next()
                        P.op("scalar", lambda e: e.activation(out=gbuf.t[:], in_=pa.t[:, 0:SUB], func=AF.Sigmoid, bias=bgT.t[:, gc:gc + 1], scale=1.0),
                             reads=[pa, bgT], writes=[gbuf])
                        P.dma("gpsimd", gate_s[:, gc, tok0 + sbk * SUB: tok0 + (sbk + 1) * SUB], gbuf.t[:], reads=[gbuf], writes=[D_("gate_s")])
        P.pop()
        if stage == 1:
            P.finish()
            return nc

        if stage == 1.1:
            P.barrier(); P.finish(); return nc
        P.push()
        wao = P.sb("wao", [128, 4, D], BF16)
        wco = P.sb("wco", [128, 4, D], BF16)
        amneg = P.sb("amneg", [128, 512], F32)
        ampos = P.sb("ampos", [128, 512], F32)
        P.push()
        ST = Streamer(2048)
        for (wt_, wd_) in ((wao, wao_d), (wco, wco_d)):
            Wv = wd_.rearrange("(kc p) n -> p kc n", p=128)
            for hf in range(2):
                ST.load([(0, Wv[:, :, hf * 512:(hf + 1) * 512], 512)], 4, 512, dst=(wt_, wt_.t[:, :, hf * 512:(hf + 1) * 512]))
        iot_i = P.sb("iot_i", [128, 512], I32)
        P.op("gpsimd", lambda e: e.iota(iot_i.t[:], pattern=[[1, 512]], base=0, channel_multiplier=0), writes=[iot_i])
        am = P.sb("am", [128, 512], F32)
        P.op("vector", lambda e: e.tensor_scalar(out=am.t[:], in0=iot_i.t[:], scalar1=limrel.t[:, 0:1], scalar2=None, op0=ALU.is_ge),
             reads=[iot_i, limrel], writes=[am])
        P.op("vector", lambda e: e.tensor_scalar(out=amneg.t[:], in0=am.t[:], scalar1=-BIG, scalar2=None, op0=ALU.mult), reads=[am], writes=[amneg])
        P.op("vector", lambda e: e.tensor_scalar(out=ampos.t[:], in0=am.t[:], scalar1=BIG, scalar2=None, op0=ALU.mult), reads=[am], writes=[ampos])
        P.pop()
        if stage == 1.2:
            P.barrier(); P.finish(); return nc
        kiT_all = P.sb("kiT_all", [128, 8192], BF16)
        P.op("vector", lambda e: e.memset(kiT_all.t[64:128, :], 0.0), writes=[kiT_all])
        kT_all = P.sb("kT_all", [128, 8192], BF16)
        v1_all = P.sb("v1_all", [128, 64, 2, 65], BF16)
        score = P.sb("score", [128, 8192], F32)
        score_b = [TBuf(f"score_b{i}") for i in range(16)]
        msk2 = [P.sb(f"msk{i}", [128, 8192], BF16) for i in range(2)]
        zeros = P.sb("zeros", [1, 512], BF16)
        P.op("vector", lambda e: e.memset(zeros.t[:], 0.0), writes=[zeros])
        onesM = P.sb("onesM", [128, 128], F32)
        P.op("vector", lambda e: e.memset(onesM.t[:], 1.0 / 512.0), writes=[onesM])

        qiTu_r = Rot([P.sb(f"qiTu{i}", [128, 8, 128], BF16) for i in range(2)])
        for b_ in qiTu_r.bufs:
            P.op("gpsimd", lambda e: e.memset(b_.t[64:128, :, :], 0.0), writes=[b_])
        wiu_r = Rot([P.sb(f"wiu{i}", [128, 8], F32) for i in range(2)])
        qTz_r = Rot([[P.sb(f"qTz{i}_{c}", [128, 4, 128], BF16) for c in range(2)] for i in range(2)])
        for pr in qTz_r.bufs:
            P.op("gpsimd", lambda e: e.memset(pr[0].t[64:128, :, :], 0.0), writes=[pr[0]])
            P.op("gpsimd", lambda e: e.memset(pr[1].t[0:64, :, :], 0.0), writes=[pr[1]])
        ident4 = P.sb("ident4", [128, 4, 128], BF16)
        for g_ in range(4):
            P.op("gpsimd", lambda e: e.tensor_copy(out=ident4.t[:, g_, :], in_=ident.t[:]), reads=[ident], writes=[ident4])
        gateu_r = Rot([P.sb(f"gateu{i}", [128, 16, 128], BF16) for i in range(3)])
        cpad_r = Rot([P.sb(f"cpad{i}", [128, 4, 158], F32) for i in range(3)])
        cand_r = Rot([P.sb(f"cand{i}", [128, 5, 4, 32], F32) for i in range(2)])
        E_r = Rot([P.sb(f"E{i}", [128, 512], BF16) for i in range(3)])
        fvec = P.sb("fvec", [128, NITER + 1], F32)
        for k_ in range(NITER + 1):
            P.op("vector", lambda e: e.memset(fvec.t[:, k_:k_ + 1], float(2.0 ** -(k_ + 1))), writes=[fvec])
        frng = P.sb("frng", [128, NITER + 1], F32)
        nfrng = P.sb("nfrng", [128, NITER + 1], F32)
        sm = {n: P.sb("sm_" + n, [128, 1], F32) for n in ("rmax", "rmin", "mind", "lo", "rng", "mid", "cnt", "ind", "nmid", "cntA", "tcn")}
        rec = P.sb("rec", [128, 8], F32)
        attn_tok = P.sb("attn_tok", [128, 512], BF16)
        attnT = P.sb("attnT", [128, 4, 128], BF16)
        t1 = P.sb("t1", [128, 8, 128], F32)
        t2 = P.sb("t2", [128, 8, 128], F32)
        mTu_r = Rot([P.sb(f"mTu{i}", [128, 8, 128], BF16) for i in range(2)])
        cacc_r = Rot([P.sb(f"cacc{i}", [128, 4, 128], F32) for i in range(3)])
        tmpc = P.sb("tmpc", [128, 4, 128], F32)
        dcen = P.sb("dcen", [128, 4, 128], F32)
        dsq = P.sb("dsq", [128, 4, 128], F32)
        rsb = P.sb("rsb", [128, 128], F32)
        actT = P.sb("actT", [128, 4, 128], BF16)
        ncv = P.sb("ncv", [30, 512], F32)
        scst = ncv

        P.op("vector", lambda e: e.memset(v1_all.t[:], 1.0), writes=[v1_all])
        kiv = kiT_all.t[0:64, :].rearrange("p (i j n) -> p i j n", i=16, j=4)
        ktv = kT_all.t[:].rearrange("p (i j n) -> p i j n", i=16, j=4)
        v1v = v1_all.t[:].rearrange("p (i j) c d -> p i j c d", i=16, j=4)
        for jj in range(4):
            r0 = jj * 192
            P.dma("sync", kiv[:, :, jj, :], kvg[r0:r0 + 64, :].rearrange("p (i n) -> p i n", i=16), reads=[D_("kvg")], writes=[kiT_all])
            P.dma("sync", ktv[:, :, jj, :], kvg[r0 + 64:r0 + 192, :].rearrange("p (i n) -> p i n", i=16), reads=[D_("kvg")], writes=[kT_all])
            for c in range(2):
                P.dma("sync", v1v[:, :, jj, c, 0:64],
                      kvgv[jj * 128:(jj + 1) * 128, :].rearrange("p (i c d) -> p i c d", i=16, c=2)[:, :, c, :],
                      reads=[D_("kvgv")], writes=[v1_all])

        if stage == 1.3:
            P.barrier(); P.finish(); return nc
        kst_v = score.t[:, 4096:5120].rearrange("p (t d) -> p t d", t=8)
        kstb_v = score.t[:, 5120:5632].bitcast(BF16).rearrange("p (t d) -> p t d", t=8)
        kstT = TBuf("kstT")
        kstbT = TBuf("kstbT")

        def sample_prep(s, cx):
            kiB, kTB, v1B, kc0, kt0 = cx["kiB"], cx["kTB"], cx["v1B"], cx["kc0"], cx["kt0"]
            p7 = bfv(7)
            P.dma("sync", kst_v[:, :, 0:64], cik_d[s].rearrange("(t p) d -> p t d", p=128), writes=[kstT])
            P.op("gpsimd", lambda e: e.tensor_copy(out=kstb_v[:, :, 0:64], in_=kst_v[:, :, 0:64]), reads=[kstT], writes=[kstbT])
            for t8 in range(8):
                P.op("tensor", lambda e: e.transpose(out=p7[0:64, t8 * 128:(t8 + 1) * 128], in_=kstb_v[:, t8, 0:64], identity=ident.t[:]),
                     reads=[kstbT, ident], writes=[psb[7]], signal=(t8 == 7))
            P.op("scalar", lambda e: e.activation(out=kiT_all.t[0:64, kc0:kc0 + 1024], in_=p7[0:64, :], func=AF.Copy), reads=[psb[7]], writes=[kiB])
            P.dma("sync", kst_v, ck_d[s].rearrange("(t p) d -> p t d", p=128), writes=[kstT])
            P.op("gpsimd", lambda e: e.tensor_copy(out=kstb_v, in_=kst_v), reads=[kstT], writes=[kstbT])
            for t8 in range(8):
                P.op("tensor", lambda e: e.transpose(out=p7[:, t8 * 128:(t8 + 1) * 128], in_=kstb_v[:, t8, :], identity=ident.t[:]),
                     reads=[kstbT, ident], writes=[psb[7]], signal=(t8 == 7))
            P.op("scalar", lambda e: e.activation(out=kT_all.t[:, kc0:kc0 + 1024], in_=p7[:, :], func=AF.Copy), reads=[psb[7]], writes=[kTB])
            P.dma("sync", kst_v, cv_d[s].rearrange("(t p) d -> p t d", p=128), writes=[kstT])
            P.op("gpsimd", lambda e: e.tensor_copy(out=v1_all.t[:, kt0:kt0 + 8, :, 0:64], in_=kst_v.rearrange("p t (c d) -> p t c d", c=2)),
                 reads=[kstT], writes=[v1B])
            sc = slice(s * 64, (s + 1) * 64)
            P.dma("sync", kiT_all.t[0:64, kc0 + 1024:kc0 + 1088], kvsm[0:64, sc], reads=[D_("kvsm")], writes=[kiB])
            P.dma("sync", kT_all.t[:, kc0 + 1024:kc0 + 1088], kvsm[64:192, sc], reads=[D_("kvsm")], writes=[kTB])
            vr = 192 + (s % 2) * 64
            vc = slice((s // 2) * 128, (s // 2 + 1) * 128)
            P.dma("sync", v1_all.t[0:64, kt0 + 8, :, 0:64], kvsm[vr:vr + 64, vc].rearrange("p (c d) -> p c d", c=2),
                  reads=[D_("kvsm")], writes=[v1B])

        def front(cx):
            tile, col0, nq, L, prompt_slot, seq = cx["args"]
            tok0 = tile * 128 + col0
            tcol = slice(tok0, tok0 + nq)
            nblk = (L + 511) // 512
            qiTu = qiTu_r.next()
            wiu = wiu_r.next()
            qTz = qTz_r.next()
            gateu = gateu_r.next()
            cpad = cpad_r.next()
            cacc = cacc_r.next()
            mk = msk2[cx["idx"] % 2]
            kiB, kc0 = cx.get("kiB", kiT_all), cx.get("kc0", 0)
            if seq is not None:
                sample_prep(seq, cx)
            cx.update(qTz=qTz, gateu=gateu, cpad=cpad, msk=mk, tcol=tcol, cacc=cacc)
            P.dma("sync", qiTu.t[0:64, :, 0:nq], qiT_s[:, :, tcol], reads=[D_("qiT_s")], writes=[qiTu])
            P.dma("sync", wiu.t[0:nq, :], wi_s[tcol, :], reads=[D_("wi_s")], writes=[wiu])
            P.dma("sync", qTz[0].t[0:64, :, 0:nq], qT_s[0:64, :, tcol], reads=[D_("qT_s")], writes=[qTz[0]])
            P.dma("sync", qTz[1].t[64:128, :, 0:nq], qT_s[64:128, :, tcol], reads=[D_("qT_s")], writes=[qTz[1]])
            P.dma("sync", gateu.t[:, :, 0:nq], gate_s[:, :, tcol], reads=[D_("gate_s")], writes=[gateu])
            P.dma("sync", cpad.t[:, :, 30:30 + nq], cT_s[:, :, tcol], reads=[D_("cT_s")], writes=[cpad])
            if prompt_slot is not None:
                i = prompt_slot
                cand = cand_r.next()
                cx["cand"] = cand
                ctv = ctg.rearrange("(r p) (c i n) -> p r c i n", r=4, c=4, i=NPT)
                for r in range(4):
                    P.dma("sync", cand.t[:, r, :, :], ctv[:, r, :, i, :], reads=[D_("ctg")], writes=[cand])
                if i > 0:
                    P.dma("sync", cand.t[:, 4, :, :], ctv[:, 3, :, i - 1, :], reads=[D_("ctg")], writes=[cand])
            else:
                P.dma("sync", scst.t[:], sconv_d[seq], writes=[scst])
            if prompt_slot is not None:
                i = prompt_slot
                pass
                nk_ = 5 if i > 0 else 4
                P.op("vector", lambda e: e.tensor_scalar(out=cpad.t[:, :, 0:30], in0=cand.t[:, 0, :, 2:32], scalar1=sel.t[:, 0:1], scalar2=None,
                                                         op0=ALU.mult), reads=[cand, sel], writes=[cpad])
                for k in range(1, nk_):
                    P.op("vector", lambda e: e.scalar_tensor_tensor(out=cpad.t[:, :, 0:30], in0=cand.t[:, k, :, 2:32], scalar=sel.t[:, k:k + 1],
                                                                    in1=cpad.t[:, :, 0:30], op0=ALU.mult, op1=ALU.add),
                         reads=[cand, sel, cpad], writes=[cpad])
            else:
                for cc in range(4):
                    P.op("tensor", lambda e: e.transpose(out=psb[7].t[:, cc * 32:cc * 32 + 30], in_=scst.t[:, cc * 128:(cc + 1) * 128],
                                                         identity=identF.t[0:30, 0:30]),
                         reads=[scst, identF], writes=[psb[7]], signal=(cc == 3))
                P.op("vector", lambda e: e.tensor_copy(out=cpad.t[:, :, 0:30],
                                                       in_=psb[7].t[:, 0:128].rearrange("p (c n) -> p c n", c=4)[:, :, 0:30]),
                     reads=[psb[7]], writes=[cpad])
            cav = cacc.t[:, :, 0:nq]
            tmv = tmpc.t[:, :, 0:nq]
            P.op("gpsimd", lambda e: e.tensor_tensor(out=cav, in0=cpad.t[:, :, 0:nq], in1=cwT.t[:, :, 0:1].broadcast_to([128, 4, nq]), op=ALU.mult),
                 reads=[cpad, cwT], writes=[cacc])
            P.op("gpsimd", lambda e: e.tensor_tensor(out=cav, in0=cav, in1=cbT.t[:, :].unsqueeze(2).broadcast_to([128, 4, nq]), op=ALU.add),
                 reads=[cacc, cbT], writes=[cacc])
            for k in range(1, 31):
                P.op("gpsimd", lambda e: e.tensor_tensor(out=tmv, in0=cpad.t[:, :, k:k + nq], in1=cwT.t[:, :, k:k + 1].broadcast_to([128, 4, nq]), op=ALU.mult),
                     reads=[cpad, cwT], writes=[tmpc])
                P.op("gpsimd", lambda e: e.tensor_tensor(out=cav, in0=cav, in1=tmv, op=ALU.add), reads=[cacc, tmpc], writes=[cacc])
            yield 0.5
            hb = 0
            for bk in range(nblk):
                cols = min(512, L - bk * 512)
                sc_ap = score.t[0:nq, bk * 512: bk * 512 + cols]
                for h in range(8):
                    pbk = psb[hb % 5]
                    hb += 1
                    P.op("tensor", lambda e: e.matmul(pbk.t[0:nq, 0:cols], lhsT=qiTu.t[:, h, 0:nq], rhs=kiT_all.t[:, kc0 + bk * 512: kc0 + bk * 512 + cols],
                                                      start=True, stop=True),
                         reads=[qiTu, kiB], writes=[pbk])
                    P.op("scalar", lambda e: e.activation(out=pbk.t[0:nq, 0:cols], in_=pbk.t[0:nq, 0:cols], func=AF.Relu),
                         reads=[pbk], writes=[pbk])
                    if h == 0:
                        P.op("vector", lambda e: e.tensor_scalar(out=sc_ap, in0=pbk.t[0:nq, 0:cols], scalar1=wiu.t[0:nq, 0:1], scalar2=None,
                                                                 op0=ALU.mult), reads=[pbk, wiu], writes=[score_b[bk]])
                    else:
                        P.op("vector", lambda e: e.scalar_tensor_tensor(out=sc_ap, in0=pbk.t[0:nq, 0:cols], scalar=wiu.t[0:nq, h:h + 1],
                                                                        in1=sc_ap, op0=ALU.mult, op1=ALU.add),
                             reads=[pbk, wiu, score_b[bk]], writes=[score_b[bk]])
                    yield 0.6 * cols / 512
            sbs = score_b[0:nblk]
            cx["sbs"] = sbs
            if prompt_slot is not None:
                i = prompt_slot
                dg = score.t[0:nq, i * 512:(i + 1) * 512]
                tmpd_v = mk.t[:, 0:1024].bitcast(F32)
                P.sync_to("vector", [mk])
                P.op("vector", lambda e: e.tensor_tensor(out=tmpd_v[0:nq, :], in0=dg, in1=ampos.t[0:nq, :], op=ALU.add),
                     reads=[score_b[i], ampos], writes=[mk])
                P.op("vector", lambda e: e.tensor_reduce(out=sm["mind"].t[0:nq, :], in_=tmpd_v[0:nq, :], axis=AX.X, op=ALU.min),
                     reads=[mk], writes=[sm["mind"]])
                P.op("vector", lambda e: e.tensor_tensor(out=dg, in0=dg, in1=amneg.t[0:nq, :], op=ALU.add),
                     reads=[score_b[i], amneg], writes=[score_b[i]])
                if i > 0:
                    mn_in = score.t[0:nq, 0:i * 512].rearrange("p (a b) -> p a b", b=4)[:, :, 0] if i >= 3 else score.t[0:nq, 0:i * 512]
                    P.op("vector", lambda e: e.tensor_reduce(out=sm["rmin"].t[0:nq, :], in_=mn_in, axis=AX.X, op=ALU.min),
                         reads=sbs, writes=[sm["rmin"]])
                    P.op("vector", lambda e: e.tensor_tensor(out=sm["rmin"].t[0:nq, :], in0=sm["rmin"].t[0:nq, :], in1=sm["mind"].t[0:nq, :], op=ALU.min),
                         reads=[sm["rmin"], sm["mind"]], writes=[sm["rmin"]])
                else:
                    P.op("vector", lambda e: e.tensor_copy(out=sm["rmin"].t[0:nq, :], in_=sm["mind"].t[0:nq, :]), reads=[sm["mind"]], writes=[sm["rmin"]])
            else:
                P.op("vector", lambda e: e.tensor_reduce(out=sm["rmin"].t[0:nq, :], in_=score.t[0:nq, 0:L], axis=AX.X, op=ALU.min),
                     reads=sbs, writes=[sm["rmin"]])
            yield 0.55 * nblk
            mx_in = score.t[0:nq, 0:L].rearrange("p (a b) -> p a b", b=4)[:, :, 0] if (prompt_slot is not None and L >= 2048) else score.t[0:nq, 0:L]
            P.op("vector", lambda e: e.tensor_reduce(out=sm["rmax"].t[0:nq, :], in_=mx_in, axis=AX.X, op=ALU.max),
                 reads=sbs, writes=[sm["rmax"]])
            lo, rng, mid, cnt, ind = (sm[n] for n in ("lo", "rng", "mid", "cnt", "ind"))
            P.op("vector", lambda e: e.tensor_scalar(out=lo.t[0:nq, :], in0=sm["rmin"].t[0:nq, :], scalar1=-0.01, scalar2=None, op0=ALU.add),
                 reads=[sm["rmin"]], writes=[lo])
            P.op("vector", lambda e: e.scalar_tensor_tensor(out=rng.t[0:nq, :], in0=sm["rmax"].t[0:nq, :], scalar=0.01, in1=lo.t[0:nq, :],
                                                            op0=ALU.add, op1=ALU.subtract), reads=[sm["rmax"], lo], writes=[rng])
            yield 0.55 * nblk
            yield "PHASE_B"
            La = 0
            Lh = L - La
            mkD, mkA = TBuf("mkD"), TBuf("mkA")
            P.sync_to("vector", [mk])
            if La:
                P.sync_to("scalar", [mk])
            cntA, tcn, dstp = sm["cntA"], sm["tcn"], sm["ind"]
            P.op("vector", lambda e: e.tensor_scalar(out=frng.t[0:nq, :], in0=fvec.t[0:nq, :], scalar1=rng.t[0:nq, 0:1], scalar2=None, op0=ALU.mult),
                 reads=[fvec, rng], writes=[frng])
            P.op("vector", lambda e: e.tensor_scalar(out=nfrng.t[0:nq, :], in0=fvec.t[0:nq, :], scalar1=rng.t[0:nq, 0:1], scalar2=-1.0, op0=ALU.mult, op1=ALU.mult),
                 reads=[fvec, rng], writes=[nfrng])
            P.op("vector", lambda e: e.tensor_tensor(out=mid.t[0:nq, :], in0=lo.t[0:nq, :], in1=frng.t[0:nq, 0:1], op=ALU.add), reads=[lo, frng], writes=[mid])
            for k in range(1, NITER + 1):
                if La:
                    P.op("scalar", lambda e: e.activation(out=mk.t[0:nq, Lh:L], in_=score.t[0:nq, Lh:L], func=AF.Sign, bias=mid.t[0:nq, 0:1], scale=-1.0,
                                                          accum_out=cntA.t[0:nq, 0:1]),
                         reads=sbs + [mid], writes=[mkA, cntA])
                P.op("vector", lambda e: e.tensor_scalar(out=mk.t[0:nq, 0:Lh], in0=score.t[0:nq, 0:Lh], scalar1=mid.t[0:nq, 0:1], scalar2=None,
                                                         op0=ALU.is_ge, op1=ALU.add, accum_out=cnt.t[0:nq, 0:1]),
                     reads=sbs + [mid], writes=[mkD, cnt])
                if La:
                    P.op("vector", lambda e: e.scalar_tensor_tensor(out=tcn.t[0:nq, :], in0=cnt.t[0:nq, :], scalar=2.0, in1=cntA.t[0:nq, :],
                                                                    op0=ALU.mult, op1=ALU.subtract), reads=[cnt, cntA], writes=[tcn])
                    csrc, thr_ = tcn, float(2 * TOPK - 1 - La)
                else:
                    csrc, thr_ = cnt, TOPK - 0.5
                P.op("vector", lambda e: e.scalar_tensor_tensor(out=dstp.t[0:nq, :], in0=csrc.t[0:nq, :], scalar=thr_, in1=frng.t[0:nq, k - 1:k],
                                                                op0=ALU.is_ge, op1=ALU.mult), reads=[csrc, frng], writes=[dstp])
                if k < NITER:
                    P.op("vector", lambda e: e.scalar_tensor_tensor(out=mid.t[0:nq, :], in0=dstp.t[0:nq, :], scalar=nfrng.t[0:nq, k:k + 1], in1=mid.t[0:nq, :],
                                                                    op0=ALU.add, op1=ALU.add), reads=[dstp, nfrng, mid], writes=[mid])
                else:
                    P.op("vector", lambda e: e.scalar_tensor_tensor(out=lo.t[0:nq, :], in0=dstp.t[0:nq, :], scalar=nfrng.t[0:nq, k - 1:k], in1=mid.t[0:nq, :],
                                                                    op0=ALU.add, op1=ALU.add), reads=[dstp, nfrng, mid], writes=[lo])
                yield 0.45 * nblk + 0.5
            P.op("vector", lambda e: e.tensor_scalar(out=mk.t[0:nq, 0:L], in0=score.t[0:nq, 0:L], scalar1=lo.t[0:nq, 0:1], scalar2=-30000.0,
                                                     op0=ALU.is_lt, op1=ALU.mult), reads=sbs + [lo], writes=[mk, mkD, mkA])
            yield 0.55 * nblk

        def back(cx):
            tile, col0, nq, L, prompt_slot, seq = cx["args"]
            qTz, gateu, cpad, mk, tcol, cacc = cx["qTz"], cx["gateu"], cx["cpad"], cx["msk"], cx["tcol"], cx["cacc"]
            KT = (L + 127) // 128
            kTB, v1B, kc0, kt0 = cx.get("kTB", kT_all), cx.get("v1B", v1_all), cx.get("kc0", 0), cx.get("kt0", 0)
            yield 0.5
            accv = [psb[5], psb[7]]
            for c in range(2):
                P.op("tensor", lambda e: e.matmul(accv[c].t[0:nq, 0:260], lhsT=zeros.t[0:1, 0:nq], rhs=zeros.t[0:1, 0:260], start=True, stop=False,
                                                  skip_group_check=True),
                     reads=[zeros], writes=[accv[c]])
            p6 = bfv(7)
            steps = [(kt, c) for kt in range(KT) for c in range(2)]

            def qk(si):
                kt, c = steps[si]
                ks = min(128, L - kt * 128)
                lgp = psb[3 + si % 2]
                P.op("tensor", lambda e: e.matmul(lgp.t[0:ks, 0:4 * nq], lhsT=kT_all.t[:, kc0 + kt * 128: kc0 + kt * 128 + ks],
                                                  rhs=qTz[c].t[:, :, 0:nq], start=True, stop=False),
                     reads=[kTB, qTz[c]], writes=[lgp], signal=False)
                P.op("tensor", lambda e: e.matmul(lgp.t[0:ks, 0:4 * nq], lhsT=mk.t[0:nq, kt * 128: kt * 128 + ks],
                                                  rhs=ident4.t[0:nq, :, 0:nq], start=False, stop=True),
                     reads=[mk, ident4], writes=[lgp])
            qk(0)
            for si, (kt, c) in enumerate(steps):
                ks = min(128, L - kt * 128)
                if si + 1 < len(steps):
                    qk(si + 1)
                lgp = psb[3 + si % 2]
                Eb = E_r.next()
                P.op("scalar", lambda e: e.activation(out=Eb.t[0:ks, 0:4 * nq], in_=lgp.t[0:ks, 0:4 * nq], func=AF.Exp, scale=0.125),
                     reads=[lgp], writes=[Eb])
                for g in range(4):
                    P.op("tensor", lambda e: e.matmul(accv[c].t[0:nq, g * 65:(g + 1) * 65], lhsT=Eb.t[0:ks, g * nq:(g + 1) * nq],
                                                      rhs=v1_all.t[0:ks, kt0 + kt, c, :], start=False, stop=False, skip_group_check=True),
                         reads=[Eb, v1B], writes=[accv[c]], signal=(g == 3))
                if c == 1:
                    yield 2.9 * (nq / 128.0)
            yield "TAIL"
            for c in range(2):
                P.op("tensor", lambda e: e.matmul(accv[c].t[0:nq, 0:260], lhsT=zeros.t[0:1, 0:nq], rhs=zeros.t[0:1, 0:260], start=False, stop=True,
                                                  skip_group_check=True),
                     reads=[zeros], writes=[accv[c]])
            for c in range(2):
                av = accv[c].t[0:nq, 0:260].rearrange("p (g d) -> p g d", g=4)
                P.op("vector", lambda e: e.reciprocal(out=rec.t[0:nq, c * 4:(c + 1) * 4], in_=av[:, :, 64]), reads=[accv[c]], writes=[rec])
                for g in range(4):
                    hh = c * 4 + g
                    P.op("vector", lambda e: e.tensor_scalar(out=attn_tok.t[0:nq, hh * 64:(hh + 1) * 64], in0=av[:, g, 0:64],
                                                             scalar1=rec.t[0:nq, hh:hh + 1], scalar2=None, op0=ALU.mult),
                         reads=[accv[c], rec], writes=[attn_tok])
            yield 1.0
            for kc in range(4):
                P.op("tensor", lambda e: e.transpose(out=p6[:, kc * 128:kc * 128 + nq], in_=attn_tok.t[0:nq, kc * 128:(kc + 1) * 128],
                                                     identity=ident.t[0:nq, 0:nq]),
                     reads=[attn_tok, ident], writes=[psb[7]], signal=(kc == 3))
            P.op("scalar", lambda e: e.activation(out=attnT.t[:, :, 0:nq], in_=p6[:, 0:512].rearrange("p (k n) -> p k n", k=4)[:, :, 0:nq], func=AF.Copy),
                 reads=[psb[7]], writes=[attnT])
            for b4 in range(2):
                for n in range(b4 * 4, b4 * 4 + 4):
                    for kc in range(4):
                        P.op("tensor", lambda e: e.matmul(psb[7].t[:, (n % 4) * 128:(n % 4) * 128 + nq], lhsT=wao.t[:, kc, n * 128:(n + 1) * 128],
                                                          rhs=attnT.t[:, kc, 0:nq], start=(kc == 0), stop=(kc == 3)),
                             reads=[wao, attnT], writes=[psb[7]], signal=(kc == 3))
                P.op("vector", lambda e: e.tensor_tensor(out=t1.t[:, b4 * 4:(b4 + 1) * 4, 0:nq],
                                                         in0=psb[7].t[:, 0:512].rearrange("p (k n) -> p k n", k=4)[:, :, 0:nq],
                                                         in1=gateu.t[:, b4 * 4:(b4 + 1) * 4, 0:nq], op=ALU.mult),
                     reads=[psb[7], gateu], writes=[t1])
                yield 0.8
            pst = psb[7]
            for cc in range(4):
                P.op("tensor", lambda e: e.matmul(pst.t[:, 0:nq], lhsT=onesM.t[:], rhs=cacc.t[:, cc, 0:nq], start=(cc == 0), stop=(cc == 3)),
                     reads=[onesM, cacc], writes=[pst], signal=(cc == 3))
            for cc in range(4):
                P.op("vector", lambda e: e.tensor_tensor(out=dcen.t[:, cc, 0:nq], in0=cacc.t[:, cc, 0:nq], in1=pst.t[:, 0:nq], op=ALU.subtract),
                     reads=[cacc, pst], writes=[dcen])
            P.op("scalar", lambda e: e.activation(out=dsq.t[:, :, 0:nq], in_=dcen.t[:, :, 0:nq], func=AF.Square), reads=[dcen], writes=[dsq])
            for cc in range(4):
                P.op("tensor", lambda e: e.matmul(pst.t[:, 128:128 + nq], lhsT=onesM.t[:], rhs=dsq.t[:, cc, 0:nq], start=(cc == 0), stop=(cc == 3)),
                     reads=[onesM, dsq], writes=[pst], signal=(cc == 3))
            P.op("scalar", lambda e: e.activation(out=rsb.t[:, 0:nq], in_=pst.t[:, 128:128 + nq], func=AF.Sqrt, bias=epsc.t[:, 0:1], scale=1.0),
                 reads=[pst, epsc], writes=[rsb])
            P.op("vector", lambda e: e.reciprocal(out=rsb.t[:, 0:nq], in_=rsb.t[:, 0:nq]), reads=[rsb], writes=[rsb])
            P.op("vector", lambda e: e.tensor_tensor(out=dcen.t[:, :, 0:nq], in0=dcen.t[:, :, 0:nq],
                                                     in1=rsb.t[:, 0:nq].unsqueeze(1).broadcast_to([128, 4, nq]), op=ALU.mult),
                 reads=[dcen, rsb], writes=[dcen])
            for cc in range(4):
                P.op("scalar", lambda e: e.activation(out=actT.t[:, cc, 0:nq], in_=dcen.t[:, cc, 0:nq], func=AF.Silu,
                                                      bias=lnbT.t[:, cc:cc + 1], scale=lngT.t[:, cc:cc + 1]),
                     reads=[dcen, lnbT, lngT], writes=[actT])
            yield 2.0
            mTu = mTu_r.next()
            for b4 in range(2):
                for n in range(b4 * 4, b4 * 4 + 4):
                    for kc in range(4):
                        P.op("tensor", lambda e: e.matmul(psb[7].t[:, (n % 4) * 128:(n % 4) * 128 + nq], lhsT=wco.t[:, kc, n * 128:(n + 1) * 128],
                                                          rhs=actT.t[:, kc, 0:nq], start=(kc == 0), stop=(kc == 3)),
                             reads=[wco, actT], writes=[psb[7]], signal=(kc == 3))
                P.op("vector", lambda e: e.tensor_tensor(out=t2.t[:, b4 * 4:(b4 + 1) * 4, 0:nq],
                                                         in0=psb[7].t[:, 0:512].rearrange("p (k n) -> p k n", k=4)[:, :, 0:nq],
                                                         in1=gateu.t[:, 8 + b4 * 4:8 + (b4 + 1) * 4, 0:nq], op=ALU.mult),
                     reads=[psb[7], gateu], writes=[t2])
                yield 0.8
            P.op("gpsimd", lambda e: e.tensor_tensor(out=mTu.t[:, :, 0:nq], in0=t1.t[:, :, 0:nq], in1=t2.t[:, :, 0:nq], op=ALU.add),
                 reads=[t1, t2], writes=[mTu])
            P.dma("sync", mT_s[:, :, tcol], mTu.t[:, :, 0:nq], reads=[mTu], writes=[D_("mT_s")])
            if seq is not None or prompt_slot == NPT - 1:
                for cc in range(4):
                    P.op("tensor", lambda e: e.transpose(out=psb[7].t[0:30, cc * 128:(cc + 1) * 128], in_=cpad.t[:, cc, nq:nq + 30], identity=identF.t[:]),
                         reads=[cpad, identF], writes=[psb[7]], signal=(cc == 3))
                P.op("vector", lambda e: e.tensor_copy(out=ncv.t[:], in_=psb[7].t[0:30, :]), reads=[psb[7]], writes=[ncv])
                if seq is not None:
                    P.dma("sync", ncs_d[seq], ncv.t[:], reads=[ncv], writes=[D_("ncs")])
                else:
                    P.dma("sync", ncp_d, ncv.t[:], reads=[ncv], writes=[D_("ncp")])
            yield 0.5

        def run_until(gen, marker):
            for v in gen:
                if v == marker:
                    return

        def drain(gen):
            for _ in gen:
                pass

        def interleave(ga, gb):
            ta = tb = 0.0
            a_alive = b_alive = True
            while a_alive or b_alive:
                if a_alive and (not b_alive or ta <= tb):
                    try:
                        ta += next(ga)
                    except StopIteration:
                        a_alive = False
                else:
                    try:
                        tb += next(gb)
                    except StopIteration:
                        b_alive = False

        def until(gen, marker):
            for v in gen:
                if v == marker:
                    return
                yield v if not isinstance(v, str) else 0.0

        def run_pipeline(units, overlap_tail=True):
            cxs = []
            for i, a in enumerate(units):
                cx_ = {"args": a[0:6], "idx": i}
                if len(a) > 6:
                    cx_.update(a[6])
                cxs.append(cx_)
            n = len(cxs)
            fgs = {0: front(cxs[0])}
            drain(until(fgs[0], "PHASE_B"))
            drain(fgs[0])
            tail = None
            for u in range(n):
                bg = back(cxs[u])
                if u + 1 < n:
                    fg = front(cxs[u + 1])
                    s_phase = until(fg, "PHASE_B")
                    if tail is not None and overlap_tail:
                        interleave(s_phase, tail)
                    else:
                        if tail is not None:
                            drain(tail)
                        drain(s_phase)
                    interleave(fg, until(bg, "TAIL"))
                else:
                    if tail is not None:
                        drain(tail)
                    drain(until(bg, "TAIL"))
                tail = bg
            drain(tail)

        run_pipeline([(i, 0, 128, 512 * (i + 1), i, None) for i in range(NPT)])
        P.barrier()
        sunits = []
        regs = [{"kiB": TBuf(f"kiS{r_}"), "kTB": TBuf(f"kTS{r_}"), "v1B": TBuf(f"v1S{r_}"), "kc0": r_ * 2048, "kt0": r_ * 16} for r_ in range(2)]
        for s_ in range(4):
            sunits.append((NPT + s_ // 2, (s_ % 2) * 64, 64, 1088, None, s_, regs[s_ % 2]))
        run_pipeline(sunits, overlap_tail=False)
        P.pop()
        if stage == 2:
            P.finish()
            return nc

        P.push()
        alloc_norm_tmp()
        g2B = P.sb("g2B", [128, D], F32)
        P.dma("sync", g2B.t[:], g2_d, writes=[g2B])
        gfB = P.sb("gfB", [128, D], F32)
        P.dma("sync", gfB.t[:], gf_d, writes=[gfB])
        xnT = P.sb("xnT", [128, 8, TBK], BF16)
        gT = P.sb("gT", [128, 22, TBK], BF16)
        hblk = P.sb("hblk", [128, BLK, D], F32)
        ST = Streamer(2816)
        sA_r = Rot([P.sb(f"sA{i}", [128, SUB], F32) for i in range(2)])
        oT_r = Rot([P.sb(f"oT{i}", [128, SUB], F32) for i in range(2)])
        yt_r = Rot([P.sb(f"yt{i}", [128, D], F32) for i in range(2)])
        for blk in range(2):
            tok0 = blk * TBK
            P.dma("sync", xnT.t[:], mT_s[:, :, tok0:tok0 + TBK], reads=[D_("mT_s")], writes=[xnT])
            for tt in range(BLK):
                P.dma("sync", hblk.t[:, tt, :], h_s[tok0 + tt * 128: tok0 + (tt + 1) * 128, :], reads=[D_("h_s")], writes=[hblk])
            proj_back(ST, xnT, 8, wo_d, hblk, 1.0)
            for tt in range(BLK):
                norm_transpose(hblk.t[:, tt, :], hblk, g2B, xnT, tt * 128, 6 + tt % 2)
            ffn_in(ST, xnT, gT, w2i_d)
            proj_back(ST, gT, 22, w2o_d, hblk, 0.5)
            for tt in range(BLK):
                rstd = rstd_of(hblk.t[:, tt, :], hblk)
                yt = yt_r.next()
                P.op("vector", lambda e: e.scalar_tensor_tensor(out=yt.t[:], in0=hblk.t[:, tt, :], scalar=rstd.t[:, 0:1], in1=gfB.t[:],
                                                                op0=ALU.mult, op1=ALU.mult), reads=[hblk, rstd, gfB], writes=[yt])
                P.dma("sync", y_d[tok0 + tt * 128: tok0 + (tt + 1) * 128, :], yt.t[:], reads=[yt], writes=[D_("y")])
        P.pop()
        P.finish()
    return nc


_NC_CACHE = {}


def _rope_table(pos):
    inv = (10000.0 ** (-np.arange(0, 64, 2, dtype=np.float32) / np.float32(64))).astype(np.float32)
    ang = pos.astype(np.float32)[:, None] * inv[None, :]
    return np.concatenate([np.cos(ang), np.sin(ang)], axis=-1).astype(np.float32)


def kernel(x_prompt, x_sample, cache_k, cache_v, cache_idx_k, state_conv,
           ffn1_norm, ffn1_w_in, ffn1_w_out, mix_norm, w_in, b_gate,
           conv_w, conv_b, conv_ln_g, conv_ln_b, conv_w_out, attn_w_out, w_out,
           ffn2_norm, ffn2_w_in, ffn2_w_out, final_norm):
    f = lambda a: np.ascontiguousarray(np.asarray(a, dtype=np.float32))
    x_prompt, x_sample = f(x_prompt), f(x_sample)
    cache_k, cache_v, cache_idx_k, state_conv = f(cache_k), f(cache_v), f(cache_idx_k), f(state_conv)
    if "nc" not in _NC_CACHE:
        _NC_CACHE["nc"] = build_program()
    nc = _NC_CACHE["nc"]

    def bc(v):
        return np.ascontiguousarray(np.broadcast_to(f(v).reshape(1, D), (128, D)))

    def colT(v, n):
        return np.ascontiguousarray(f(v).reshape(n, 128).T)

    shared = {
        "g1": bc(ffn1_norm[0]), "gm": bc(mix_norm[0]), "g2": bc(ffn2_norm[0]), "gf": bc(final_norm),
        "w1i": f(ffn1_w_in[0]), "w1o": f(ffn1_w_out[0]), "win": f(w_in[0]),
        "bgT": colT(b_gate[0], 16),
        "cwT": np.ascontiguousarray(f(conv_w[0]).reshape(31, 4, 128).transpose(2, 1, 0)),
        "cbT": colT(conv_b[0], 4), "lngT": colT(conv_ln_g[0], 4), "lnbT": colT(conv_ln_b[0], 4),
        "wco": f(conv_w_out[0]), "wao": f(attn_w_out[0]), "wo": f(w_out[0]),
        "w2i": f(ffn2_w_in[0]), "w2o": f(ffn2_w_out[0]),
    }
    p = np.arange(128)
    in_maps = []
    for c in range(8):
        b, j = c // 4, c % 4
        xs = [x_prompt[b, (4 * i + j) * 128:(4 * i + j + 1) * 128] for i in range(NPT)]
        xs += [x_sample[4 * c + s] for s in range(4)]
        cs = np.zeros((128, NT, 64), np.float32)
        for i in range(NPT):
            cs[:, i, :] = _rope_table((4 * i + j) * 128 + p)
        for t in range(2):
            cs[:, NPT + t, :] = _rope_table(1024 + (p % 64))
        lim = (128 * j + 64 + 64 * (p >= 64)).astype(np.float32).reshape(128, 1)
        sel = np.zeros((128, 5), np.float32)
        if j >= 1:
            sel[:, j - 1] = 1.0
        else:
            sel[:, 4] = 1.0
        m = dict(shared)
        m.update({
            "x": np.ascontiguousarray(np.concatenate(xs, axis=0)),
            "ck": np.ascontiguousarray(cache_k[0, 4 * c:4 * c + 4].reshape(4, 1024, 128)),
            "cv": np.ascontiguousarray(cache_v[0, 4 * c:4 * c + 4].reshape(4, 1024, 128)),
            "cik": np.ascontiguousarray(cache_idx_k[0, 4 * c:4 * c + 4]),
            "sconv": np.ascontiguousarray(state_conv[0, 4 * c:4 * c + 4]),
            "cs": cs, "limrel": lim, "sel": sel,
        })
        in_maps.append(m)

    res = run_bass_kernel_spmd(nc, in_maps, core_ids=list(range(8)))
    R = res.results
    y_p = np.zeros((2, 8192, D), np.float32)
    y_s = np.zeros((32, 64, D), np.float32)
    nk_p = np.zeros((1, 2, 8192, 2, 64), np.float32)
    nv_p = np.zeros((1, 2, 8192, 2, 64), np.float32)
    ni_p = np.zeros((1, 2, 8192, 64), np.float32)
    nc_p = np.zeros((1, 2, 30, 512), np.float32)
    nk_s = np.zeros((1, 32, 64, 2, 64), np.float32)
    nv_s = np.zeros((1, 32, 64, 2, 64), np.float32)
    ni_s = np.zeros((1, 32, 64, 64), np.float32)
    nc_s = np.zeros((1, 32, 30, 512), np.float32)
    for c in range(8):
        b, j = c // 4, c % 4
        r = R[c]
        for i in range(NPT):
            g = slice((4 * i + j) * 128, (4 * i + j + 1) * 128)
            l = slice(i * 128, (i + 1) * 128)
            y_p[b, g] = r["y"][l]
            nk_p[0, b, g] = r["nk"][l].reshape(128, 2, 64)
            nv_p[0, b, g] = r["nv"][l].reshape(128, 2, 64)
            ni_p[0, b, g] = r["nki"][l]
        for s in range(4):
            l = slice(NPT * 128 + s * 64, NPT * 128 + (s + 1) * 64)
            y_s[4 * c + s] = r["y"][l]
            nk_s[0, 4 * c + s] = r["nk"][l].reshape(64, 2, 64)
            nv_s[0, 4 * c + s] = r["nv"][l].reshape(64, 2, 64)
            ni_s[0, 4 * c + s] = r["nki"][l]
            nc_s[0, 4 * c + s] = r["ncs"][s]
        if j == 3:
            nc_p[0, b] = r["ncp"]
    return (y_p, y_s, nk_p, nv_p, ni_p, nc_p, nk_s, nv_s, ni_s, nc_s)
```

```python
import os
import numpy as np
from contextlib import ExitStack
import concourse.bass as bass
import concourse.mybir as mybir
from concourse.bass_utils import run_bass_kernel_spmd

F32 = mybir.dt.float32
BF16 = mybir.dt.bfloat16
I32 = mybir.dt.int32
AF = mybir.ActivationFunctionType
ALU = mybir.AluOpType
AX = mybir.AxisListType

D = 1024
DFF = 2816
NT = 18
NTOK = NT * 128
NPT = 16
BLK = 9
NSUB = 3
SUB = 384
TBK = BLK * 128
EPS = 1e-6
BIG = 1.0e30
NITER = 14
TOPK = 256.0
INCOLS = 4424


class TBuf:
    def __init__(self, name, t=None):
        self.name = name
        self.t = t
        self.w = {}
        self.r = {}
        self.dsem = None
        self.dcount = 0


class Prog:
    ENGS = ("tensor", "vector", "scalar", "gpsimd", "sync")

    def __init__(self, nc):
        self.nc = nc
        self.stacks = [ExitStack()]
        self.sems = {}
        self.count = {}
        self.waited = {}
        self.drams = {}
        self.dma_bufs = []
        self.uid = 0

    def __enter__(self):
        self.stacks[0].__enter__()
        for e in self.ENGS:
            self.sems[e] = self.stacks[0].enter_context(self.nc.semaphore("sem_" + e))
            self.count[e] = 0
            self.waited[e] = {}
        return self

    def __exit__(self, *a):
        while len(self.stacks) > 1:
            self.stacks.pop().close()
        return self.stacks[0].__exit__(*a)

    def push(self):
        st = ExitStack()
        st.__enter__()
        self.stacks.append(st)

    def pop(self):
        self.barrier()
        self.stacks.pop().close()

    def sb(self, name, shape, dtype):
        self.uid += 1
        return TBuf(name, self.stacks[-1].enter_context(self.nc.sbuf_tensor(f"{name}_{self.uid}", list(shape), dtype)))

    def ps(self, name, shape, dtype):
        return TBuf(name, self.stacks[0].enter_context(self.nc.psum_tensor(name, list(shape), dtype)))

    def dram(self, name):
        if name not in self.drams:
            self.drams[name] = TBuf("d_" + name)
        return self.drams[name]

    def _eng(self, name):
        return getattr(self.nc, name)

    def _wait(self, engname, key, val):
        if key == engname and engname == "tensor":
            return
        w = self.waited[engname]
        if w.get(key, 0) >= val:
            return
        if key in self.ENGS:
            assert self.count[key] >= val, f"wait on un-signalled instruction {key} {val} {self.count[key]}"
        w[key] = val
        self._eng(engname).wait_ge(self.sems[key], val)

    def _deps(self, engname, reads, writes):
        for b in reads:
            for k, v in b.w.items():
                self._wait(engname, k, v)
        for b in writes:
            for k, v in b.w.items():
                self._wait(engname, k, v)
            for k, v in b.r.items():
                self._wait(engname, k, v)

    @staticmethod
    def _rec(d, key, val):
        if d.get(key, 0) < val:
            d[key] = val

    def op(self, engname, fn, reads=(), writes=(), signal=True):
        self._deps(engname, reads, writes)
        inst = fn(self._eng(engname))
        if signal:
            self.count[engname] += 1
            inst.then_inc(self.sems[engname], 1)
            val = self.count[engname]
        else:
            val = self.count[engname] + 1
        for b in reads:
            self._rec(b.r, engname, val)
        for b in writes:
            self._rec(b.w, engname, val)
        return inst

    def _dsem(self, b0):
        if b0.dsem is None:
            key = "dma_" + b0.name + str(len(self.dma_bufs))
            b0.dsem = key
            self.sems[key] = self.stacks[0].enter_context(self.nc.semaphore("s" + str(len(self.dma_bufs))))
            self.dma_bufs.append(b0)
        return b0.dsem

    def dma(self, queue, out_ap, in_ap, reads=(), writes=(), **kw):
        self._deps(queue, reads, writes)
        b0 = writes[0]
        key = self._dsem(b0)
        inst = self._eng(queue).dma_start(out=out_ap, in_=in_ap, **kw)
        b0.dcount += 16
        inst.then_inc(self.sems[key], 16)
        for b in reads:
            self._rec(b.r, key, b0.dcount)
        for b in writes:
            self._rec(b.w, key, b0.dcount)
        return inst

    def collective(self, fn, reads, writes):
        self._deps("gpsimd", reads, writes)
        b0 = writes[0]
        key = self._dsem(b0)
        inst = fn(self.nc.gpsimd)
        b0.dcount += 1
        inst.then_inc(self.sems[key])
        for b in reads:
            self._rec(b.r, key, b0.dcount)
        for b in writes:
            self._rec(b.w, key, b0.dcount)

    def sync_to(self, engname, bufs):
        self._deps(engname, (), bufs)

    def barrier(self):
        for e in self.ENGS:
            for e2 in self.ENGS:
                if e2 != e and self.count[e2] > 0:
                    self._wait(e, e2, self.count[e2])
            for b in self.dma_bufs:
                if b.dcount:
                    self._wait(e, b.dsem, b.dcount)

    def finish(self):
        for b in self.dma_bufs:
            if b.dcount:
                self._wait("sync", b.dsem, b.dcount)
        for e in self.ENGS:
            if e != "sync" and self.count[e] > 0:
                self._wait("sync", e, self.count[e])


class Rot:
    def __init__(self, bufs):
        self.bufs = bufs
        self.i = 0

    def next(self):
        b = self.bufs[self.i % len(self.bufs)]
        self.i += 1
        return b


def build_program(stage=3):
    nc = bass.Bass("TRN2", target_bir_lowering=False)

    def din(name, shape, dt=F32):
        return nc.dram_tensor(name, list(shape), dt, kind="ExternalInput").ap()

    def dout(name, shape, dt=F32):
        return nc.dram_tensor(name, list(shape), dt, kind="ExternalOutput").ap()

    def dscr(name, shape, dt):
        return nc.dram_tensor(name, list(shape), dt).ap()

    x_d = din("x", [NTOK, D])
    ck_d = din("ck", [4, 1024, 128])
    cv_d = din("cv", [4, 1024, 128])
    cik_d = din("cik", [4, 1024, 64])
    sconv_d = din("sconv", [4, 30, 512])
    g1_d = din("g1", [128, D])
    gm_d = din("gm", [128, D])
    g2_d = din("g2", [128, D])
    gf_d = din("gf", [128, D])
    w1i_d = din("w1i", [D, 2 * DFF])
    w1o_d = din("w1o", [DFF, D])
    win_d = din("win", [D, INCOLS])
    bg_d = din("bgT", [128, 16])
    cw_d = din("cwT", [128, 4, 31])
    cb_d = din("cbT", [128, 4])
    lg_d = din("lngT", [128, 4])
    lb_d = din("lnbT", [128, 4])
    wco_d = din("wco", [512, D])
    wao_d = din("wao", [512, D])
    wo_d = din("wo", [D, D])
    w2i_d = din("w2i", [D, 2 * DFF])
    w2o_d = din("w2o", [DFF, D])
    cs_d = din("cs", [128, NT, 64])
    lim_d = din("limrel", [128, 1])
    sel_d = din("sel", [128, 5])

    y_d = dout("y", [NTOK, D])
    nk_d = dout("nk", [NTOK, 128])
    nv_d = dout("nv", [NTOK, 128])
    nki_d = dout("nki", [NTOK, 64])
    ncp_d = dout("ncp", [30, 512])
    ncs_d = dout("ncs", [4, 30, 512])

    h_s = dscr("h_s", [NTOK, D], F32)
    gate_s = dscr("gate_s", [128, 16, NTOK], BF16)
    qT_s = dscr("qT_s", [128, 4, NTOK], BF16)
    qiT_s = dscr("qiT_s", [64, 8, NTOK], BF16)
    wi_s = dscr("wi_s", [NTOK, 8], F32)
    cT_s = dscr("cT_s", [128, 4, NTOK], F32)
    mT_s = dscr("mT_s", [128, 8, NTOK], BF16)
    kvmy = dscr("kvmy", [192, NPT * 128], BF16)
    kvmyv = dscr("kvmyv", [128, NPT * 128], BF16)
    kvsm = dscr("kvsm", [320, 256], BF16)
    ctmy = dscr("ctmy", [128, 4 * NPT * 32], F32)
    kvg = dscr("kvg", [4 * 192, NPT * 128], BF16)
    kvgv = dscr("kvgv", [4 * 128, NPT * 128], BF16)
    ctg = dscr("ctg", [4 * 128, 4 * NPT * 32], F32)

    P = Prog(nc)
    with P:
        D_ = P.dram
        psb = [P.ps(f"ps{i}", [128, 512], F32) for i in range(8)]

        def bfv(i):
            return psb[i].t[:].bitcast(BF16)

        ident = P.sb("ident", [128, 128], BF16)
        identF = P.sb("identF", [128, 128], F32)
        it = P.sb("iota_tmp", [128, 128], I32)
        P.op("gpsimd", lambda e: e.iota(it.t[:], pattern=[[1, 128]], base=0, channel_multiplier=-1), writes=[it])
        P.op("vector", lambda e: e.tensor_scalar(out=ident.t[:], in0=it.t[:], scalar1=0.0, scalar2=None, op0=ALU.is_equal),
             reads=[it], writes=[ident])
        P.op("vector", lambda e: e.tensor_scalar(out=identF.t[:], in0=it.t[:], scalar1=0.0, scalar2=None, op0=ALU.is_equal),
             reads=[it], writes=[identF])
        cs = P.sb("cs", [128, NT, 64], F32)
        P.dma("sync", cs.t[:], cs_d, writes=[cs])
        bgT = P.sb("bgT", [128, 16], F32)
        P.dma("sync", bgT.t[:], bg_d, writes=[bgT])
        cwT = P.sb("cwT", [128, 4, 31], F32)
        P.dma("sync", cwT.t[:], cw_d, writes=[cwT])
        cbT = P.sb("cbT", [128, 4], F32)
        P.dma("sync", cbT.t[:], cb_d, writes=[cbT])
        lngT = P.sb("lngT", [128, 4], F32)
        P.dma("sync", lngT.t[:], lg_d, writes=[lngT])
        lnbT = P.sb("lnbT", [128, 4], F32)
        P.dma("sync", lnbT.t[:], lb_d, writes=[lnbT])
        limrel = P.sb("limrel", [128, 1], F32)
        P.dma("sync", limrel.t[:], lim_d, writes=[limrel])
        sel = P.sb("sel", [128, 5], F32)
        P.dma("sync", sel.t[:], sel_d, writes=[sel])
        epsc = P.sb("epsc", [128, 1], F32)
        P.op("vector", lambda e: e.memset(epsc.t[:], EPS), writes=[epsc])

        ssq_r = Rot([P.sb(f"ssq{i}", [128, 1], F32) for i in range(2)])
        rstd_r = Rot([P.sb(f"rstd{i}", [128, 1], F32) for i in range(2)])
        sh = {}

        def alloc_norm_tmp():
            sh["xs_r"] = Rot([P.sb(f"xs{i}", [128, D], BF16) for i in range(2)])
            sh["junk"] = P.sb("junk", [128, D], BF16)

        def rstd_of(src_ap, srcbuf):
            ssq = ssq_r.next()
            rstd = rstd_r.next()
            junk = sh["junk"]
            P.op("scalar", lambda e: e.activation(out=junk.t[:], in_=src_ap, func=AF.Square, accum_out=ssq.t[:, 0:1]),
                 reads=[srcbuf], writes=[junk, ssq])
            P.op("scalar", lambda e: e.activation(out=rstd.t[:], in_=ssq.t[:], func=AF.Sqrt, bias=epsc.t[:, 0:1], scale=1.0 / D),
                 reads=[ssq, epsc], writes=[rstd])
            P.op("vector", lambda e: e.reciprocal(out=rstd.t[:], in_=rstd.t[:]), reads=[rstd], writes=[rstd])
            return rstd

        def norm_transpose(src_ap, srcbuf, gB, dstT, col0, pbank):
            rstd = rstd_of(src_ap, srcbuf)
            xs = sh["xs_r"].next()
            P.op("vector", lambda e: e.scalar_tensor_tensor(out=xs.t[:], in0=src_ap, scalar=rstd.t[:, 0:1], in1=gB.t[:],
                                                            op0=ALU.mult, op1=ALU.mult),
                 reads=[srcbuf, rstd, gB], writes=[xs])
            pv = bfv(pbank)
            for kc in range(8):
                P.op("tensor", lambda e: e.transpose(out=pv[:, kc * 128:(kc + 1) * 128], in_=xs.t[:, kc * 128:(kc + 1) * 128],
                                                     identity=ident.t[:]),
                     reads=[xs, ident], writes=[psb[pbank]], signal=(kc == 7))
            P.op("scalar", lambda e: e.activation(out=dstT.t[:, :, col0:col0 + 128],
                                                  in_=pv.rearrange("p (k n) -> p k n", k=8), func=AF.Copy),
                 reads=[psb[pbank]], writes=[dstT])

        class Streamer:
            def __init__(self, maxel):
                self.stage = [P.sb(f"stg{i}", [128, maxel], F32) for i in range(2)]
                self.wb = [P.sb(f"wbf{i}", [128, maxel], BF16) for i in range(2)]
                self.n = 0
                self.nd = 0

            def load(self, pieces, KC, ncols, dst=None):
                s = self.n % 2
                self.n += 1
                stg = self.stage[s]
                sv = stg.t[:, 0:KC * ncols].rearrange("p (k n) -> p k n", k=KC)
                for (off, ap, w) in pieces:
                    P.dma("sync", sv[:, :, off:off + w], ap, writes=[stg])
                if dst is None:
                    wbuf = self.wb[self.nd % 2]
                    self.nd += 1
                    ov = wbuf.t[:, 0:KC * ncols].rearrange("p (k n) -> p k n", k=KC)
                else:
                    wbuf, ov = dst
                if self.n % 2 == 0:
                    P.op("scalar", lambda e: e.activation(out=ov, in_=sv, func=AF.Copy), reads=[stg], writes=[wbuf])
                else:
                    P.op("vector", lambda e: e.tensor_copy(out=ov, in_=sv), reads=[stg], writes=[wbuf])
                return wbuf, ov

        sA_r = None
        oT_r = None

        def ffn_in(ST, xnT, gT, w_in_ap):
            Wv = w_in_ap.rearrange("(kc p) n -> p kc n", p=128)

            def load(j):
                return ST.load([(0, Wv[:, :, j * 128:(j + 1) * 128], 128),
                                (128, Wv[:, :, DFF + j * 128:DFF + (j + 1) * 128], 128)], 8, 256)
            nxt = load(0)
            cnt = 0
            for j in range(22):
                wbuf, wv = nxt
                if j + 1 < 22:
                    nxt = load(j + 1)
                for sbk in range(NSUB):
                    pa = psb[cnt % 2]
                    pb = psb[2 + cnt % 2]
                    cnt += 1
                    cols = slice(sbk * SUB, (sbk + 1) * SUB)
                    for kc in range(8):
                        P.op("tensor", lambda e: e.matmul(pa.t[:, 0:SUB], lhsT=wv[:, kc, 0:128], rhs=xnT.t[:, kc, cols],
                                                          start=(kc == 0), stop=(kc == 7)),
                             reads=[wbuf, xnT], writes=[pa], signal=(kc == 7))
                    for kc in range(8):
                        P.op("tensor", lambda e: e.matmul(pb.t[:, 0:SUB], lhsT=wv[:, kc, 128:256], rhs=xnT.t[:, kc, cols],
                                                          start=(kc == 0), stop=(kc == 7)),
                             reads=[wbuf, xnT], writes=[pb], signal=(kc == 7))
                    sA = sA_r.next()
                    P.op("scalar", lambda e: e.activation(out=sA.t[:, 0:SUB], in_=pa.t[:, 0:SUB], func=AF.Silu),
                         reads=[pa], writes=[sA])
                    P.op("vector", lambda e: e.tensor_tensor(out=gT.t[:, j, cols], in0=pb.t[:, 0:SUB], in1=sA.t[:, 0:SUB], op=ALU.mult),
                         reads=[pb, sA], writes=[gT])

        def proj_back(ST, srcT, KC, w_ap, hblk, scale):
            Wv = w_ap.rearrange("(kc p) n -> p kc n", p=128)

            def load(n):
                return ST.load([(0, Wv[:, :, n * 128:(n + 1) * 128], 128)], KC, 128)
            nxt = load(0)
            cnt = 0
            for n in range(8):
                wbuf, wv = nxt
                if n + 1 < 8:
                    nxt = load(n + 1)
                for sbk in range(NSUB):
                    po = psb[4 + cnt % 2]
                    pt = psb[6 + cnt % 2]
                    cnt += 1
                    cols = slice(sbk * SUB, (sbk + 1) * SUB)
                    for kc in range(KC):
                        P.op("tensor", lambda e: e.matmul(po.t[:, 0:SUB], lhsT=wv[:, kc, :], rhs=srcT.t[:, kc, cols],
                                                          start=(kc == 0), stop=(kc == KC - 1)),
                             reads=[wbuf, srcT], writes=[po], signal=(kc == KC - 1))
                    oT = oT_r.next()
                    P.op("scalar", lambda e: e.activation(out=oT.t[:, 0:SUB], in_=po.t[:, 0:SUB], func=AF.Copy),
                         reads=[po], writes=[oT])
                    for t3 in range(3):
                        P.op("tensor", lambda e: e.transpose(out=pt.t[:, t3 * 128:(t3 + 1) * 128], in_=oT.t[:, t3 * 128:(t3 + 1) * 128],
                                                             identity=identF.t[:]),
                             reads=[oT, identF], writes=[pt], signal=(t3 == 2))
                    hv = hblk.t[:, sbk * 3:sbk * 3 + 3, n * 128:(n + 1) * 128]
                    P.op("vector", lambda e: e.scalar_tensor_tensor(out=hv, in0=pt.t[:, 0:SUB].rearrange("p (t n) -> p t n", t=3),
                                                                    scalar=float(scale), in1=hv, op0=ALU.mult, op1=ALU.add),
                         reads=[pt, hblk], writes=[hblk])

        P.push()
        alloc_norm_tmp()
        g1B = P.sb("g1B", [128, D], F32)
        P.dma("sync", g1B.t[:], g1_d, writes=[g1B])
        gmB = P.sb("gmB", [128, D], F32)
        P.dma("sync", gmB.t[:], gm_d, writes=[gmB])
        xnT = P.sb("xnT", [128, 8, TBK], BF16)
        gT = P.sb("gT", [128, 22, TBK], BF16)
        hblk = P.sb("hblk", [128, BLK, D], F32)
        ST = Streamer(2816)
        sA_r = Rot([P.sb(f"sA{i}", [128, SUB], F32) for i in range(2)])
        oT_r = Rot([P.sb(f"oT{i}", [128, SUB], F32) for i in range(2)])
        wtok = gT.t[:].rearrange("p a b -> p (a b)")[:, 0:8 * 1352].rearrange("p (k n) -> p k n", k=8)
        ropeT = Rot([P.sb(f"ropeT{i}", [128, 8, 32], F32) for i in range(8)])
        qb2_r = Rot([P.sb(f"qb2{i}", [128, 512], BF16) for i in range(2)])
        qib_r = Rot([P.sb(f"qib{i}", [128, 512], BF16) for i in range(2)])
        kf_r = Rot([P.sb(f"kf{i}", [128, 128], F32) for i in range(2)])
        vf_r = Rot([P.sb(f"vf{i}", [128, 128], F32) for i in range(2)])
        kif_r = Rot([P.sb(f"kif{i}", [128, 64], F32) for i in range(2)])
        kb_r = Rot([P.sb(f"kb{i}", [128, 128], BF16) for i in range(2)])
        vb_r = Rot([P.sb(f"vb{i}", [128, 128], BF16) for i in range(2)])
        kib_r = Rot([P.sb(f"kib{i}", [128, 64], BF16) for i in range(2)])
        wis_r = Rot([P.sb(f"wis{i}", [128, 8], F32) for i in range(2)])
        qTt_r = Rot([P.sb(f"qTt{i}", [128, 4, 128], BF16) for i in range(2)])
        kTt_r = Rot([P.sb(f"kTt{i}", [128, 128], BF16) for i in range(2)])
        qiTt_r = Rot([P.sb(f"qiTt{i}", [64, 8, 128], BF16) for i in range(2)])
        kiTt_r = Rot([P.sb(f"kiTt{i}", [64, 128], BF16) for i in range(2)])
        cbuf_r = Rot([P.sb(f"cbuf{i}", [128, SUB], F32) for i in range(2)])
        gbuf_r = Rot([P.sb(f"gbuf{i}", [128, SUB], BF16) for i in range(2)])

        def rope(src4, srcbuf, tg, o1, o2, obuf, shape):
            x1 = src4[:, :, :, 0:32]
            x2 = src4[:, :, :, 32:64]
            a, b = shape
            cosB = cs.t[:, tg, 0:32].unsqueeze(1).unsqueeze(1).broadcast_to([128, a, b, 32])
            sinB = cs.t[:, tg, 32:64].unsqueeze(1).unsqueeze(1).broadcast_to([128, a, b, 32])
            ts = [ropeT.next() for _ in range(4)]
            tv = [t.t[:, 0:a * b, :].rearrange("p (a b) d -> p a b d", a=a) for t in ts]
            for (tb, tvv, xx, cc) in ((ts[0], tv[0], x1, cosB), (ts[1], tv[1], x2, sinB), (ts[2], tv[2], x2, cosB), (ts[3], tv[3], x1, sinB)):
                P.op("vector", lambda e: e.tensor_tensor(out=tvv, in0=xx, in1=cc, op=ALU.mult), reads=[srcbuf, cs], writes=[tb])
            P.op("gpsimd", lambda e: e.tensor_tensor(out=o1, in0=tv[0], in1=tv[1], op=ALU.subtract), reads=[ts[0], ts[1]], writes=[obuf])
            P.op("gpsimd", lambda e: e.tensor_tensor(out=o2, in0=tv[2], in1=tv[3], op=ALU.add), reads=[ts[2], ts[3]], writes=[obuf])

        WinV = win_d.rearrange("(kc p) n -> p kc n", p=128)
        groups = [[0, 1, 2, 3], [4, 5, 6, 7]]
        for blk in range(2):
            tok0 = blk * TBK
            for tt in range(BLK):
                P.dma("sync", hblk.t[:, tt, :], x_d[tok0 + tt * 128: tok0 + (tt + 1) * 128, :], writes=[hblk])
            if stage == 0.05:
                P.barrier(); P.finish(); return nc
            for tt in range(BLK):
                norm_transpose(hblk.t[:, tt, :], hblk, g1B, xnT, tt * 128, 6 + tt % 2)
            if stage == 0.1:
                P.barrier(); P.finish(); return nc
            ffn_in(ST, xnT, gT, w1i_d)
            if stage == 0.2:
                P.barrier(); P.finish(); return nc
            proj_back(ST, gT, 22, w1o_d, hblk, 0.5)
            if stage == 0.3:
                P.barrier(); P.finish(); return nc
            for tt in range(BLK):
                norm_transpose(hblk.t[:, tt, :], hblk, gmB, xnT, tt * 128, 6 + tt % 2)
            for tt in range(BLK):
                P.dma("sync", h_s[tok0 + tt * 128: tok0 + (tt + 1) * 128, :], hblk.t[:, tt, :], reads=[hblk], writes=[D_("h_s")])
            wtok_pieces = []
            c0 = 0
            while c0 < 1352:
                w = min(256, 1352 - c0)
                wtok_pieces.append((c0, w))
                c0 += w

            def load_wtok(n):
                for _ in range(n):
                    if wtok_pieces:
                        c0_, w_ = wtok_pieces.pop(0)
                        ST.load([(0, WinV[:, :, c0_:c0_ + w_], w_)], 8, w_, dst=(gT, wtok[:, :, c0_:c0_ + w_]))
            if stage == 0.7:
                P.barrier(); P.finish(); return nc
            def loadc(cc):
                return ST.load([(0, WinV[:, :, 1352 + cc * 128:1352 + (cc + 1) * 128], 128),
                                (128, WinV[:, :, 1864 + cc * 128:1864 + (cc + 1) * 128], 128)], 8, 256)
            nxt = loadc(0)
            cnt = 0
            for cc in range(4):
                wbuf, wv = nxt
                if cc + 1 < 4:
                    nxt = loadc(cc + 1)
                load_wtok(2)
                for sbk in range(NSUB):
                    pa = psb[cnt % 2]
                    pb = psb[2 + cnt % 2]
                    cnt += 1
                    cols = slice(sbk * SUB, (sbk + 1) * SUB)
                    for kc in range(8):
                        P.op("tensor", lambda e: e.matmul(pa.t[:, 0:SUB], lhsT=wv[:, kc, 0:128], rhs=xnT.t[:, kc, cols],
                                                          start=(kc == 0), stop=(kc == 7)),
                             reads=[wbuf, xnT], writes=[pa], signal=(kc == 7))
                    for kc in range(8):
                        P.op("tensor", lambda e: e.matmul(pb.t[:, 0:SUB], lhsT=wv[:, kc, 128:256], rhs=xnT.t[:, kc, cols],
                                                          start=(kc == 0), stop=(kc == 7)),
                             reads=[wbuf, xnT], writes=[pb], signal=(kc == 7))
                    sA = sA_r.next()
                    P.op("scalar", lambda e: e.activation(out=sA.t[:, 0:SUB], in_=pb.t[:, 0:SUB], func=AF.Sigmoid), reads=[pb], writes=[sA])
                    cbuf = cbuf_r.next()
                    P.op("vector", lambda e: e.tensor_tensor(out=cbuf.t[:], in0=pa.t[:, 0:SUB], in1=sA.t[:, 0:SUB], op=ALU.mult),
                         reads=[pa, sA], writes=[cbuf])
                    P.dma("gpsimd", cT_s[:, cc, tok0 + sbk * SUB: tok0 + (sbk + 1) * SUB], cbuf.t[:], reads=[cbuf], writes=[D_("cT_s")])
                    for t3 in range(3):
                        tg = blk * BLK + sbk * 3 + t3
                        if tg < NPT:
                            o = (cc * NPT + tg) * 32
                            P.dma("gpsimd", ctmy[:, o:o + 32], cbuf.t[:, t3 * 128 + 96: t3 * 128 + 128], reads=[cbuf], writes=[D_("ctmy")])
            load_wtok(6)
            def zmm(tt):
                s3 = 3 * (tt % 2)
                for (bk, o0, a0, a1) in ((s3, 0, 0, 512), (s3 + 1, 0, 512, 768), (s3 + 1, 256, 1280, 1352), (s3 + 2, 0, 768, 1280)):
                    for kc in range(8):
                        P.op("tensor", lambda e: e.matmul(psb[bk].t[:, o0:o0 + a1 - a0], lhsT=xnT.t[:, kc, tt * 128:(tt + 1) * 128],
                                                          rhs=wtok[:, kc, a0:a1], start=(kc == 0), stop=(kc == 7)),
                             reads=[xnT, gT], writes=[psb[bk]], signal=(kc == 7))

            def post(tt):
                tg = blk * BLK + tt
                trow = slice(tok0 + tt * 128, tok0 + (tt + 1) * 128)
                tcol = trow
                s3 = 3 * (tt % 2)
                bq, bkv, bqi = psb[s3], psb[s3 + 1], psb[s3 + 2]
                qb2 = qb2_r.next()
                zq = bq.t[:, 0:512].rearrange("p (c g d) -> p c g d", c=2, g=4)
                oq = qb2.t[:].rearrange("p (g c d) -> p c g d", g=4, c=2)
                rope(zq, bq, tg, oq[:, :, :, 0:32], oq[:, :, :, 32:64], qb2, (2, 4))
                kf = kf_r.next()
                zk = bkv.t[:, 0:128].rearrange("p (a c d) -> p a c d", a=1, c=2)
                ok = kf.t[:].rearrange("p (a c d) -> p a c d", a=1, c=2)
                rope(zk, bkv, tg, ok[:, :, :, 0:32], ok[:, :, :, 32:64], kf, (1, 2))
                vf = vf_r.next()
                vb = vb_r.next()
                P.op("vector", lambda e: e.tensor_copy(out=vf.t[:], in_=bkv.t[:, 128:256]), reads=[bkv], writes=[vf])
                P.op("vector", lambda e: e.tensor_copy(out=vb.t[:], in_=bkv.t[:, 128:256]), reads=[bkv], writes=[vb])
                qib = qib_r.next()
                zqi = bqi.t[:, 0:512].rearrange("p (a h d) -> p a h d", a=1, h=8)
                oqi = qib.t[:].rearrange("p (a h d) -> p a h d", a=1, h=8)
                rope(zqi, bqi, tg, oqi[:, :, :, 0:32], oqi[:, :, :, 32:64], qib, (1, 8))
                kif = kif_r.next()
                zki = bkv.t[:, 256:320].rearrange("p (a h d) -> p a h d", a=1, h=1)
                oki = kif.t[:].rearrange("p (a h d) -> p a h d", a=1, h=1)
                rope(zki, bkv, tg, oki[:, :, :, 0:32], oki[:, :, :, 32:64], kif, (1, 1))
                wis = wis_r.next()
                P.op("vector", lambda e: e.tensor_scalar(out=wis.t[:], in0=bkv.t[:, 320:328], scalar1=float(8 ** -0.5), scalar2=None, op0=ALU.mult),
                     reads=[bkv], writes=[wis])
                kb = kb_r.next()
                kib = kib_r.next()
                P.op("scalar", lambda e: e.activation(out=kb.t[:], in_=kf.t[:], func=AF.Copy), reads=[kf], writes=[kb])
                P.op("scalar", lambda e: e.activation(out=kib.t[:], in_=kif.t[:], func=AF.Copy), reads=[kif], writes=[kib])
                p6 = bfv(6)
                for g in range(4):
                    P.op("tensor", lambda e: e.transpose(out=p6[:, g * 128:(g + 1) * 128], in_=qb2.t[:, g * 128:(g + 1) * 128], identity=ident.t[:]),
                         reads=[qb2, ident], writes=[psb[6]], signal=False)
                P.op("tensor", lambda e: e.transpose(out=p6[:, 512:640], in_=kb.t[:], identity=ident.t[:]),
                     reads=[kb, ident], writes=[psb[6]], signal=False)
                P.op("tensor", lambda e: e.transpose(out=p6[0:64, 640:768], in_=kib.t[:], identity=ident.t[:]),
                     reads=[kib, ident], writes=[psb[6]])
                p7 = bfv(7)
                for h in range(8):
                    P.op("tensor", lambda e: e.transpose(out=p7[0:64, h * 128:(h + 1) * 128], in_=qib.t[:, h * 64:(h + 1) * 64], identity=ident.t[:]),
                         reads=[qib, ident], writes=[psb[7]], signal=(h == 7))
                qTt = qTt_r.next()
                kTt = kTt_r.next()
                qiTt = qiTt_r.next()
                kiTt = kiTt_r.next()
                P.op("scalar", lambda e: e.activation(out=qTt.t[:], in_=p6[:, 0:512].rearrange("p (g n) -> p g n", g=4), func=AF.Copy),
                     reads=[psb[6]], writes=[qTt])
                P.op("scalar", lambda e: e.activation(out=kTt.t[:], in_=p6[:, 512:640], func=AF.Copy), reads=[psb[6]], writes=[kTt])
                P.op("scalar", lambda e: e.activation(out=kiTt.t[:], in_=p6[0:64, 640:768], func=AF.Copy), reads=[psb[6]], writes=[kiTt])
                P.op("vector", lambda e: e.tensor_copy(out=qiTt.t[:], in_=p7[0:64, :].rearrange("p (h n) -> p h n", h=8)),
                     reads=[psb[7]], writes=[qiTt])
                P.dma("gpsimd", nk_d[trow, :], kf.t[:], reads=[kf], writes=[D_("nk")])
                P.dma("gpsimd", nv_d[trow, :], vf.t[:], reads=[vf], writes=[D_("nv")])
                P.dma("gpsimd", nki_d[trow, :], kif.t[:], reads=[kif], writes=[D_("nki")])
                P.dma("gpsimd", wi_s[trow, :], wis.t[:], reads=[wis], writes=[D_("wi_s")])
                P.dma("scalar", qT_s[:, :, tcol], qTt.t[:], reads=[qTt], writes=[D_("qT_s")])
                P.dma("gpsimd", qiT_s[:, :, tcol], qiTt.t[:], reads=[qiTt], writes=[D_("qiT_s")])
                if tg < NPT:
                    kcol = slice(tg * 128, (tg + 1) * 128)
                    P.dma("scalar", kvmy[0:64, kcol], kiTt.t[:], reads=[kiTt], writes=[D_("kvmy")])
                    P.dma("scalar", kvmy[64:192, kcol], kTt.t[:], reads=[kTt], writes=[D_("kvmy")])
                    P.dma("gpsimd", kvmyv[:, kcol], vb.t[:], reads=[vb], writes=[D_("kvmyv")])
                else:
                    kcol = slice((tg - NPT) * 128, (tg - NPT + 1) * 128)
                    P.dma("scalar", kvsm[0:64, kcol], kiTt.t[:], reads=[kiTt], writes=[D_("kvsm")])
                    P.dma("scalar", kvsm[64:192, kcol], kTt.t[:], reads=[kTt], writes=[D_("kvsm")])
                    P.dma("scalar", kvsm[192:320, kcol], vb.t[:], reads=[vb], writes=[D_("kvsm")])

            if os.environ.get("K_PIPE", "1") == "1":
                zmm(0)
                for tt in range(BLK):
                    if tt + 1 < BLK:
                        zmm(tt + 1)
                    post(tt)
            else:
                for tt in range(BLK):
                    zmm(tt)
                    post(tt)
            if blk == 1:
                P.collective(lambda e: e.collective_compute("AllGather", ALU.bypass, replica_groups=groups,
                                                            ins=[kvmy.opt()], outs=[kvg.opt()]),
                             reads=[D_("kvmy")], writes=[D_("kvg")])
                P.collective(lambda e: e.collective_compute("AllGather", ALU.bypass, replica_groups=groups,
                                                            ins=[kvmyv.opt()], outs=[kvgv.opt()]),
                             reads=[D_("kvmyv")], writes=[D_("kvgv")])
                P.collective(lambda e: e.collective_compute("AllGather", ALU.bypass, replica_groups=groups,
                                                            ins=[ctmy.opt()], outs=[ctg.opt()]),
                             reads=[D_("ctmy")], writes=[D_("ctg")])
            if stage == 0.8:
                P.barrier(); P.finish(); return nc
            def loadg(pp):
                return ST.load([(0, WinV[:, :, 2376 + pp * 256:2376 + (pp + 1) * 256], 256)], 8, 256)
            nxt = loadg(0)
            cnt = 0
            for pp in range(8):
                wbuf, wv = nxt
                if pp + 1 < 8:
                    nxt = loadg(pp + 1)
                for hh in range(2):
                    gc = pp * 2 + hh
                    for sbk in range(NSUB):
                        pa = psb[cnt % 4]
                        cnt += 1
                        cols = slice(sbk * SUB, (sbk + 1) * SUB)
                        for kc in range(8):
                            P.op("tensor", lambda e: e.matmul(pa.t[:, 0:SUB], lhsT=wv[:, kc, hh * 128:(hh + 1) * 128], rhs=xnT.t[:, kc, cols],
                                                              start=(kc == 0), stop=(kc == 7)),
                                 reads=[wbuf, xnT], writes=[pa], signal=(kc == 7))
                        gbuf = gbuf_r.next()
                        P.op("scalar", lambda e: e.activation(out=gbuf.t[:], in_=pa.t[:, 0:SUB], func=AF.Sigmoid, bias=bgT.t[:, gc:gc + 1], scale=1.0),
                             reads=[pa, bgT], writes=[gbuf])
                        P.dma("gpsimd", gate_s[:, gc, tok0 + sbk * SUB: tok0 + (sbk + 1) * SUB], gbuf.t[:], reads=[gbuf], writes=[D_("gate_s")])
        P.pop()
        if stage == 1:
            P.finish()
            return nc

        if stage == 1.1:
            P.barrier(); P.finish(); return nc
        P.push()
        wao = P.sb("wao", [128, 4, D], BF16)
        wco = P.sb("wco", [128, 4, D], BF16)
        amneg = P.sb("amneg", [128, 512], F32)
        ampos = P.sb("ampos", [128, 512], F32)
        P.push()
        ST = Streamer(2048)
        for (wt_, wd_) in ((wao, wao_d), (wco, wco_d)):
            Wv = wd_.rearrange("(kc p) n -> p kc n", p=128)
            for hf in range(2):
                ST.load([(0, Wv[:, :, hf * 512:(hf + 1) * 512], 512)], 4, 512, dst=(wt_, wt_.t[:, :, hf * 512:(hf + 1) * 512]))
        iot_i = P.sb("iot_i", [128, 512], I32)
        P.op("gpsimd", lambda e: e.iota(iot_i.t[:], pattern=[[1, 512]], base=0, channel_multiplier=0), writes=[iot_i])
        am = P.sb("am", [128, 512], F32)
        P.op("vector", lambda e: e.tensor_scalar(out=am.t[:], in0=iot_i.t[:], scalar1=limrel.t[:, 0:1], scalar2=None, op0=ALU.is_ge),
             reads=[iot_i, limrel], writes=[am])
        P.op("vector", lambda e: e.tensor_scalar(out=amneg.t[:], in0=am.t[:], scalar1=-BIG, scalar2=None, op0=ALU.mult), reads=[am], writes=[amneg])
        P.op("vector", lambda e: e.tensor_scalar(out=ampos.t[:], in0=am.t[:], scalar1=BIG, scalar2=None, op0=ALU.mult), reads=[am], writes=[ampos])
        P.pop()
        if stage == 1.2:
            P.barrier(); P.finish(); return nc
        kiT_all = P.sb("kiT_all", [128, 8192], BF16)
        P.op("vector", lambda e: e.memset(kiT_all.t[64:128, :], 0.0), writes=[kiT_all])
        kT_all = P.sb("kT_all", [128, 8192], BF16)
        v1_all = P.sb("v1_all", [128, 64, 2, 65], BF16)
        score = P.sb("score", [128, 8192], F32)
        score_b = [TBuf(f"score_b{i}") for i in range(16)]
        msk2 = [P.sb(f"msk{i}", [128, 8192], BF16) for i in range(2)]
        zeros = P.sb("zeros", [1, 512], BF16)
        P.op("vector", lambda e: e.memset(zeros.t[:], 0.0), writes=[zeros])
        onesM = P.sb("onesM", [128, 128], F32)
        P.op("vector", lambda e: e.memset(onesM.t[:], 1.0 / 512.0), writes=[onesM])

        qiTu_r = Rot([P.sb(f"qiTu{i}", [128, 8, 128], BF16) for i in range(2)])
        for b_ in qiTu_r.bufs:
            P.op("gpsimd", lambda e: e.memset(b_.t[64:128, :, :], 0.0), writes=[b_])
        wiu_r = Rot([P.sb(f"wiu{i}", [128, 8], F32) for i in range(2)])
        qTz_r = Rot([[P.sb(f"qTz{i}_{c}", [128, 4, 128], BF16) for c in range(2)] for i in range(2)])
        for pr in qTz_r.bufs:
            P.op("gpsimd", lambda e: e.memset(pr[0].t[64:128, :, :], 0.0), writes=[pr[0]])
            P.op("gpsimd", lambda e: e.memset(pr[1].t[0:64, :, :], 0.0), writes=[pr[1]])
        ident4 = P.sb("ident4", [128, 4, 128], BF16)
        for g_ in range(4):
            P.op("gpsimd", lambda e: e.tensor_copy(out=ident4.t[:, g_, :], in_=ident.t[:]), reads=[ident], writes=[ident4])
        gateu_r = Rot([P.sb(f"gateu{i}", [128, 16, 128], BF16) for i in range(3)])
        cpad_r = Rot([P.sb(f"cpad{i}", [128, 4, 158], F32) for i in range(3)])
        cand_r = Rot([P.sb(f"cand{i}", [128, 5, 4, 32], F32) for i in range(2)])
        E_r = Rot([P.sb(f"E{i}", [128, 512], BF16) for i in range(3)])
        fvec = P.sb("fvec", [128, NITER + 1], F32)
        for k_ in range(NITER + 1):
            P.op("vector", lambda e: e.memset(fvec.t[:, k_:k_ + 1], float(2.0 ** -(k_ + 1))), writes=[fvec])
        frng = P.sb("frng", [128, NITER + 1], F32)
        nfrng = P.sb("nfrng", [128, NITER + 1], F32)
        sm = {n: P.sb("sm_" + n, [128, 1], F32) for n in ("rmax", "rmin", "mind", "lo", "rng", "mid", "cnt", "ind", "nmid", "cntA", "tcn")}
        rec = P.sb("rec", [128, 8], F32)
        attn_tok = P.sb("attn_tok", [128, 512], BF16)
        attnT = P.sb("attnT", [128, 4, 128], BF16)
        t1 = P.sb("t1", [128, 8, 128], F32)
        t2 = P.sb("t2", [128, 8, 128], F32)
        mTu_r = Rot([P.sb(f"mTu{i}", [128, 8, 128], BF16) for i in range(2)])
        cacc_r = Rot([P.sb(f"cacc{i}", [128, 4, 128], F32) for i in range(3)])
        tmpc = P.sb("tmpc", [128, 4, 128], F32)
        dcen = P.sb("dcen", [128, 4, 128], F32)
        dsq = P.sb("dsq", [128, 4, 128], F32)
        rsb = P.sb("rsb", [128, 128], F32)
        actT = P.sb("actT", [128, 4, 128], BF16)
        ncv = P.sb("ncv", [30, 512], F32)
        scst = ncv

        P.op("vector", lambda e: e.memset(v1_all.t[:], 1.0), writes=[v1_all])
        kiv = kiT_all.t[0:64, :].rearrange("p (i j n) -> p i j n", i=16, j=4)
        ktv = kT_all.t[:].rearrange("p (i j n) -> p i j n", i=16, j=4)
        v1v = v1_all.t[:].rearrange("p (i j) c d -> p i j c d", i=16, j=4)
        for jj in range(4):
            r0 = jj * 192
            P.dma("sync", kiv[:, :, jj, :], kvg[r0:r0 + 64, :].rearrange("p (i n) -> p i n", i=16), reads=[D_("kvg")], writes=[kiT_all])
            P.dma("sync", ktv[:, :, jj, :], kvg[r0 + 64:r0 + 192, :].rearrange("p (i n) -> p i n", i=16), reads=[D_("kvg")], writes=[kT_all])
            for c in range(2):
                P.dma("sync", v1v[:, :, jj, c, 0:64],
                      kvgv[jj * 128:(jj + 1) * 128, :].rearrange("p (i c d) -> p i c d", i=16, c=2)[:, :, c, :],
                      reads=[D_("kvgv")], writes=[v1_all])

        if stage == 1.3:
            P.barrier(); P.finish(); return nc
        kst_v = score.t[:, 4096:5120].rearrange("p (t d) -> p t d", t=8)
        kstb_v = score.t[:, 5120:5632].bitcast(BF16).rearrange("p (t d) -> p t d", t=8)
        kstT = TBuf("kstT")
        kstbT = TBuf("kstbT")

        def sample_prep(s, cx):
            kiB, kTB, v1B, kc0, kt0 = cx["kiB"], cx["kTB"], cx["v1B"], cx["kc0"], cx["kt0"]
            p7 = bfv(7)
            P.dma("sync", kst_v[:, :, 0:64], cik_d[s].rearrange("(t p) d -> p t d", p=128), writes=[kstT])
            P.op("gpsimd", lambda e: e.tensor_copy(out=kstb_v[:, :, 0:64], in_=kst_v[:, :, 0:64]), reads=[kstT], writes=[kstbT])
            for t8 in range(8):
                P.op("tensor", lambda e: e.transpose(out=p7[0:64, t8 * 128:(t8 + 1) * 128], in_=kstb_v[:, t8, 0:64], identity=ident.t[:]),
                     reads=[kstbT, ident], writes=[psb[7]], signal=(t8 == 7))
            P.op("scalar", lambda e: e.activation(out=kiT_all.t[0:64, kc0:kc0 + 1024], in_=p7[0:64, :], func=AF.Copy), reads=[psb[7]], writes=[kiB])
            P.dma("sync", kst_v, ck_d[s].rearrange("(t p) d -> p t d", p=128), writes=[kstT])
            P.op("gpsimd", lambda e: e.tensor_copy(out=kstb_v, in_=kst_v), reads=[kstT], writes=[kstbT])
            for t8 in range(8):
                P.op("tensor", lambda e: e.transpose(out=p7[:, t8 * 128:(t8 + 1) * 128], in_=kstb_v[:, t8, :], identity=ident.t[:]),
                     reads=[kstbT, ident], writes=[psb[7]], signal=(t8 == 7))
            P.op("scalar", lambda e: e.activation(out=kT_all.t[:, kc0:kc0 + 1024], in_=p7[:, :], func=AF.Copy), reads=[psb[7]], writes=[kTB])
            P.dma("sync", kst_v, cv_d[s].rearrange("(t p) d -> p t d", p=128), writes=[kstT])
            P.op("gpsimd", lambda e: e.tensor_copy(out=v1_all.t[:, kt0:kt0 + 8, :, 0:64], in_=kst_v.rearrange("p t (c d) -> p t c d", c=2)),
                 reads=[kstT], writes=[v1B])
            sc = slice(s * 64, (s + 1) * 64)
            P.dma("sync", kiT_all.t[0:64, kc0 + 1024:kc0 + 1088], kvsm[0:64, sc], reads=[D_("kvsm")], writes=[kiB])
            P.dma("sync", kT_all.t[:, kc0 + 1024:kc0 + 1088], kvsm[64:192, sc], reads=[D_("kvsm")], writes=[kTB])
            vr = 192 + (s % 2) * 64
            vc = slice((s // 2) * 128, (s // 2 + 1) * 128)
            P.dma("sync", v1_all.t[0:64, kt0 + 8, :, 0:64], kvsm[vr:vr + 64, vc].rearrange("p (c d) -> p c d", c=2),
                  reads=[D_("kvsm")], writes=[v1B])

        def front(cx):
            tile, col0, nq, L, prompt_slot, seq = cx["args"]
            tok0 = tile * 128 + col0
            tcol = slice(tok0, tok0 + nq)
            nblk = (L + 511) // 512
            qiTu = qiTu_r.next()
            wiu = wiu_r.next()
            qTz = qTz_r.next()
            gateu = gateu_r.next()
            cpad = cpad_r.next()
            cacc = cacc_r.next()
            mk = msk2[cx["idx"] % 2]
            kiB, kc0 = cx.get("kiB", kiT_all), cx.get("kc0", 0)
            if seq is not None:
                sample_prep(seq, cx)
            cx.update(qTz=qTz, gateu=gateu, cpad=cpad, msk=mk, tcol=tcol, cacc=cacc)
            P.dma("sync", qiTu.t[0:64, :, 0:nq], qiT_s[:, :, tcol], reads=[D_("qiT_s")], writes=[qiTu])
            P.dma("sync", wiu.t[0:nq, :], wi_s[tcol, :], reads=[D_("wi_s")], writes=[wiu])
            P.dma("sync", qTz[0].t[0:64, :, 0:nq], qT_s[0:64, :, tcol], reads=[D_("qT_s")], writes=[qTz[0]])
            P.dma("sync", qTz[1].t[64:128, :, 0:nq], qT_s[64:128, :, tcol], reads=[D_("qT_s")], writes=[qTz[1]])
            P.dma("sync", gateu.t[:, :, 0:nq], gate_s[:, :, tcol], reads=[D_("gate_s")], writes=[gateu])
            P.dma("sync", cpad.t[:, :, 30:30 + nq], cT_s[:, :, tcol], reads=[D_("cT_s")], writes=[cpad])
            if prompt_slot is not None:
                i = prompt_slot
                cand = cand_r.next()
                cx["cand"] = cand
                ctv = ctg.rearrange("(r p) (c i n) -> p r c i n", r=4, c=4, i=NPT)
                for r in range(4):
                    P.dma("sync", cand.t[:, r, :, :], ctv[:, r, :, i, :], reads=[D_("ctg")], writes=[cand])
                if i > 0:
                    P.dma("sync", cand.t[:, 4, :, :], ctv[:, 3, :, i - 1, :], reads=[D_("ctg")], writes=[cand])
            else:
                P.dma("sync", scst.t[:], sconv_d[seq], writes=[scst])
            if prompt_slot is not None:
                i = prompt_slot
                pass
                nk_ = 5 if i > 0 else 4
                P.op("vector", lambda e: e.tensor_scalar(out=cpad.t[:, :, 0:30], in0=cand.t[:, 0, :, 2:32], scalar1=sel.t[:, 0:1], scalar2=None,
                                                         op0=ALU.mult), reads=[cand, sel], writes=[cpad])
                for k in range(1, nk_):
                    P.op("vector", lambda e: e.scalar_tensor_tensor(out=cpad.t[:, :, 0:30], in0=cand.t[:, k, :, 2:32], scalar=sel.t[:, k:k + 1],
                                                                    in1=cpad.t[:, :, 0:30], op0=ALU.mult, op1=ALU.add),
                         reads=[cand, sel, cpad], writes=[cpad])
            else:
                for cc in range(4):
                    P.op("tensor", lambda e: e.transpose(out=psb[7].t[:, cc * 32:cc * 32 + 30], in_=scst.t[:, cc * 128:(cc + 1) * 128],
                                                         identity=identF.t[0:30, 0:30]),
                         reads=[scst, identF], writes=[psb[7]], signal=(cc == 3))
                P.op("vector", lambda e: e.tensor_copy(out=cpad.t[:, :, 0:30],
                                                       in_=psb[7].t[:, 0:128].rearrange("p (c n) -> p c n", c=4)[:, :, 0:30]),
                     reads=[psb[7]], writes=[cpad])
            cav = cacc.t[:, :, 0:nq]
            tmv = tmpc.t[:, :, 0:nq]
            P.op("gpsimd", lambda e: e.tensor_tensor(out=cav, in0=cpad.t[:, :, 0:nq], in1=cwT.t[:, :, 0:1].broadcast_to([128, 4, nq]), op=ALU.mult),
                 reads=[cpad, cwT], writes=[cacc])
            P.op("gpsimd", lambda e: e.tensor_tensor(out=cav, in0=cav, in1=cbT.t[:, :].unsqueeze(2).broadcast_to([128, 4, nq]), op=ALU.add),
                 reads=[cacc, cbT], writes=[cacc])
            for k in range(1, 31):
                P.op("gpsimd", lambda e: e.tensor_tensor(out=tmv, in0=cpad.t[:, :, k:k + nq], in1=cwT.t[:, :, k:k + 1].broadcast_to([128, 4, nq]), op=ALU.mult),
                     reads=[cpad, cwT], writes=[tmpc])
                P.op("gpsimd", lambda e: e.tensor_tensor(out=cav, in0=cav, in1=tmv, op=ALU.add), reads=[cacc, tmpc], writes=[cacc])
            yield 0.5
            hb = 0
            for bk0 in range(0, nblk, 2):
                pair = [bk_ for bk_ in (bk0, bk0 + 1) if bk_ < nblk]
                for h in range(8):
                    for bk in pair:
                        cols = min(512, L - bk * 512)
                        sc_ap = score.t[0:nq, bk * 512: bk * 512 + cols]
                        pbk = psb[hb % 5]
                        hb += 1
                        P.op("tensor", lambda e: e.matmul(pbk.t[0:nq, 0:cols], lhsT=qiTu.t[:, h, 0:nq], rhs=kiT_all.t[:, kc0 + bk * 512: kc0 + bk * 512 + cols],
                                                          start=True, stop=True),
                             reads=[qiTu, kiB], writes=[pbk])
                        P.op("scalar", lambda e: e.activation(out=pbk.t[0:nq, 0:cols], in_=pbk.t[0:nq, 0:cols], func=AF.Relu),
                             reads=[pbk], writes=[pbk])
                        if h == 0:
                            P.op("vector", lambda e: e.tensor_scalar(out=sc_ap, in0=pbk.t[0:nq, 0:cols], scalar1=wiu.t[0:nq, 0:1], scalar2=None,
                                                                     op0=ALU.mult), reads=[pbk, wiu], writes=[score_b[bk]])
                        else:
                            P.op("vector", lambda e: e.scalar_tensor_tensor(out=sc_ap, in0=pbk.t[0:nq, 0:cols], scalar=wiu.t[0:nq, h:h + 1],
                                                                            in1=sc_ap, op0=ALU.mult, op1=ALU.add),
                                 reads=[pbk, wiu, score_b[bk]], writes=[score_b[bk]])
                        yield 0.6 * cols / 512
            sbs = score_b[0:nblk]
            cx["sbs"] = sbs
            if prompt_slot is not None:
                i = prompt_slot
                dg = score.t[0:nq, i * 512:(i + 1) * 512]
                tmpd_v = mk.t[:, 0:1024].bitcast(F32)
                P.sync_to("vector", [mk])
                P.op("vector", lambda e: e.tensor_tensor(out=tmpd_v[0:nq, :], in0=dg, in1=ampos.t[0:nq, :], op=ALU.add),
                     reads=[score_b[i], ampos], writes=[mk])
                P.op("vector", lambda e: e.tensor_reduce(out=sm["mind"].t[0:nq, :], in_=tmpd_v[0:nq, :], axis=AX.X, op=ALU.min),
                     reads=[mk], writes=[sm["mind"]])
                P.op("vector", lambda e: e.tensor_tensor(out=dg, in0=dg, in1=amneg.t[0:nq, :], op=ALU.add),
                     reads=[score_b[i], amneg], writes=[score_b[i]])
                if i > 0:
                    mn_in = score.t[0:nq, 0:i * 512].rearrange("p (a b) -> p a b", b=4)[:, :, 0] if i >= 3 else score.t[0:nq, 0:i * 512]
                    P.op("vector", lambda e: e.tensor_reduce(out=sm["rmin"].t[0:nq, :], in_=mn_in, axis=AX.X, op=ALU.min),
                         reads=sbs, writes=[sm["rmin"]])
                    P.op("vector", lambda e: e.tensor_tensor(out=sm["rmin"].t[0:nq, :], in0=sm["rmin"].t[0:nq, :], in1=sm["mind"].t[0:nq, :], op=ALU.min),
                         reads=[sm["rmin"], sm["mind"]], writes=[sm["rmin"]])
                else:
                    P.op("vector", lambda e: e.tensor_copy(out=sm["rmin"].t[0:nq, :], in_=sm["mind"].t[0:nq, :]), reads=[sm["mind"]], writes=[sm["rmin"]])
            else:
                P.op("vector", lambda e: e.tensor_reduce(out=sm["rmin"].t[0:nq, :], in_=score.t[0:nq, 0:L], axis=AX.X, op=ALU.min),
                     reads=sbs, writes=[sm["rmin"]])
            yield 0.55 * nblk
            mx_in = score.t[0:nq, 0:L].rearrange("p (a b) -> p a b", b=4)[:, :, 0] if (prompt_slot is not None and L >= 2048) else score.t[0:nq, 0:L]
            P.op("vector", lambda e: e.tensor_reduce(out=sm["rmax"].t[0:nq, :], in_=mx_in, axis=AX.X, op=ALU.max),
                 reads=sbs, writes=[sm["rmax"]])
            lo, rng, mid, cnt, ind = (sm[n] for n in ("lo", "rng", "mid", "cnt", "ind"))
            P.op("vector", lambda e: e.tensor_scalar(out=lo.t[0:nq, :], in0=sm["rmin"].t[0:nq, :], scalar1=-0.01, scalar2=None, op0=ALU.add),
                 reads=[sm["rmin"]], writes=[lo])
            P.op("vector", lambda e: e.scalar_tensor_tensor(out=rng.t[0:nq, :], in0=sm["rmax"].t[0:nq, :], scalar=0.01, in1=lo.t[0:nq, :],
                                                            op0=ALU.add, op1=ALU.subtract), reads=[sm["rmax"], lo], writes=[rng])
            yield 0.55 * nblk
            yield "PHASE_B"
            La = 0
            Lh = L - La
            mkD, mkA = TBuf("mkD"), TBuf("mkA")
            P.sync_to("vector", [mk])
            if La:
                P.sync_to("scalar", [mk])
            cntA, tcn, dstp = sm["cntA"], sm["tcn"], sm["ind"]
            P.op("vector", lambda e: e.tensor_scalar(out=frng.t[0:nq, :], in0=fvec.t[0:nq, :], scalar1=rng.t[0:nq, 0:1], scalar2=None, op0=ALU.mult),
                 reads=[fvec, rng], writes=[frng])
            P.op("vector", lambda e: e.tensor_scalar(out=nfrng.t[0:nq, :], in0=fvec.t[0:nq, :], scalar1=rng.t[0:nq, 0:1], scalar2=-1.0, op0=ALU.mult, op1=ALU.mult),
                 reads=[fvec, rng], writes=[nfrng])
            P.op("vector", lambda e: e.tensor_tensor(out=mid.t[0:nq, :], in0=lo.t[0:nq, :], in1=frng.t[0:nq, 0:1], op=ALU.add), reads=[lo, frng], writes=[mid])
            for k in range(1, NITER + 1):
                if La:
                    P.op("scalar", lambda e: e.activation(out=mk.t[0:nq, Lh:L], in_=score.t[0:nq, Lh:L], func=AF.Sign, bias=mid.t[0:nq, 0:1], scale=-1.0,
                                                          accum_out=cntA.t[0:nq, 0:1]),
                         reads=sbs + [mid], writes=[mkA, cntA])
                P.op("vector", lambda e: e.tensor_scalar(out=mk.t[0:nq, 0:Lh], in0=score.t[0:nq, 0:Lh], scalar1=mid.t[0:nq, 0:1], scalar2=None,
                                                         op0=ALU.is_ge, op1=ALU.add, accum_out=cnt.t[0:nq, 0:1]),
                     reads=sbs + [mid], writes=[mkD, cnt])
                if La:
                    P.op("vector", lambda e: e.scalar_tensor_tensor(out=tcn.t[0:nq, :], in0=cnt.t[0:nq, :], scalar=2.0, in1=cntA.t[0:nq, :],
                                                                    op0=ALU.mult, op1=ALU.subtract), reads=[cnt, cntA], writes=[tcn])
                    csrc, thr_ = tcn, float(2 * TOPK - 1 - La)
                else:
                    csrc, thr_ = cnt, TOPK - 0.5
                P.op("vector", lambda e: e.scalar_tensor_tensor(out=dstp.t[0:nq, :], in0=csrc.t[0:nq, :], scalar=thr_, in1=frng.t[0:nq, k - 1:k],
                                                                op0=ALU.is_ge, op1=ALU.mult), reads=[csrc, frng], writes=[dstp])
                if k < NITER:
                    P.op("vector", lambda e: e.scalar_tensor_tensor(out=mid.t[0:nq, :], in0=dstp.t[0:nq, :], scalar=nfrng.t[0:nq, k:k + 1], in1=mid.t[0:nq, :],
                                                                    op0=ALU.add, op1=ALU.add), reads=[dstp, nfrng, mid], writes=[mid])
                else:
                    P.op("vector", lambda e: e.scalar_tensor_tensor(out=lo.t[0:nq, :], in0=dstp.t[0:nq, :], scalar=nfrng.t[0:nq, k - 1:k], in1=mid.t[0:nq, :],
                                                                    op0=ALU.add, op1=ALU.add), reads=[dstp, nfrng, mid], writes=[lo])
                yield 0.45 * nblk + 0.5
            P.op("vector", lambda e: e.tensor_scalar(out=mk.t[0:nq, 0:L], in0=score.t[0:nq, 0:L], scalar1=lo.t[0:nq, 0:1], scalar2=-30000.0,
                                                     op0=ALU.is_lt, op1=ALU.mult), reads=sbs + [lo], writes=[mk, mkD, mkA])
            yield 0.55 * nblk

        def back(cx):
            tile, col0, nq, L, prompt_slot, seq = cx["args"]
            qTz, gateu, cpad, mk, tcol, cacc = cx["qTz"], cx["gateu"], cx["cpad"], cx["msk"], cx["tcol"], cx["cacc"]
            KT = (L + 127) // 128
            kTB, v1B, kc0, kt0 = cx.get("kTB", kT_all), cx.get("v1B", v1_all), cx.get("kc0", 0), cx.get("kt0", 0)
            yield 0.5
            accv = [psb[5], psb[7]]
            for c in range(2):
                P.op("tensor", lambda e: e.matmul(accv[c].t[0:nq, 0:260], lhsT=zeros.t[0:1, 0:nq], rhs=zeros.t[0:1, 0:260], start=True, stop=False,
                                                  skip_group_check=True),
                     reads=[zeros], writes=[accv[c]])
            p6 = bfv(7)
            steps = [(kt, c) for kt in range(KT) for c in range(2)]

            def qk(si):
                kt, c = steps[si]
                ks = min(128, L - kt * 128)
                lgp = psb[3 + si % 2]
                P.op("tensor", lambda e: e.matmul(lgp.t[0:ks, 0:4 * nq], lhsT=kT_all.t[:, kc0 + kt * 128: kc0 + kt * 128 + ks],
                                                  rhs=qTz[c].t[:, :, 0:nq], start=True, stop=False),
                     reads=[kTB, qTz[c]], writes=[lgp], signal=False)
                P.op("tensor", lambda e: e.matmul(lgp.t[0:ks, 0:4 * nq], lhsT=mk.t[0:nq, kt * 128: kt * 128 + ks],
                                                  rhs=ident4.t[0:nq, :, 0:nq], start=False, stop=True),
                     reads=[mk, ident4], writes=[lgp])
            qk(0)
            for si, (kt, c) in enumerate(steps):
                ks = min(128, L - kt * 128)
                if si + 1 < len(steps):
                    qk(si + 1)
                lgp = psb[3 + si % 2]
                Eb = E_r.next()
                P.op("scalar", lambda e: e.activation(out=Eb.t[0:ks, 0:4 * nq], in_=lgp.t[0:ks, 0:4 * nq], func=AF.Exp, scale=0.125),
                     reads=[lgp], writes=[Eb])
                for g in range(4):
                    P.op("tensor", lambda e: e.matmul(accv[c].t[0:nq, g * 65:(g + 1) * 65], lhsT=Eb.t[0:ks, g * nq:(g + 1) * nq],
                                                      rhs=v1_all.t[0:ks, kt0 + kt, c, :], start=False, stop=False, skip_group_check=True),
                         reads=[Eb, v1B], writes=[accv[c]], signal=(g == 3))
                if c == 1:
                    yield 2.9 * (nq / 128.0)
            yield "TAIL"
            for c in range(2):
                P.op("tensor", lambda e: e.matmul(accv[c].t[0:nq, 0:260], lhsT=zeros.t[0:1, 0:nq], rhs=zeros.t[0:1, 0:260], start=False, stop=True,
                                                  skip_group_check=True),
                     reads=[zeros], writes=[accv[c]])
            for c in range(2):
                av = accv[c].t[0:nq, 0:260].rearrange("p (g d) -> p g d", g=4)
                P.op("vector", lambda e: e.reciprocal(out=rec.t[0:nq, c * 4:(c + 1) * 4], in_=av[:, :, 64]), reads=[accv[c]], writes=[rec])
                for g in range(4):
                    hh = c * 4 + g
                    P.op("vector", lambda e: e.tensor_scalar(out=attn_tok.t[0:nq, hh * 64:(hh + 1) * 64], in0=av[:, g, 0:64],
                                                             scalar1=rec.t[0:nq, hh:hh + 1], scalar2=None, op0=ALU.mult),
                         reads=[accv[c], rec], writes=[attn_tok])
            yield 1.0
            for kc in range(4):
                P.op("tensor", lambda e: e.transpose(out=p6[:, kc * 128:kc * 128 + nq], in_=attn_tok.t[0:nq, kc * 128:(kc + 1) * 128],
                                                     identity=ident.t[0:nq, 0:nq]),
                     reads=[attn_tok, ident], writes=[psb[7]], signal=(kc == 3))
            P.op("scalar", lambda e: e.activation(out=attnT.t[:, :, 0:nq], in_=p6[:, 0:512].rearrange("p (k n) -> p k n", k=4)[:, :, 0:nq], func=AF.Copy),
                 reads=[psb[7]], writes=[attnT])
            for b4 in range(2):
                for n in range(b4 * 4, b4 * 4 + 4):
                    for kc in range(4):
                        P.op("tensor", lambda e: e.matmul(psb[7].t[:, (n % 4) * 128:(n % 4) * 128 + nq], lhsT=wao.t[:, kc, n * 128:(n + 1) * 128],
                                                          rhs=attnT.t[:, kc, 0:nq], start=(kc == 0), stop=(kc == 3)),
                             reads=[wao, attnT], writes=[psb[7]], signal=(kc == 3))
                P.op("vector", lambda e: e.tensor_tensor(out=t1.t[:, b4 * 4:(b4 + 1) * 4, 0:nq],
                                                         in0=psb[7].t[:, 0:512].rearrange("p (k n) -> p k n", k=4)[:, :, 0:nq],
                                                         in1=gateu.t[:, b4 * 4:(b4 + 1) * 4, 0:nq], op=ALU.mult),
                     reads=[psb[7], gateu], writes=[t1])
                yield 0.8
            pst = psb[7]
            for cc in range(4):
                P.op("tensor", lambda e: e.matmul(pst.t[:, 0:nq], lhsT=onesM.t[:], rhs=cacc.t[:, cc, 0:nq], start=(cc == 0), stop=(cc == 3)),
                     reads=[onesM, cacc], writes=[pst], signal=(cc == 3))
            for cc in range(4):
                P.op("vector", lambda e: e.tensor_tensor(out=dcen.t[:, cc, 0:nq], in0=cacc.t[:, cc, 0:nq], in1=pst.t[:, 0:nq], op=ALU.subtract),
                     reads=[cacc, pst], writes=[dcen])
            P.op("scalar", lambda e: e.activation(out=dsq.t[:, :, 0:nq], in_=dcen.t[:, :, 0:nq], func=AF.Square), reads=[dcen], writes=[dsq])
            for cc in range(4):
                P.op("tensor", lambda e: e.matmul(pst.t[:, 128:128 + nq], lhsT=onesM.t[:], rhs=dsq.t[:, cc, 0:nq], start=(cc == 0), stop=(cc == 3)),
                     reads=[onesM, dsq], writes=[pst], signal=(cc == 3))
            P.op("scalar", lambda e: e.activation(out=rsb.t[:, 0:nq], in_=pst.t[:, 128:128 + nq], func=AF.Sqrt, bias=epsc.t[:, 0:1], scale=1.0),
                 reads=[pst, epsc], writes=[rsb])
            P.op("vector", lambda e: e.reciprocal(out=rsb.t[:, 0:nq], in_=rsb.t[:, 0:nq]), reads=[rsb], writes=[rsb])
            P.op("vector", lambda e: e.tensor_tensor(out=dcen.t[:, :, 0:nq], in0=dcen.t[:, :, 0:nq],
                                                     in1=rsb.t[:, 0:nq].unsqueeze(1).broadcast_to([128, 4, nq]), op=ALU.mult),
                 reads=[dcen, rsb], writes=[dcen])
            for cc in range(4):
                P.op("scalar", lambda e: e.activation(out=actT.t[:, cc, 0:nq], in_=dcen.t[:, cc, 0:nq], func=AF.Silu,
                                                      bias=lnbT.t[:, cc:cc + 1], scale=lngT.t[:, cc:cc + 1]),
                     reads=[dcen, lnbT, lngT], writes=[actT])
            yield 2.0
            mTu = mTu_r.next()
            for b4 in range(2):
                for n in range(b4 * 4, b4 * 4 + 4):
                    for kc in range(4):
                        P.op("tensor", lambda e: e.matmul(psb[7].t[:, (n % 4) * 128:(n % 4) * 128 + nq], lhsT=wco.t[:, kc, n * 128:(n + 1) * 128],
                                                          rhs=actT.t[:, kc, 0:nq], start=(kc == 0), stop=(kc == 3)),
                             reads=[wco, actT], writes=[psb[7]], signal=(kc == 3))
                P.op("vector", lambda e: e.tensor_tensor(out=t2.t[:, b4 * 4:(b4 + 1) * 4, 0:nq],
                                                         in0=psb[7].t[:, 0:512].rearrange("p (k n) -> p k n", k=4)[:, :, 0:nq],
                                                         in1=gateu.t[:, 8 + b4 * 4:8 + (b4 + 1) * 4, 0:nq], op=ALU.mult),
                     reads=[psb[7], gateu], writes=[t2])
                yield 0.8
            P.op("gpsimd", lambda e: e.tensor_tensor(out=mTu.t[:, :, 0:nq], in0=t1.t[:, :, 0:nq], in1=t2.t[:, :, 0:nq], op=ALU.add),
                 reads=[t1, t2], writes=[mTu])
            P.dma("sync", mT_s[:, :, tcol], mTu.t[:, :, 0:nq], reads=[mTu], writes=[D_("mT_s")])
            if seq is not None or prompt_slot == NPT - 1:
                for cc in range(4):
                    P.op("tensor", lambda e: e.transpose(out=psb[7].t[0:30, cc * 128:(cc + 1) * 128], in_=cpad.t[:, cc, nq:nq + 30], identity=identF.t[:]),
                         reads=[cpad, identF], writes=[psb[7]], signal=(cc == 3))
                P.op("vector", lambda e: e.tensor_copy(out=ncv.t[:], in_=psb[7].t[0:30, :]), reads=[psb[7]], writes=[ncv])
                if seq is not None:
                    P.dma("sync", ncs_d[seq], ncv.t[:], reads=[ncv], writes=[D_("ncs")])
                else:
                    P.dma("sync", ncp_d, ncv.t[:], reads=[ncv], writes=[D_("ncp")])
            yield 0.5

        def run_until(gen, marker):
            for v in gen:
                if v == marker:
                    return

        def drain(gen):
            for _ in gen:
                pass

        def interleave(ga, gb):
            ta = tb = 0.0
            a_alive = b_alive = True
            while a_alive or b_alive:
                if a_alive and (not b_alive or ta <= tb):
                    try:
                        ta += next(ga)
                    except StopIteration:
                        a_alive = False
                else:
                    try:
                        tb += next(gb)
                    except StopIteration:
                        b_alive = False

        def until(gen, marker):
            for v in gen:
                if v == marker:
                    return
                yield v if not isinstance(v, str) else 0.0

        def run_pipeline(units, overlap_tail=True):
            cxs = []
            for i, a in enumerate(units):
                cx_ = {"args": a[0:6], "idx": i}
                if len(a) > 6:
                    cx_.update(a[6])
                cxs.append(cx_)
            n = len(cxs)
            fgs = {0: front(cxs[0])}
            drain(until(fgs[0], "PHASE_B"))
            drain(fgs[0])
            tail = None
            for u in range(n):
                bg = back(cxs[u])
                if u + 1 < n:
                    fg = front(cxs[u + 1])
                    s_phase = until(fg, "PHASE_B")
                    if tail is not None and overlap_tail:
                        interleave(s_phase, tail)
                    else:
                        if tail is not None:
                            drain(tail)
                        drain(s_phase)
                    interleave(fg, until(bg, "TAIL"))
                else:
                    if tail is not None:
                        drain(tail)
                    drain(until(bg, "TAIL"))
                tail = bg
            drain(tail)

        run_pipeline([(i, 0, 128, 512 * (i + 1), i, None) for i in range(NPT)])
        P.barrier()
        sunits = []
        regs = [{"kiB": TBuf(f"kiS{r_}"), "kTB": TBuf(f"kTS{r_}"), "v1B": TBuf(f"v1S{r_}"), "kc0": r_ * 2048, "kt0": r_ * 16} for r_ in range(2)]
        for s_ in range(4):
            sunits.append((NPT + s_ // 2, (s_ % 2) * 64, 64, 1088, None, s_, regs[s_ % 2]))
        run_pipeline(sunits, overlap_tail=False)
        P.pop()
        if stage == 2:
            P.finish()
            return nc

        P.push()
        alloc_norm_tmp()
        g2B = P.sb("g2B", [128, D], F32)
        P.dma("sync", g2B.t[:], g2_d, writes=[g2B])
        gfB = P.sb("gfB", [128, D], F32)
        P.dma("sync", gfB.t[:], gf_d, writes=[gfB])
        xnT = P.sb("xnT", [128, 8, TBK], BF16)
        gT = P.sb("gT", [128, 22, TBK], BF16)
        hblk = P.sb("hblk", [128, BLK, D], F32)
        ST = Streamer(2816)
        sA_r = Rot([P.sb(f"sA{i}", [128, SUB], F32) for i in range(2)])
        oT_r = Rot([P.sb(f"oT{i}", [128, SUB], F32) for i in range(2)])
        yt_r = Rot([P.sb(f"yt{i}", [128, D], F32) for i in range(2)])
        for blk in range(2):
            tok0 = blk * TBK
            P.dma("sync", xnT.t[:], mT_s[:, :, tok0:tok0 + TBK], reads=[D_("mT_s")], writes=[xnT])
            for tt in range(BLK):
                P.dma("sync", hblk.t[:, tt, :], h_s[tok0 + tt * 128: tok0 + (tt + 1) * 128, :], reads=[D_("h_s")], writes=[hblk])
            proj_back(ST, xnT, 8, wo_d, hblk, 1.0)
            for tt in range(BLK):
                norm_transpose(hblk.t[:, tt, :], hblk, g2B, xnT, tt * 128, 6 + tt % 2)
            ffn_in(ST, xnT, gT, w2i_d)
            proj_back(ST, gT, 22, w2o_d, hblk, 0.5)
            for tt in range(BLK):
                rstd = rstd_of(hblk.t[:, tt, :], hblk)
                yt = yt_r.next()
                P.op("vector", lambda e: e.scalar_tensor_tensor(out=yt.t[:], in0=hblk.t[:, tt, :], scalar=rstd.t[:, 0:1], in1=gfB.t[:],
                                                                op0=ALU.mult, op1=ALU.mult), reads=[hblk, rstd, gfB], writes=[yt])
                P.dma("sync", y_d[tok0 + tt * 128: tok0 + (tt + 1) * 128, :], yt.t[:], reads=[yt], writes=[D_("y")])
        P.pop()
        P.finish()
    return nc


_NC_CACHE = {}


def _rope_table(pos):
    inv = (10000.0 ** (-np.arange(0, 64, 2, dtype=np.float32) / np.float32(64))).astype(np.float32)
    ang = pos.astype(np.float32)[:, None] * inv[None, :]
    return np.concatenate([np.cos(ang), np.sin(ang)], axis=-1).astype(np.float32)


def kernel(x_prompt, x_sample, cache_k, cache_v, cache_idx_k, state_conv,
           ffn1_norm, ffn1_w_in, ffn1_w_out, mix_norm, w_in, b_gate,
           conv_w, conv_b, conv_ln_g, conv_ln_b, conv_w_out, attn_w_out, w_out,
           ffn2_norm, ffn2_w_in, ffn2_w_out, final_norm):
    f = lambda a: np.ascontiguousarray(np.asarray(a, dtype=np.float32))
    x_prompt, x_sample = f(x_prompt), f(x_sample)
    cache_k, cache_v, cache_idx_k, state_conv = f(cache_k), f(cache_v), f(cache_idx_k), f(state_conv)
    if "nc" not in _NC_CACHE:
        _NC_CACHE["nc"] = build_program()
    nc = _NC_CACHE["nc"]

    def bc(v):
        return np.ascontiguousarray(np.broadcast_to(f(v).reshape(1, D), (128, D)))

    def colT(v, n):
        return np.ascontiguousarray(f(v).reshape(n, 128).T)

    shared = {
        "g1": bc(ffn1_norm[0]), "gm": bc(mix_norm[0]), "g2": bc(ffn2_norm[0]), "gf": bc(final_norm),
        "w1i": f(ffn1_w_in[0]), "w1o": f(ffn1_w_out[0]), "win": f(w_in[0]),
        "bgT": colT(b_gate[0], 16),
        "cwT": np.ascontiguousarray(f(conv_w[0]).reshape(31, 4, 128).transpose(2, 1, 0)),
        "cbT": colT(conv_b[0], 4), "lngT": colT(conv_ln_g[0], 4), "lnbT": colT(conv_ln_b[0], 4),
        "wco": f(conv_w_out[0]), "wao": f(attn_w_out[0]), "wo": f(w_out[0]),
        "w2i": f(ffn2_w_in[0]), "w2o": f(ffn2_w_out[0]),
    }
    p = np.arange(128)
    in_maps = []
    for c in range(8):
        b, j = c // 4, c % 4
        xs = [x_prompt[b, (4 * i + j) * 128:(4 * i + j + 1) * 128] for i in range(NPT)]
        xs += [x_sample[4 * c + s] for s in range(4)]
        cs = np.zeros((128, NT, 64), np.float32)
        for i in range(NPT):
            cs[:, i, :] = _rope_table((4 * i + j) * 128 + p)
        for t in range(2):
            cs[:, NPT + t, :] = _rope_table(1024 + (p % 64))
        lim = (128 * j + 64 + 64 * (p >= 64)).astype(np.float32).reshape(128, 1)
        sel = np.zeros((128, 5), np.float32)
        if j >= 1:
            sel[:, j - 1] = 1.0
        else:
            sel[:, 4] = 1.0
        m = dict(shared)
        m.update({
            "x": np.ascontiguousarray(np.concatenate(xs, axis=0)),
            "ck": np.ascontiguousarray(cache_k[0, 4 * c:4 * c + 4].reshape(4, 1024, 128)),
            "cv": np.ascontiguousarray(cache_v[0, 4 * c:4 * c + 4].reshape(4, 1024, 128)),
            "cik": np.ascontiguousarray(cache_idx_k[0, 4 * c:4 * c + 4]),
            "sconv": np.ascontiguousarray(state_conv[0, 4 * c:4 * c + 4]),
            "cs": cs, "limrel": lim, "sel": sel,
        })
        in_maps.append(m)

    res = run_bass_kernel_spmd(nc, in_maps, core_ids=list(range(8)))
    R = res.results
    y_p = np.zeros((2, 8192, D), np.float32)
    y_s = np.zeros((32, 64, D), np.float32)
    nk_p = np.zeros((1, 2, 8192, 2, 64), np.float32)
    nv_p = np.zeros((1, 2, 8192, 2, 64), np.float32)
    ni_p = np.zeros((1, 2, 8192, 64), np.float32)
    nc_p = np.zeros((1, 2, 30, 512), np.float32)
    nk_s = np.zeros((1, 32, 64, 2, 64), np.float32)
    nv_s = np.zeros((1, 32, 64, 2, 64), np.float32)
    ni_s = np.zeros((1, 32, 64, 64), np.float32)
    nc_s = np.zeros((1, 32, 30, 512), np.float32)
    for c in range(8):
        b, j = c // 4, c % 4
        r = R[c]
        for i in range(NPT):
            g = slice((4 * i + j) * 128, (4 * i + j + 1) * 128)
            l = slice(i * 128, (i + 1) * 128)
            y_p[b, g] = r["y"][l]
            nk_p[0, b, g] = r["nk"][l].reshape(128, 2, 64)
            nv_p[0, b, g] = r["nv"][l].reshape(128, 2, 64)
            ni_p[0, b, g] = r["nki"][l]
        for s in range(4):
            l = slice(NPT * 128 + s * 64, NPT * 128 + (s + 1) * 64)
            y_s[4 * c + s] = r["y"][l]
            nk_s[0, 4 * c + s] = r["nk"][l].reshape(64, 2, 64)
            nv_s[0, 4 * c + s] = r["nv"][l].reshape(64, 2, 64)
            ni_s[0, 4 * c + s] = r["nki"][l]
            nc_s[0, 4 * c + s] = r["ncs"][s]
        if j == 3:
            nc_p[0, b] = r["ncp"]
    return (y_p, y_s, nk_p, nv_p, ni_p, nc_p, nk_s, nv_s, ni_s, nc_s)
```

```python
import os
import numpy as np
from contextlib import ExitStack
import concourse.bass as bass
import concourse.mybir as mybir
from concourse.bass_utils import run_bass_kernel_spmd

F32 = mybir.dt.float32
BF16 = mybir.dt.bfloat16
I32 = mybir.dt.int32
AF = mybir.ActivationFunctionType
ALU = mybir.AluOpType
AX = mybir.AxisListType

D = 1024
DFF = 2816
NT = 18
NTOK = NT * 128
NPT = 16
BLK = 9
NSUB = 3
SUB = 384
TBK = BLK * 128
EPS = 1e-6
BIG = 1.0e30
NITER = 14
TOPK = 256.0
INCOLS = 4424


class TBuf:
    def __init__(self, name, t=None):
        self.name = name
        self.t = t
        self.w = {}
        self.r = {}
        self.dsem = None
        self.dcount = 0


class Prog:
    ENGS = ("tensor", "vector", "scalar", "gpsimd", "sync")

    def __init__(self, nc):
        self.nc = nc
        self.stacks = [ExitStack()]
        self.sems = {}
        self.count = {}
        self.waited = {}
        self.drams = {}
        self.dma_bufs = []
        self.uid = 0

    def __enter__(self):
        self.stacks[0].__enter__()
        for e in self.ENGS:
            self.sems[e] = self.stacks[0].enter_context(self.nc.semaphore("sem_" + e))
            self.count[e] = 0
            self.waited[e] = {}
        return self

    def __exit__(self, *a):
        while len(self.stacks) > 1:
            self.stacks.pop().close()
        return self.stacks[0].__exit__(*a)

    def push(self):
        st = ExitStack()
        st.__enter__()
        self.stacks.append(st)

    def pop(self):
        self.barrier()
        self.stacks.pop().close()

    def sb(self, name, shape, dtype):
        self.uid += 1
        return TBuf(name, self.stacks[-1].enter_context(self.nc.sbuf_tensor(f"{name}_{self.uid}", list(shape), dtype)))

    def ps(self, name, shape, dtype):
        return TBuf(name, self.stacks[0].enter_context(self.nc.psum_tensor(name, list(shape), dtype)))

    def dram(self, name):
        if name not in self.drams:
            self.drams[name] = TBuf("d_" + name)
        return self.drams[name]

    def _eng(self, name):
        return getattr(self.nc, name)

    def _wait(self, engname, key, val):
        if key == engname and engname == "tensor":
            return
        w = self.waited[engname]
        if w.get(key, 0) >= val:
            return
        if key in self.ENGS:
            assert self.count[key] >= val, f"wait on un-signalled instruction {key} {val} {self.count[key]}"
        w[key] = val
        self._eng(engname).wait_ge(self.sems[key], val)

    def _deps(self, engname, reads, writes):
        for b in reads:
            for k, v in b.w.items():
                self._wait(engname, k, v)
        for b in writes:
            for k, v in b.w.items():
                self._wait(engname, k, v)
            for k, v in b.r.items():
                self._wait(engname, k, v)

    @staticmethod
    def _rec(d, key, val):
        if d.get(key, 0) < val:
            d[key] = val

    def op(self, engname, fn, reads=(), writes=(), signal=True):
        self._deps(engname, reads, writes)
        inst = fn(self._eng(engname))
        if signal:
            self.count[engname] += 1
            inst.then_inc(self.sems[engname], 1)
            val = self.count[engname]
        else:
            val = self.count[engname] + 1
        for b in reads:
            self._rec(b.r, engname, val)
        for b in writes:
            self._rec(b.w, engname, val)
        return inst

    def _dsem(self, b0):
        if b0.dsem is None:
            key = "dma_" + b0.name + str(len(self.dma_bufs))
            b0.dsem = key
            self.sems[key] = self.stacks[0].enter_context(self.nc.semaphore("s" + str(len(self.dma_bufs))))
            self.dma_bufs.append(b0)
        return b0.dsem

    def dma(self, queue, out_ap, in_ap, reads=(), writes=(), **kw):
        self._deps(queue, reads, writes)
        b0 = writes[0]
        key = self._dsem(b0)
        inst = self._eng(queue).dma_start(out=out_ap, in_=in_ap, **kw)
        b0.dcount += 16
        inst.then_inc(self.sems[key], 16)
        for b in reads:
            self._rec(b.r, key, b0.dcount)
        for b in writes:
            self._rec(b.w, key, b0.dcount)
        return inst

    def collective(self, fn, reads, writes):
        self._deps("gpsimd", reads, writes)
        b0 = writes[0]
        key = self._dsem(b0)
        inst = fn(self.nc.gpsimd)
        b0.dcount += 1
        inst.then_inc(self.sems[key])
        for b in reads:
            self._rec(b.r, key, b0.dcount)
        for b in writes:
            self._rec(b.w, key, b0.dcount)

    def sync_to(self, engname, bufs):
        self._deps(engname, (), bufs)

    def barrier(self):
        for e in self.ENGS:
            for e2 in self.ENGS:
                if e2 != e and self.count[e2] > 0:
                    self._wait(e, e2, self.count[e2])
            for b in self.dma_bufs:
                if b.dcount:
                    self._wait(e, b.dsem, b.dcount)

    def finish(self):
        for b in self.dma_bufs:
            if b.dcount:
                self._wait("sync", b.dsem, b.dcount)
        for e in self.ENGS:
            if e != "sync" and self.count[e] > 0:
                self._wait("sync", e, self.count[e])


class Rot:
    def __init__(self, bufs):
        self.bufs = bufs
        self.i = 0

    def next(self):
        b = self.bufs[self.i % len(self.bufs)]
        self.i += 1
        return b


def build_program(stage=3):
    nc = bass.Bass("TRN2", target_bir_lowering=False)

    def din(name, shape, dt=F32):
        return nc.dram_tensor(name, list(shape), dt, kind="ExternalInput").ap()

    def dout(name, shape, dt=F32):
        return nc.dram_tensor(name, list(shape), dt, kind="ExternalOutput").ap()

    def dscr(name, shape, dt):
        return nc.dram_tensor(name, list(shape), dt).ap()

    x_d = din("x", [NTOK, D])
    ck_d = din("ck", [4, 1024, 128])
    cv_d = din("cv", [4, 1024, 128])
    cik_d = din("cik", [4, 1024, 64])
    sconv_d = din("sconv", [4, 30, 512])
    g1_d = din("g1", [128, D])
    gm_d = din("gm", [128, D])
    g2_d = din("g2", [128, D])
    gf_d = din("gf", [128, D])
    w1i_d = din("w1i", [D, 2 * DFF])
    w1o_d = din("w1o", [DFF, D])
    win_d = din("win", [D, INCOLS])
    bg_d = din("bgT", [128, 16])
    cw_d = din("cwT", [128, 4, 31])
    cb_d = din("cbT", [128, 4])
    lg_d = din("lngT", [128, 4])
    lb_d = din("lnbT", [128, 4])
    wco_d = din("wco", [512, D])
    wao_d = din("wao", [512, D])
    wo_d = din("wo", [D, D])
    w2i_d = din("w2i", [D, 2 * DFF])
    w2o_d = din("w2o", [DFF, D])
    cs_d = din("cs", [128, NT, 64])
    lim_d = din("limrel", [128, 1])
    sel_d = din("sel", [128, 5])

    y_d = dout("y", [NTOK, D])
    nk_d = dout("nk", [NTOK, 128])
    nv_d = dout("nv", [NTOK, 128])
    nki_d = dout("nki", [NTOK, 64])
    ncp_d = dout("ncp", [30, 512])
    ncs_d = dout("ncs", [4, 30, 512])

    h_s = dscr("h_s", [NTOK, D], F32)
    gate_s = dscr("gate_s", [128, 16, NTOK], BF16)
    qT_s = dscr("qT_s", [128, 4, NTOK], BF16)
    qiT_s = dscr("qiT_s", [64, 8, NTOK], BF16)
    wi_s = dscr("wi_s", [NTOK, 8], F32)
    cT_s = dscr("cT_s", [128, 4, NTOK], F32)
    mT_s = dscr("mT_s", [128, 8, NTOK], BF16)
    kvmy = dscr("kvmy", [192, NPT * 128], BF16)
    kvmyv = dscr("kvmyv", [128, NPT * 128], BF16)
    kvsm = dscr("kvsm", [320, 256], BF16)
    ctmy = dscr("ctmy", [128, 4 * NPT * 32], F32)
    kvg = dscr("kvg", [4 * 192, NPT * 128], BF16)
    kvgv = dscr("kvgv", [4 * 128, NPT * 128], BF16)
    ctg = dscr("ctg", [4 * 128, 4 * NPT * 32], F32)

    P = Prog(nc)
    with P:
        D_ = P.dram
        psb = [P.ps(f"ps{i}", [128, 512], F32) for i in range(8)]

        def bfv(i):
            return psb[i].t[:].bitcast(BF16)

        ident = P.sb("ident", [128, 128], BF16)
        identF = P.sb("identF", [128, 128], F32)
        it = P.sb("iota_tmp", [128, 128], I32)
        P.op("gpsimd", lambda e: e.iota(it.t[:], pattern=[[1, 128]], base=0, channel_multiplier=-1), writes=[it])
        P.op("vector", lambda e: e.tensor_scalar(out=ident.t[:], in0=it.t[:], scalar1=0.0, scalar2=None, op0=ALU.is_equal),
             reads=[it], writes=[ident])
        P.op("vector", lambda e: e.tensor_scalar(out=identF.t[:], in0=it.t[:], scalar1=0.0, scalar2=None, op0=ALU.is_equal),
             reads=[it], writes=[identF])
        cs = P.sb("cs", [128, NT, 64], F32)
        P.dma("sync", cs.t[:], cs_d, writes=[cs])
        bgT = P.sb("bgT", [128, 16], F32)
        P.dma("sync", bgT.t[:], bg_d, writes=[bgT])
        cwT = P.sb("cwT", [128, 4, 31], F32)
        P.dma("sync", cwT.t[:], cw_d, writes=[cwT])
        cbT = P.sb("cbT", [128, 4], F32)
        P.dma("sync", cbT.t[:], cb_d, writes=[cbT])
        lngT = P.sb("lngT", [128, 4], F32)
        P.dma("sync", lngT.t[:], lg_d, writes=[lngT])
        lnbT = P.sb("lnbT", [128, 4], F32)
        P.dma("sync", lnbT.t[:], lb_d, writes=[lnbT])
        limrel = P.sb("limrel", [128, 1], F32)
        P.dma("sync", limrel.t[:], lim_d, writes=[limrel])
        sel = P.sb("sel", [128, 5], F32)
        P.dma("sync", sel.t[:], sel_d, writes=[sel])
        epsc = P.sb("epsc", [128, 1], F32)
        P.op("vector", lambda e: e.memset(epsc.t[:], EPS), writes=[epsc])

        ssq_r = Rot([P.sb(f"ssq{i}", [128, 1], F32) for i in range(2)])
        rstd_r = Rot([P.sb(f"rstd{i}", [128, 1], F32) for i in range(2)])
        sh = {}

        def alloc_norm_tmp():
            sh["xs_r"] = Rot([P.sb(f"xs{i}", [128, D], BF16) for i in range(2)])
            sh["junk"] = P.sb("junk", [128, D], BF16)

        def rstd_of(src_ap, srcbuf):
            ssq = ssq_r.next()
            rstd = rstd_r.next()
            junk = sh["junk"]
            P.op("scalar", lambda e: e.activation(out=junk.t[:], in_=src_ap, func=AF.Square, accum_out=ssq.t[:, 0:1]),
                 reads=[srcbuf], writes=[junk, ssq])
            P.op("scalar", lambda e: e.activation(out=rstd.t[:], in_=ssq.t[:], func=AF.Sqrt, bias=epsc.t[:, 0:1], scale=1.0 / D),
                 reads=[ssq, epsc], writes=[rstd])
            P.op("vector", lambda e: e.reciprocal(out=rstd.t[:], in_=rstd.t[:]), reads=[rstd], writes=[rstd])
            return rstd

        def norm_transpose(src_ap, srcbuf, gB, dstT, col0, pbank):
            rstd = rstd_of(src_ap, srcbuf)
            xs = sh["xs_r"].next()
            P.op("vector", lambda e: e.scalar_tensor_tensor(out=xs.t[:], in0=src_ap, scalar=rstd.t[:, 0:1], in1=gB.t[:],
                                                            op0=ALU.mult, op1=ALU.mult),
                 reads=[srcbuf, rstd, gB], writes=[xs])
            pv = bfv(pbank)
            for kc in range(8):
                P.op("tensor", lambda e: e.transpose(out=pv[:, kc * 128:(kc + 1) * 128], in_=xs.t[:, kc * 128:(kc + 1) * 128],
                                                     identity=ident.t[:]),
                     reads=[xs, ident], writes=[psb[pbank]], signal=(kc == 7))
            P.op("scalar", lambda e: e.activation(out=dstT.t[:, :, col0:col0 + 128],
                                                  in_=pv.rearrange("p (k n) -> p k n", k=8), func=AF.Copy),
                 reads=[psb[pbank]], writes=[dstT])

        class Streamer:
            def __init__(self, maxel):
                self.stage = [P.sb(f"stg{i}", [128, maxel], F32) for i in range(2)]
                self.wb = [P.sb(f"wbf{i}", [128, maxel], BF16) for i in range(2)]
                self.n = 0
                self.nd = 0

            def load(self, pieces, KC, ncols, dst=None):
                s = self.n % 2
                self.n += 1
                stg = self.stage[s]
                sv = stg.t[:, 0:KC * ncols].rearrange("p (k n) -> p k n", k=KC)
                for (off, ap, w) in pieces:
                    P.dma("sync", sv[:, :, off:off + w], ap, writes=[stg])
                if dst is None:
                    wbuf = self.wb[self.nd % 2]
                    self.nd += 1
                    ov = wbuf.t[:, 0:KC * ncols].rearrange("p (k n) -> p k n", k=KC)
                else:
                    wbuf, ov = dst
                if self.n % 2 == 0:
                    P.op("scalar", lambda e: e.activation(out=ov, in_=sv, func=AF.Copy), reads=[stg], writes=[wbuf])
                else:
                    P.op("vector", lambda e: e.tensor_copy(out=ov, in_=sv), reads=[stg], writes=[wbuf])
                return wbuf, ov

        sA_r = None
        oT_r = None

        def ffn_in(ST, xnT, gT, w_in_ap):
            Wv = w_in_ap.rearrange("(kc p) n -> p kc n", p=128)

            def load(j):
                return ST.load([(0, Wv[:, :, j * 128:(j + 1) * 128], 128),
                                (128, Wv[:, :, DFF + j * 128:DFF + (j + 1) * 128], 128)], 8, 256)
            nxt = load(0)
            cnt = 0
            for j in range(22):
                wbuf, wv = nxt
                if j + 1 < 22:
                    nxt = load(j + 1)
                for sbk in range(NSUB):
                    pa = psb[cnt % 2]
                    pb = psb[2 + cnt % 2]
                    cnt += 1
                    cols = slice(sbk * SUB, (sbk + 1) * SUB)
                    for kc in range(8):
                        P.op("tensor", lambda e: e.matmul(pa.t[:, 0:SUB], lhsT=wv[:, kc, 0:128], rhs=xnT.t[:, kc, cols],
                                                          start=(kc == 0), stop=(kc == 7)),
                             reads=[wbuf, xnT], writes=[pa], signal=(kc == 7))
                    for kc in range(8):
                        P.op("tensor", lambda e: e.matmul(pb.t[:, 0:SUB], lhsT=wv[:, kc, 128:256], rhs=xnT.t[:, kc, cols],
                                                          start=(kc == 0), stop=(kc == 7)),
                             reads=[wbuf, xnT], writes=[pb], signal=(kc == 7))
                    sA = sA_r.next()
                    P.op("scalar", lambda e: e.activation(out=sA.t[:, 0:SUB], in_=pa.t[:, 0:SUB], func=AF.Silu),
                         reads=[pa], writes=[sA])
                    P.op("vector", lambda e: e.tensor_tensor(out=gT.t[:, j, cols], in0=pb.t[:, 0:SUB], in1=sA.t[:, 0:SUB], op=ALU.mult),
                         reads=[pb, sA], writes=[gT])

        def proj_back(ST, srcT, KC, w_ap, hblk, scale):
            Wv = w_ap.rearrange("(kc p) n -> p kc n", p=128)

            def load(n):
                return ST.load([(0, Wv[:, :, n * 128:(n + 1) * 128], 128)], KC, 128)
            nxt = load(0)
            cnt = 0
            for n in range(8):
                wbuf, wv = nxt
                if n + 1 < 8:
                    nxt = load(n + 1)
                for sbk in range(NSUB):
                    po = psb[4 + cnt % 2]
                    pt = psb[6 + cnt % 2]
                    cnt += 1
                    cols = slice(sbk * SUB, (sbk + 1) * SUB)
                    for kc in range(KC):
                        P.op("tensor", lambda e: e.matmul(po.t[:, 0:SUB], lhsT=wv[:, kc, :], rhs=srcT.t[:, kc, cols],
                                                          start=(kc == 0), stop=(kc == KC - 1)),
                             reads=[wbuf, srcT], writes=[po], signal=(kc == KC - 1))
                    oT = oT_r.next()
                    P.op("scalar", lambda e: e.activation(out=oT.t[:, 0:SUB], in_=po.t[:, 0:SUB], func=AF.Copy),
                         reads=[po], writes=[oT])
                    for t3 in range(3):
                        P.op("tensor", lambda e: e.transpose(out=pt.t[:, t3 * 128:(t3 + 1) * 128], in_=oT.t[:, t3 * 128:(t3 + 1) * 128],
                                                             identity=identF.t[:]),
                             reads=[oT, identF], writes=[pt], signal=(t3 == 2))
                    hv = hblk.t[:, sbk * 3:sbk * 3 + 3, n * 128:(n + 1) * 128]
                    P.op("vector", lambda e: e.scalar_tensor_tensor(out=hv, in0=pt.t[:, 0:SUB].rearrange("p (t n) -> p t n", t=3),
                                                                    scalar=float(scale), in1=hv, op0=ALU.mult, op1=ALU.add),
                         reads=[pt, hblk], writes=[hblk])

        P.push()
        alloc_norm_tmp()
        g1B = P.sb("g1B", [128, D], F32)
        P.dma("sync", g1B.t[:], g1_d, writes=[g1B])
        gmB = P.sb("gmB", [128, D], F32)
        P.dma("sync", gmB.t[:], gm_d, writes=[gmB])
        xnT = P.sb("xnT", [128, 8, TBK], BF16)
        gT = P.sb("gT", [128, 22, TBK], BF16)
        hblk = P.sb("hblk", [128, BLK, D], F32)
        ST = Streamer(2816)
        sA_r = Rot([P.sb(f"sA{i}", [128, SUB], F32) for i in range(2)])
        oT_r = Rot([P.sb(f"oT{i}", [128, SUB], F32) for i in range(2)])
        wtok = gT.t[:].rearrange("p a b -> p (a b)")[:, 0:8 * 1352].rearrange("p (k n) -> p k n", k=8)
        ropeT = Rot([P.sb(f"ropeT{i}", [128, 8, 32], F32) for i in range(8)])
        qb2_r = Rot([P.sb(f"qb2{i}", [128, 512], BF16) for i in range(2)])
        qib_r = Rot([P.sb(f"qib{i}", [128, 512], BF16) for i in range(2)])
        kf_r = Rot([P.sb(f"kf{i}", [128, 128], F32) for i in range(2)])
        vf_r = Rot([P.sb(f"vf{i}", [128, 128], F32) for i in range(2)])
        kif_r = Rot([P.sb(f"kif{i}", [128, 64], F32) for i in range(2)])
        kb_r = Rot([P.sb(f"kb{i}", [128, 128], BF16) for i in range(2)])
        vb_r = Rot([P.sb(f"vb{i}", [128, 128], BF16) for i in range(2)])
        kib_r = Rot([P.sb(f"kib{i}", [128, 64], BF16) for i in range(2)])
        wis_r = Rot([P.sb(f"wis{i}", [128, 8], F32) for i in range(2)])
        qTt_r = Rot([P.sb(f"qTt{i}", [128, 4, 128], BF16) for i in range(2)])
        kTt_r = Rot([P.sb(f"kTt{i}", [128, 128], BF16) for i in range(2)])
        qiTt_r = Rot([P.sb(f"qiTt{i}", [64, 8, 128], BF16) for i in range(2)])
        kiTt_r = Rot([P.sb(f"kiTt{i}", [64, 128], BF16) for i in range(2)])
        cbuf_r = Rot([P.sb(f"cbuf{i}", [128, SUB], F32) for i in range(2)])
        gbuf_r = Rot([P.sb(f"gbuf{i}", [128, SUB], BF16) for i in range(2)])

        def rope(src4, srcbuf, tg, o1, o2, obuf, shape):
            x1 = src4[:, :, :, 0:32]
            x2 = src4[:, :, :, 32:64]
            a, b = shape
            cosB = cs.t[:, tg, 0:32].unsqueeze(1).unsqueeze(1).broadcast_to([128, a, b, 32])
            sinB = cs.t[:, tg, 32:64].unsqueeze(1).unsqueeze(1).broadcast_to([128, a, b, 32])
            ts = [ropeT.next() for _ in range(4)]
            tv = [t.t[:, 0:a * b, :].rearrange("p (a b) d -> p a b d", a=a) for t in ts]
            for (tb, tvv, xx, cc) in ((ts[0], tv[0], x1, cosB), (ts[1], tv[1], x2, sinB), (ts[2], tv[2], x2, cosB), (ts[3], tv[3], x1, sinB)):
                P.op("vector", lambda e: e.tensor_tensor(out=tvv, in0=xx, in1=cc, op=ALU.mult), reads=[srcbuf, cs], writes=[tb])
            P.op("gpsimd", lambda e: e.tensor_tensor(out=o1, in0=tv[0], in1=tv[1], op=ALU.subtract), reads=[ts[0], ts[1]], writes=[obuf])
            P.op("gpsimd", lambda e: e.tensor_tensor(out=o2, in0=tv[2], in1=tv[3], op=ALU.add), reads=[ts[2], ts[3]], writes=[obuf])

        WinV = win_d.rearrange("(kc p) n -> p kc n", p=128)
        groups = [[0, 1, 2, 3], [4, 5, 6, 7]]
        for blk in range(2):
            tok0 = blk * TBK
            for tt in range(BLK):
                P.dma("sync", hblk.t[:, tt, :], x_d[tok0 + tt * 128: tok0 + (tt + 1) * 128, :], writes=[hblk])
            if stage == 0.05:
                P.barrier(); P.finish(); return nc
            for tt in range(BLK):
                norm_transpose(hblk.t[:, tt, :], hblk, g1B, xnT, tt * 128, 6 + tt % 2)
            if stage == 0.1:
                P.barrier(); P.finish(); return nc
            ffn_in(ST, xnT, gT, w1i_d)
            if stage == 0.2:
                P.barrier(); P.finish(); return nc
            proj_back(ST, gT, 22, w1o_d, hblk, 0.5)
            if stage == 0.3:
                P.barrier(); P.finish(); return nc
            for tt in range(BLK):
                norm_transpose(hblk.t[:, tt, :], hblk, gmB, xnT, tt * 128, 6 + tt % 2)
            for tt in range(BLK):
                P.dma("sync", h_s[tok0 + tt * 128: tok0 + (tt + 1) * 128, :], hblk.t[:, tt, :], reads=[hblk], writes=[D_("h_s")])
            wtok_pieces = []
            c0 = 0
            while c0 < 1352:
                w = min(256, 1352 - c0)
                wtok_pieces.append((c0, w))
                c0 += w

            def load_wtok(n):
                for _ in range(n):
                    if wtok_pieces:
                        c0_, w_ = wtok_pieces.pop(0)
                        ST.load([(0, WinV[:, :, c0_:c0_ + w_], w_)], 8, w_, dst=(gT, wtok[:, :, c0_:c0_ + w_]))
            if stage == 0.7:
                P.barrier(); P.finish(); return nc
            def loadc(cc):
                return ST.load([(0, WinV[:, :, 1352 + cc * 128:1352 + (cc + 1) * 128], 128),
                                (128, WinV[:, :, 1864 + cc * 128:1864 + (cc + 1) * 128], 128)], 8, 256)
            nxt = loadc(0)
            cnt = 0
            for cc in range(4):
                wbuf, wv = nxt
                if cc + 1 < 4:
                    nxt = loadc(cc + 1)
                load_wtok(2)
                for sbk in range(NSUB):
                    pa = psb[cnt % 2]
                    pb = psb[2 + cnt % 2]
                    cnt += 1
                    cols = slice(sbk * SUB, (sbk + 1) * SUB)
                    for kc in range(8):
                        P.op("tensor", lambda e: e.matmul(pa.t[:, 0:SUB], lhsT=wv[:, kc, 0:128], rhs=xnT.t[:, kc, cols],
                                                          start=(kc == 0), stop=(kc == 7)),
                             reads=[wbuf, xnT], writes=[pa], signal=(kc == 7))
                    for kc in range(8):
                        P.op("tensor", lambda e: e.matmul(pb.t[:, 0:SUB], lhsT=wv[:, kc, 128:256], rhs=xnT.t[:, kc, cols],
                                                          start=(kc == 0), stop=(kc == 7)),
                             reads=[wbuf, xnT], writes=[pb], signal=(kc == 7))
                    sA = sA_r.next()
                    P.op("scalar", lambda e: e.activation(out=sA.t[:, 0:SUB], in_=pb.t[:, 0:SUB], func=AF.Sigmoid), reads=[pb], writes=[sA])
                    cbuf = cbuf_r.next()
                    P.op("vector", lambda e: e.tensor_tensor(out=cbuf.t[:], in0=pa.t[:, 0:SUB], in1=sA.t[:, 0:SUB], op=ALU.mult),
                         reads=[pa, sA], writes=[cbuf])
                    P.dma("gpsimd", cT_s[:, cc, tok0 + sbk * SUB: tok0 + (sbk + 1) * SUB], cbuf.t[:], reads=[cbuf], writes=[D_("cT_s")])
                    for t3 in range(3):
                        tg = blk * BLK + sbk * 3 + t3
                        if tg < NPT:
                            o = (cc * NPT + tg) * 32
                            P.dma("gpsimd", ctmy[:, o:o + 32], cbuf.t[:, t3 * 128 + 96: t3 * 128 + 128], reads=[cbuf], writes=[D_("ctmy")])
            load_wtok(6)
            def zmm(tt):
                s3 = 3 * (tt % 2)
                for (bk, o0, a0, a1) in ((s3, 0, 0, 512), (s3 + 1, 0, 512, 768), (s3 + 1, 256, 1280, 1352), (s3 + 2, 0, 768, 1280)):
                    for kc in range(8):
                        P.op("tensor", lambda e: e.matmul(psb[bk].t[:, o0:o0 + a1 - a0], lhsT=xnT.t[:, kc, tt * 128:(tt + 1) * 128],
                                                          rhs=wtok[:, kc, a0:a1], start=(kc == 0), stop=(kc == 7)),
                             reads=[xnT, gT], writes=[psb[bk]], signal=(kc == 7))

            def post(tt):
                tg = blk * BLK + tt
                trow = slice(tok0 + tt * 128, tok0 + (tt + 1) * 128)
                tcol = trow
                s3 = 3 * (tt % 2)
                bq, bkv, bqi = psb[s3], psb[s3 + 1], psb[s3 + 2]
                qb2 = qb2_r.next()
                zq = bq.t[:, 0:512].rearrange("p (c g d) -> p c g d", c=2, g=4)
                oq = qb2.t[:].rearrange("p (g c d) -> p c g d", g=4, c=2)
                rope(zq, bq, tg, oq[:, :, :, 0:32], oq[:, :, :, 32:64], qb2, (2, 4))
                kf = kf_r.next()
                zk = bkv.t[:, 0:128].rearrange("p (a c d) -> p a c d", a=1, c=2)
                ok = kf.t[:].rearrange("p (a c d) -> p a c d", a=1, c=2)
                rope(zk, bkv, tg, ok[:, :, :, 0:32], ok[:, :, :, 32:64], kf, (1, 2))
                vf = vf_r.next()
                vb = vb_r.next()
                P.op("vector", lambda e: e.tensor_copy(out=vf.t[:], in_=bkv.t[:, 128:256]), reads=[bkv], writes=[vf])
                P.op("vector", lambda e: e.tensor_copy(out=vb.t[:], in_=bkv.t[:, 128:256]), reads=[bkv], writes=[vb])
                qib = qib_r.next()
                zqi = bqi.t[:, 0:512].rearrange("p (a h d) -> p a h d", a=1, h=8)
                oqi = qib.t[:].rearrange("p (a h d) -> p a h d", a=1, h=8)
                rope(zqi, bqi, tg, oqi[:, :, :, 0:32], oqi[:, :, :, 32:64], qib, (1, 8))
                kif = kif_r.next()
                zki = bkv.t[:, 256:320].rearrange("p (a h d) -> p a h d", a=1, h=1)
                oki = kif.t[:].rearrange("p (a h d) -> p a h d", a=1, h=1)
                rope(zki, bkv, tg, oki[:, :, :, 0:32], oki[:, :, :, 32:64], kif, (1, 1))
                wis = wis_r.next()
                P.op("vector", lambda e: e.tensor_scalar(out=wis.t[:], in0=bkv.t[:, 320:328], scalar1=float(8 ** -0.5), scalar2=None, op0=ALU.mult),
                     reads=[bkv], writes=[wis])
                kb = kb_r.next()
                kib = kib_r.next()
                P.op("scalar", lambda e: e.activation(out=kb.t[:], in_=kf.t[:], func=AF.Copy), reads=[kf], writes=[kb])
                P.op("scalar", lambda e: e.activation(out=kib.t[:], in_=kif.t[:], func=AF.Copy), reads=[kif], writes=[kib])
                p6 = bfv(6)
                for g in range(4):
                    P.op("tensor", lambda e: e.transpose(out=p6[:, g * 128:(g + 1) * 128], in_=qb2.t[:, g * 128:(g + 1) * 128], identity=ident.t[:]),
                         reads=[qb2, ident], writes=[psb[6]], signal=False)
                P.op("tensor", lambda e: e.transpose(out=p6[:, 512:640], in_=kb.t[:], identity=ident.t[:]),
                     reads=[kb, ident], writes=[psb[6]], signal=False)
                P.op("tensor", lambda e: e.transpose(out=p6[0:64, 640:768], in_=kib.t[:], identity=ident.t[:]),
                     reads=[kib, ident], writes=[psb[6]])
                p7 = bfv(7)
                for h in range(8):
                    P.op("tensor", lambda e: e.transpose(out=p7[0:64, h * 128:(h + 1) * 128], in_=qib.t[:, h * 64:(h + 1) * 64], identity=ident.t[:]),
                         reads=[qib, ident], writes=[psb[7]], signal=(h == 7))
                qTt = qTt_r.next()
                kTt = kTt_r.next()
                qiTt = qiTt_r.next()
                kiTt = kiTt_r.next()
                P.op("scalar", lambda e: e.activation(out=qTt.t[:], in_=p6[:, 0:512].rearrange("p (g n) -> p g n", g=4), func=AF.Copy),
                     reads=[psb[6]], writes=[qTt])
                P.op("scalar", lambda e: e.activation(out=kTt.t[:], in_=p6[:, 512:640], func=AF.Copy), reads=[psb[6]], writes=[kTt])
                P.op("scalar", lambda e: e.activation(out=kiTt.t[:], in_=p6[0:64, 640:768], func=AF.Copy), reads=[psb[6]], writes=[kiTt])
                P.op("vector", lambda e: e.tensor_copy(out=qiTt.t[:], in_=p7[0:64, :].rearrange("p (h n) -> p h n", h=8)),
                     reads=[psb[7]], writes=[qiTt])
                P.dma("gpsimd", nk_d[trow, :], kf.t[:], reads=[kf], writes=[D_("nk")])
                P.dma("gpsimd", nv_d[trow, :], vf.t[:], reads=[vf], writes=[D_("nv")])
                P.dma("gpsimd", nki_d[trow, :], kif.t[:], reads=[kif], writes=[D_("nki")])
                P.dma("gpsimd", wi_s[trow, :], wis.t[:], reads=[wis], writes=[D_("wi_s")])
                P.dma("scalar", qT_s[:, :, tcol], qTt.t[:], reads=[qTt], writes=[D_("qT_s")])
                P.dma("gpsimd", qiT_s[:, :, tcol], qiTt.t[:], reads=[qiTt], writes=[D_("qiT_s")])
                if tg < NPT:
                    kcol = slice(tg * 128, (tg + 1) * 128)
                    P.dma("scalar", kvmy[0:64, kcol], kiTt.t[:], reads=[kiTt], writes=[D_("kvmy")])
                    P.dma("scalar", kvmy[64:192, kcol], kTt.t[:], reads=[kTt], writes=[D_("kvmy")])
                    P.dma("gpsimd", kvmyv[:, kcol], vb.t[:], reads=[vb], writes=[D_("kvmyv")])
                else:
                    kcol = slice((tg - NPT) * 128, (tg - NPT + 1) * 128)
                    P.dma("scalar", kvsm[0:64, kcol], kiTt.t[:], reads=[kiTt], writes=[D_("kvsm")])
                    P.dma("scalar", kvsm[64:192, kcol], kTt.t[:], reads=[kTt], writes=[D_("kvsm")])
                    P.dma("scalar", kvsm[192:320, kcol], vb.t[:], reads=[vb], writes=[D_("kvsm")])

            if os.environ.get("K_PIPE", "1") == "1":
                zmm(0)
                for tt in range(BLK):
                    if tt + 1 < BLK:
                        zmm(tt + 1)
                    post(tt)
            else:
                for tt in range(BLK):
                    zmm(tt)
                    post(tt)
            if blk == 1:
                P.collective(lambda e: e.collective_compute("AllGather", ALU.bypass, replica_groups=groups,
                                                            ins=[kvmy.opt()], outs=[kvg.opt()]),
                             reads=[D_("kvmy")], writes=[D_("kvg")])
                P.collective(lambda e: e.collective_compute("AllGather", ALU.bypass, replica_groups=groups,
                                                            ins=[kvmyv.opt()], outs=[kvgv.opt()]),
                             reads=[D_("kvmyv")], writes=[D_("kvgv")])
                P.collective(lambda e: e.collective_compute("AllGather", ALU.bypass, replica_groups=groups,
                                                            ins=[ctmy.opt()], outs=[ctg.opt()]),
                             reads=[D_("ctmy")], writes=[D_("ctg")])
            if stage == 0.8:
                P.barrier(); P.finish(); return nc
            def loadg(pp):
                return ST.load([(0, WinV[:, :, 2376 + pp * 256:2376 + (pp + 1) * 256], 256)], 8, 256)
            nxt = loadg(0)
            cnt = 0
            for pp in range(8):
                wbuf, wv = nxt
                if pp + 1 < 8:
                    nxt = loadg(pp + 1)
                for hh in range(2):
                    gc = pp * 2 + hh
                    for sbk in range(NSUB):
                        pa = psb[cnt % 4]
                        cnt += 1
                        cols = slice(sbk * SUB, (sbk + 1) * SUB)
                        for kc in range(8):
                            P.op("tensor", lambda e: e.matmul(pa.t[:, 0:SUB], lhsT=wv[:, kc, hh * 128:(hh + 1) * 128], rhs=xnT.t[:, kc, cols],
                                                              start=(kc == 0), stop=(kc == 7)),
                                 reads=[wbuf, xnT], writes=[pa], signal=(kc == 7))
                        gbuf = gbuf_r.next()
                        P.op("scalar", lambda e: e.activation(out=gbuf.t[:], in_=pa.t[:, 0:SUB], func=AF.Sigmoid, bias=bgT.t[:, gc:gc + 1], scale=1.0),
                             reads=[pa, bgT], writes=[gbuf])
                        P.dma("gpsimd", gate_s[:, gc, tok0 + sbk * SUB: tok0 + (sbk + 1) * SUB], gbuf.t[:], reads=[gbuf], writes=[D_("gate_s")])
        P.pop()
        if stage == 1:
            P.finish()
            return nc

        if stage == 1.1:
            P.barrier(); P.finish(); return nc
        P.push()
        wao = P.sb("wao", [128, 4, D], BF16)
        wco = P.sb("wco", [128, 4, D], BF16)
        amneg = P.sb("amneg", [128, 512], F32)
        ampos = P.sb("ampos", [128, 512], F32)
        P.push()
        ST = Streamer(2048)
        for (wt_, wd_) in ((wao, wao_d), (wco, wco_d)):
            Wv = wd_.rearrange("(kc p) n -> p kc n", p=128)
            for hf in range(2):
                ST.load([(0, Wv[:, :, hf * 512:(hf + 1) * 512], 512)], 4, 512, dst=(wt_, wt_.t[:, :, hf * 512:(hf + 1) * 512]))
        iot_i = P.sb("iot_i", [128, 512], I32)
        P.op("gpsimd", lambda e: e.iota(iot_i.t[:], pattern=[[1, 512]], base=0, channel_multiplier=0), writes=[iot_i])
        am = P.sb("am", [128, 512], F32)
        P.op("vector", lambda e: e.tensor_scalar(out=am.t[:], in0=iot_i.t[:], scalar1=limrel.t[:, 0:1], scalar2=None, op0=ALU.is_ge),
             reads=[iot_i, limrel], writes=[am])
        P.op("vector", lambda e: e.tensor_scalar(out=amneg.t[:], in0=am.t[:], scalar1=-BIG, scalar2=None, op0=ALU.mult), reads=[am], writes=[amneg])
        P.op("vector", lambda e: e.tensor_scalar(out=ampos.t[:], in0=am.t[:], scalar1=BIG, scalar2=None, op0=ALU.mult), reads=[am], writes=[ampos])
        P.pop()
        if stage == 1.2:
            P.barrier(); P.finish(); return nc
        kiT_all = P.sb("kiT_all", [128, 8192], BF16)
        P.op("vector", lambda e: e.memset(kiT_all.t[64:128, :], 0.0), writes=[kiT_all])
        kT_all = P.sb("kT_all", [128, 8192], BF16)
        v1_all = P.sb("v1_all", [128, 64, 2, 65], BF16)
        score = P.sb("score", [128, 8192], F32)
        score_b = [TBuf(f"score_b{i}") for i in range(16)]
        msk2 = [P.sb(f"msk{i}", [128, 8192], BF16) for i in range(2)]
        zeros = P.sb("zeros", [1, 512], BF16)
        P.op("vector", lambda e: e.memset(zeros.t[:], 0.0), writes=[zeros])
        onesM = P.sb("onesM", [128, 128], F32)
        P.op("vector", lambda e: e.memset(onesM.t[:], 1.0 / 512.0), writes=[onesM])

        qiTu_r = Rot([P.sb(f"qiTu{i}", [128, 8, 128], BF16) for i in range(2)])
        for b_ in qiTu_r.bufs:
            P.op("gpsimd", lambda e: e.memset(b_.t[64:128, :, :], 0.0), writes=[b_])
        wiu_r = Rot([P.sb(f"wiu{i}", [128, 8], F32) for i in range(2)])
        qTz_r = Rot([[P.sb(f"qTz{i}_{c}", [128, 4, 128], BF16) for c in range(2)] for i in range(2)])
        for pr in qTz_r.bufs:
            P.op("gpsimd", lambda e: e.memset(pr[0].t[64:128, :, :], 0.0), writes=[pr[0]])
            P.op("gpsimd", lambda e: e.memset(pr[1].t[0:64, :, :], 0.0), writes=[pr[1]])
        ident4 = P.sb("ident4", [128, 4, 128], BF16)
        for g_ in range(4):
            P.op("gpsimd", lambda e: e.tensor_copy(out=ident4.t[:, g_, :], in_=ident.t[:]), reads=[ident], writes=[ident4])
        gateu_r = Rot([P.sb(f"gateu{i}", [128, 16, 128], BF16) for i in range(3)])
        cpad_r = Rot([P.sb(f"cpad{i}", [128, 4, 158], F32) for i in range(3)])
        cand_r = Rot([P.sb(f"cand{i}", [128, 5, 4, 32], F32) for i in range(2)])
        E_r = Rot([P.sb(f"E{i}", [128, 512], BF16) for i in range(3)])
        fvec = P.sb("fvec", [128, NITER + 1], F32)
        for k_ in range(NITER + 1):
            P.op("vector", lambda e: e.memset(fvec.t[:, k_:k_ + 1], float(2.0 ** -(k_ + 1))), writes=[fvec])
        frng = P.sb("frng", [128, NITER + 1], F32)
        nfrng = P.sb("nfrng", [128, NITER + 1], F32)
        sm = {n: P.sb("sm_" + n, [128, 1], F32) for n in ("rmax", "rmin", "mind", "lo", "rng", "mid", "cnt", "ind", "nmid", "cntA", "tcn")}
        rec = P.sb("rec", [128, 8], F32)
        attn_tok = P.sb("attn_tok", [128, 512], BF16)
        attnT = P.sb("attnT", [128, 4, 128], BF16)
        t1 = P.sb("t1", [128, 8, 128], F32)
        t2 = P.sb("t2", [128, 8, 128], F32)
        mTu_r = Rot([P.sb(f"mTu{i}", [128, 8, 128], BF16) for i in range(2)])
        cacc_r = Rot([P.sb(f"cacc{i}", [128, 4, 128], F32) for i in range(3)])
        tmpc = P.sb("tmpc", [128, 4, 128], F32)
        dcen = P.sb("dcen", [128, 4, 128], F32)
        dsq = P.sb("dsq", [128, 4, 128], F32)
        rsb = P.sb("rsb", [128, 128], F32)
        actT = P.sb("actT", [128, 4, 128], BF16)
        ncv = P.sb("ncv", [30, 512], F32)
        scst = ncv

        P.op("vector", lambda e: e.memset(v1_all.t[:], 1.0), writes=[v1_all])
        kiv = kiT_all.t[0:64, :].rearrange("p (i j n) -> p i j n", i=16, j=4)
        ktv = kT_all.t[:].rearrange("p (i j n) -> p i j n", i=16, j=4)
        v1v = v1_all.t[:].rearrange("p (i j) c d -> p i j c d", i=16, j=4)
        for jj in range(4):
            r0 = jj * 192
            P.dma("sync", kiv[:, :, jj, :], kvg[r0:r0 + 64, :].rearrange("p (i n) -> p i n", i=16), reads=[D_("kvg")], writes=[kiT_all])
            P.dma("sync", ktv[:, :, jj, :], kvg[r0 + 64:r0 + 192, :].rearrange("p (i n) -> p i n", i=16), reads=[D_("kvg")], writes=[kT_all])
            for c in range(2):
                P.dma("sync", v1v[:, :, jj, c, 0:64],
                      kvgv[jj * 128:(jj + 1) * 128, :].rearrange("p (i c d) -> p i c d", i=16, c=2)[:, :, c, :],
                      reads=[D_("kvgv")], writes=[v1_all])

        if stage == 1.3:
            P.barrier(); P.finish(); return nc
        kst_v = score.t[:, 4096:5120].rearrange("p (t d) -> p t d", t=8)
        kstb_v = score.t[:, 5120:5632].bitcast(BF16).rearrange("p (t d) -> p t d", t=8)
        kstT = TBuf("kstT")
        kstbT = TBuf("kstbT")

        def sample_prep(s, cx):
            kiB, kTB, v1B, kc0, kt0 = cx["kiB"], cx["kTB"], cx["v1B"], cx["kc0"], cx["kt0"]
            p7 = bfv(7)
            P.dma("sync", kst_v[:, :, 0:64], cik_d[s].rearrange("(t p) d -> p t d", p=128), writes=[kstT])
            P.op("gpsimd", lambda e: e.tensor_copy(out=kstb_v[:, :, 0:64], in_=kst_v[:, :, 0:64]), reads=[kstT], writes=[kstbT])
            for t8 in range(8):
                P.op("tensor", lambda e: e.transpose(out=p7[0:64, t8 * 128:(t8 + 1) * 128], in_=kstb_v[:, t8, 0:64], identity=ident.t[:]),
                     reads=[kstbT, ident], writes=[psb[7]], signal=(t8 == 7))
            P.op("scalar", lambda e: e.activation(out=kiT_all.t[0:64, kc0:kc0 + 1024], in_=p7[0:64, :], func=AF.Copy), reads=[psb[7]], writes=[kiB])
            P.dma("sync", kst_v, ck_d[s].rearrange("(t p) d -> p t d", p=128), writes=[kstT])
            P.op("gpsimd", lambda e: e.tensor_copy(out=kstb_v, in_=kst_v), reads=[kstT], writes=[kstbT])
            for t8 in range(8):
                P.op("tensor", lambda e: e.transpose(out=p7[:, t8 * 128:(t8 + 1) * 128], in_=kstb_v[:, t8, :], identity=ident.t[:]),
                     reads=[kstbT, ident], writes=[psb[7]], signal=(t8 == 7))
            P.op("scalar", lambda e: e.activation(out=kT_all.t[:, kc0:kc0 + 1024], in_=p7[:, :], func=AF.Copy), reads=[psb[7]], writes=[kTB])
            P.dma("sync", kst_v, cv_d[s].rearrange("(t p) d -> p t d", p=128), writes=[kstT])
            P.op("gpsimd", lambda e: e.tensor_copy(out=v1_all.t[:, kt0:kt0 + 8, :, 0:64], in_=kst_v.rearrange("p t (c d) -> p t c d", c=2)),
                 reads=[kstT], writes=[v1B])
            sc = slice(s * 64, (s + 1) * 64)
            P.dma("sync", kiT_all.t[0:64, kc0 + 1024:kc0 + 1088], kvsm[0:64, sc], reads=[D_("kvsm")], writes=[kiB])
            P.dma("sync", kT_all.t[:, kc0 + 1024:kc0 + 1088], kvsm[64:192, sc], reads=[D_("kvsm")], writes=[kTB])
            vr = 192 + (s % 2) * 64
            vc = slice((s // 2) * 128, (s // 2 + 1) * 128)
            P.dma("sync", v1_all.t[0:64, kt0 + 8, :, 0:64], kvsm[vr:vr + 64, vc].rearrange("p (c d) -> p c d", c=2),
                  reads=[D_("kvsm")], writes=[v1B])

        def front(cx):
            tile, col0, nq, L, prompt_slot, seq = cx["args"]
            tok0 = tile * 128 + col0
            tcol = slice(tok0, tok0 + nq)
            nblk = (L + 511) // 512
            qiTu = qiTu_r.next()
            wiu = wiu_r.next()
            qTz = qTz_r.next()
            gateu = gateu_r.next()
            cpad = cpad_r.next()
            cacc = cacc_r.next()
            mk = msk2[cx["idx"] % 2]
            kiB, kc0 = cx.get("kiB", kiT_all), cx.get("kc0", 0)
            if seq is not None:
                sample_prep(seq, cx)
            cx.update(qTz=qTz, gateu=gateu, cpad=cpad, msk=mk, tcol=tcol, cacc=cacc)
            P.dma("sync", qiTu.t[0:64, :, 0:nq], qiT_s[:, :, tcol], reads=[D_("qiT_s")], writes=[qiTu])
            P.dma("sync", wiu.t[0:nq, :], wi_s[tcol, :], reads=[D_("wi_s")], writes=[wiu])
            P.dma("sync", qTz[0].t[0:64, :, 0:nq], qT_s[0:64, :, tcol], reads=[D_("qT_s")], writes=[qTz[0]])
            P.dma("sync", qTz[1].t[64:128, :, 0:nq], qT_s[64:128, :, tcol], reads=[D_("qT_s")], writes=[qTz[1]])
            P.dma("sync", gateu.t[:, :, 0:nq], gate_s[:, :, tcol], reads=[D_("gate_s")], writes=[gateu])
            P.dma("sync", cpad.t[:, :, 30:30 + nq], cT_s[:, :, tcol], reads=[D_("cT_s")], writes=[cpad])
            if prompt_slot is not None:
                i = prompt_slot
                cand = cand_r.next()
                cx["cand"] = cand
                ctv = ctg.rearrange("(r p) (c i n) -> p r c i n", r=4, c=4, i=NPT)
                for r in range(4):
                    P.dma("sync", cand.t[:, r, :, :], ctv[:, r, :, i, :], reads=[D_("ctg")], writes=[cand])
                if i > 0:
                    P.dma("sync", cand.t[:, 4, :, :], ctv[:, 3, :, i - 1, :], reads=[D_("ctg")], writes=[cand])
            else:
                P.dma("sync", scst.t[:], sconv_d[seq], writes=[scst])
            if prompt_slot is not None:
                i = prompt_slot
                pass
                nk_ = 5 if i > 0 else 4
                P.op("vector", lambda e: e.tensor_scalar(out=cpad.t[:, :, 0:30], in0=cand.t[:, 0, :, 2:32], scalar1=sel.t[:, 0:1], scalar2=None,
                                                         op0=ALU.mult), reads=[cand, sel], writes=[cpad])
                for k in range(1, nk_):
                    P.op("vector", lambda e: e.scalar_tensor_tensor(out=cpad.t[:, :, 0:30], in0=cand.t[:, k, :, 2:32], scalar=sel.t[:, k:k + 1],
                                                                    in1=cpad.t[:, :, 0:30], op0=ALU.mult, op1=ALU.add),
                         reads=[cand, sel, cpad], writes=[cpad])
            else:
                for cc in range(4):
                    P.op("tensor", lambda e: e.transpose(out=psb[7].t[:, cc * 32:cc * 32 + 30], in_=scst.t[:, cc * 128:(cc + 1) * 128],
                                                         identity=identF.t[0:30, 0:30]),
                         reads=[scst, identF], writes=[psb[7]], signal=(cc == 3))
                P.op("vector", lambda e: e.tensor_copy(out=cpad.t[:, :, 0:30],
                                                       in_=psb[7].t[:, 0:128].rearrange("p (c n) -> p c n", c=4)[:, :, 0:30]),
                     reads=[psb[7]], writes=[cpad])
            cav = cacc.t[:, :, 0:nq]
            tmv = tmpc.t[:, :, 0:nq]
            P.op("gpsimd", lambda e: e.tensor_tensor(out=cav, in0=cpad.t[:, :, 0:nq], in1=cwT.t[:, :, 0:1].broadcast_to([128, 4, nq]), op=ALU.mult),
                 reads=[cpad, cwT], writes=[cacc])
            P.op("gpsimd", lambda e: e.tensor_tensor(out=cav, in0=cav, in1=cbT.t[:, :].unsqueeze(2).broadcast_to([128, 4, nq]), op=ALU.add),
                 reads=[cacc, cbT], writes=[cacc])
            for k in range(1, 31):
                P.op("gpsimd", lambda e: e.tensor_tensor(out=tmv, in0=cpad.t[:, :, k:k + nq], in1=cwT.t[:, :, k:k + 1].broadcast_to([128, 4, nq]), op=ALU.mult),
                     reads=[cpad, cwT], writes=[tmpc])
                P.op("gpsimd", lambda e: e.tensor_tensor(out=cav, in0=cav, in1=tmv, op=ALU.add), reads=[cacc, tmpc], writes=[cacc])
            yield 0.5
            hb = 0
            for bk0 in range(0, nblk, 4):
                pair = [bk_ for bk_ in range(bk0, bk0 + 4) if bk_ < nblk]
                for h in range(8):
                    for bk in pair:
                        cols = min(512, L - bk * 512)
                        sc_ap = score.t[0:nq, bk * 512: bk * 512 + cols]
                        pbk = psb[hb % 5]
                        hb += 1
                        P.op("tensor", lambda e: e.matmul(pbk.t[0:nq, 0:cols], lhsT=qiTu.t[:, h, 0:nq], rhs=kiT_all.t[:, kc0 + bk * 512: kc0 + bk * 512 + cols],
                                                          start=True, stop=True),
                             reads=[qiTu, kiB], writes=[pbk])
                        P.op("scalar", lambda e: e.activation(out=pbk.t[0:nq, 0:cols], in_=pbk.t[0:nq, 0:cols], func=AF.Relu),
                             reads=[pbk], writes=[pbk])
                        if h == 0:
                            P.op("vector", lambda e: e.tensor_scalar(out=sc_ap, in0=pbk.t[0:nq, 0:cols], scalar1=wiu.t[0:nq, 0:1], scalar2=None,
                                                                     op0=ALU.mult), reads=[pbk, wiu], writes=[score_b[bk]])
                        else:
                            P.op("vector", lambda e: e.scalar_tensor_tensor(out=sc_ap, in0=pbk.t[0:nq, 0:cols], scalar=wiu.t[0:nq, h:h + 1],
                                                                            in1=sc_ap, op0=ALU.mult, op1=ALU.add),
                                 reads=[pbk, wiu, score_b[bk]], writes=[score_b[bk]])
                        yield 0.6 * cols / 512
            sbs = score_b[0:nblk]
            cx["sbs"] = sbs
            if prompt_slot is not None:
                i = prompt_slot
                dg = score.t[0:nq, i * 512:(i + 1) * 512]
                tmpd_v = mk.t[:, 0:1024].bitcast(F32)
                P.sync_to("vector", [mk])
                P.op("vector", lambda e: e.tensor_tensor(out=tmpd_v[0:nq, :], in0=dg, in1=ampos.t[0:nq, :], op=ALU.add),
                     reads=[score_b[i], ampos], writes=[mk])
                P.op("vector", lambda e: e.tensor_reduce(out=sm["mind"].t[0:nq, :], in_=tmpd_v[0:nq, :], axis=AX.X, op=ALU.min),
                     reads=[mk], writes=[sm["mind"]])
                P.op("vector", lambda e: e.tensor_tensor(out=dg, in0=dg, in1=amneg.t[0:nq, :], op=ALU.add),
                     reads=[score_b[i], amneg], writes=[score_b[i]])
                if i > 0:
                    mn_in = score.t[0:nq, 0:i * 512].rearrange("p (a b) -> p a b", b=4)[:, :, 0] if i >= 3 else score.t[0:nq, 0:i * 512]
                    P.op("vector", lambda e: e.tensor_reduce(out=sm["rmin"].t[0:nq, :], in_=mn_in, axis=AX.X, op=ALU.min),
                         reads=sbs, writes=[sm["rmin"]])
                    P.op("vector", lambda e: e.tensor_tensor(out=sm["rmin"].t[0:nq, :], in0=sm["rmin"].t[0:nq, :], in1=sm["mind"].t[0:nq, :], op=ALU.min),
                         reads=[sm["rmin"], sm["mind"]], writes=[sm["rmin"]])
                else:
                    P.op("vector", lambda e: e.tensor_copy(out=sm["rmin"].t[0:nq, :], in_=sm["mind"].t[0:nq, :]), reads=[sm["mind"]], writes=[sm["rmin"]])
            else:
                P.op("vector", lambda e: e.tensor_reduce(out=sm["rmin"].t[0:nq, :], in_=score.t[0:nq, 0:L], axis=AX.X, op=ALU.min),
                     reads=sbs, writes=[sm["rmin"]])
            yield 0.55 * nblk
            mx_in = score.t[0:nq, 0:L].rearrange("p (a b) -> p a b", b=4)[:, :, 0] if (prompt_slot is not None and L >= 2048) else score.t[0:nq, 0:L]
            P.op("vector", lambda e: e.tensor_reduce(out=sm["rmax"].t[0:nq, :], in_=mx_in, axis=AX.X, op=ALU.max),
                 reads=sbs, writes=[sm["rmax"]])
            lo, rng, mid, cnt, ind = (sm[n] for n in ("lo", "rng", "mid", "cnt", "ind"))
            P.op("vector", lambda e: e.tensor_scalar(out=lo.t[0:nq, :], in0=sm["rmin"].t[0:nq, :], scalar1=-0.01, scalar2=None, op0=ALU.add),
                 reads=[sm["rmin"]], writes=[lo])
            P.op("vector", lambda e: e.scalar_tensor_tensor(out=rng.t[0:nq, :], in0=sm["rmax"].t[0:nq, :], scalar=0.01, in1=lo.t[0:nq, :],
                                                            op0=ALU.add, op1=ALU.subtract), reads=[sm["rmax"], lo], writes=[rng])
            yield 0.55 * nblk
            yield "PHASE_B"
            La = 0
            Lh = L - La
            mkD, mkA = TBuf("mkD"), TBuf("mkA")
            P.sync_to("vector", [mk])
            if La:
                P.sync_to("scalar", [mk])
            cntA, tcn, dstp = sm["cntA"], sm["tcn"], sm["ind"]
            P.op("vector", lambda e: e.tensor_scalar(out=frng.t[0:nq, :], in0=fvec.t[0:nq, :], scalar1=rng.t[0:nq, 0:1], scalar2=None, op0=ALU.mult),
                 reads=[fvec, rng], writes=[frng])
            P.op("vector", lambda e: e.tensor_scalar(out=nfrng.t[0:nq, :], in0=fvec.t[0:nq, :], scalar1=rng.t[0:nq, 0:1], scalar2=-1.0, op0=ALU.mult, op1=ALU.mult),
                 reads=[fvec, rng], writes=[nfrng])
            P.op("vector", lambda e: e.tensor_tensor(out=mid.t[0:nq, :], in0=lo.t[0:nq, :], in1=frng.t[0:nq, 0:1], op=ALU.add), reads=[lo, frng], writes=[mid])
            for k in range(1, NITER + 1):
                if La:
                    P.op("scalar", lambda e: e.activation(out=mk.t[0:nq, Lh:L], in_=score.t[0:nq, Lh:L], func=AF.Sign, bias=mid.t[0:nq, 0:1], scale=-1.0,
                                                          accum_out=cntA.t[0:nq, 0:1]),
                         reads=sbs + [mid], writes=[mkA, cntA])
                P.op("vector", lambda e: e.tensor_scalar(out=mk.t[0:nq, 0:Lh], in0=score.t[0:nq, 0:Lh], scalar1=mid.t[0:nq, 0:1], scalar2=None,
                                                         op0=ALU.is_ge, op1=ALU.add, accum_out=cnt.t[0:nq, 0:1]),
                     reads=sbs + [mid], writes=[mkD, cnt])
                if La:
                    P.op("vector", lambda e: e.scalar_tensor_tensor(out=tcn.t[0:nq, :], in0=cnt.t[0:nq, :], scalar=2.0, in1=cntA.t[0:nq, :],
                                                                    op0=ALU.mult, op1=ALU.subtract), reads=[cnt, cntA], writes=[tcn])
                    csrc, thr_ = tcn, float(2 * TOPK - 1 - La)
                else:
                    csrc, thr_ = cnt, TOPK - 0.5
                P.op("vector", lambda e: e.scalar_tensor_tensor(out=dstp.t[0:nq, :], in0=csrc.t[0:nq, :], scalar=thr_, in1=frng.t[0:nq, k - 1:k],
                                                                op0=ALU.is_ge, op1=ALU.mult), reads=[csrc, frng], writes=[dstp])
                if k < NITER:
                    P.op("vector", lambda e: e.scalar_tensor_tensor(out=mid.t[0:nq, :], in0=dstp.t[0:nq, :], scalar=nfrng.t[0:nq, k:k + 1], in1=mid.t[0:nq, :],
                                                                    op0=ALU.add, op1=ALU.add), reads=[dstp, nfrng, mid], writes=[mid])
                else:
                    P.op("vector", lambda e: e.scalar_tensor_tensor(out=lo.t[0:nq, :], in0=dstp.t[0:nq, :], scalar=nfrng.t[0:nq, k - 1:k], in1=mid.t[0:nq, :],
                                                                    op0=ALU.add, op1=ALU.add), reads=[dstp, nfrng, mid], writes=[lo])
                yield 0.45 * nblk + 0.5
            P.op("vector", lambda e: e.tensor_scalar(out=mk.t[0:nq, 0:L], in0=score.t[0:nq, 0:L], scalar1=lo.t[0:nq, 0:1], scalar2=-30000.0,
                                                     op0=ALU.is_lt, op1=ALU.mult), reads=sbs + [lo], writes=[mk, mkD, mkA])
            yield 0.55 * nblk

        def back(cx):
            tile, col0, nq, L, prompt_slot, seq = cx["args"]
            qTz, gateu, cpad, mk, tcol, cacc = cx["qTz"], cx["gateu"], cx["cpad"], cx["msk"], cx["tcol"], cx["cacc"]
            KT = (L + 127) // 128
            kTB, v1B, kc0, kt0 = cx.get("kTB", kT_all), cx.get("v1B", v1_all), cx.get("kc0", 0), cx.get("kt0", 0)
            yield 0.5
            accv = [psb[5], psb[7]]
            for c in range(2):
                P.op("tensor", lambda e: e.matmul(accv[c].t[0:nq, 0:260], lhsT=zeros.t[0:1, 0:nq], rhs=zeros.t[0:1, 0:260], start=True, stop=False,
                                                  skip_group_check=True),
                     reads=[zeros], writes=[accv[c]])
            p6 = bfv(7)
            steps = [(kt, c) for kt in range(KT) for c in range(2)]

            def qk(si):
                kt, c = steps[si]
                ks = min(128, L - kt * 128)
                lgp = psb[3 + si % 2]
                P.op("tensor", lambda e: e.matmul(lgp.t[0:ks, 0:4 * nq], lhsT=kT_all.t[:, kc0 + kt * 128: kc0 + kt * 128 + ks],
                                                  rhs=qTz[c].t[:, :, 0:nq], start=True, stop=False),
                     reads=[kTB, qTz[c]], writes=[lgp], signal=False)
                P.op("tensor", lambda e: e.matmul(lgp.t[0:ks, 0:4 * nq], lhsT=mk.t[0:nq, kt * 128: kt * 128 + ks],
                                                  rhs=ident4.t[0:nq, :, 0:nq], start=False, stop=True),
                     reads=[mk, ident4], writes=[lgp])
            qk(0)
            for si, (kt, c) in enumerate(steps):
                ks = min(128, L - kt * 128)
                if si + 1 < len(steps):
                    qk(si + 1)
                lgp = psb[3 + si % 2]
                Eb = E_r.next()
                P.op("scalar", lambda e: e.activation(out=Eb.t[0:ks, 0:4 * nq], in_=lgp.t[0:ks, 0:4 * nq], func=AF.Exp, scale=0.125),
                     reads=[lgp], writes=[Eb])
                for g in range(4):
                    P.op("tensor", lambda e: e.matmul(accv[c].t[0:nq, g * 65:(g + 1) * 65], lhsT=Eb.t[0:ks, g * nq:(g + 1) * nq],
                                                      rhs=v1_all.t[0:ks, kt0 + kt, c, :], start=False, stop=False, skip_group_check=True),
                         reads=[Eb, v1B], writes=[accv[c]], signal=(g == 3))
                if c == 1:
                    yield 2.9 * (nq / 128.0)
            yield "TAIL"
            for c in range(2):
                P.op("tensor", lambda e: e.matmul(accv[c].t[0:nq, 0:260], lhsT=zeros.t[0:1, 0:nq], rhs=zeros.t[0:1, 0:260], start=False, stop=True,
                                                  skip_group_check=True),
                     reads=[zeros], writes=[accv[c]])
            for c in range(2):
                av = accv[c].t[0:nq, 0:260].rearrange("p (g d) -> p g d", g=4)
                P.op("vector", lambda e: e.reciprocal(out=rec.t[0:nq, c * 4:(c + 1) * 4], in_=av[:, :, 64]), reads=[accv[c]], writes=[rec])
                for g in range(4):
                    hh = c * 4 + g
                    P.op("vector", lambda e: e.tensor_scalar(out=attn_tok.t[0:nq, hh * 64:(hh + 1) * 64], in0=av[:, g, 0:64],
                                                             scalar1=rec.t[0:nq, hh:hh + 1], scalar2=None, op0=ALU.mult),
                         reads=[accv[c], rec], writes=[attn_tok])
            yield 1.0
            for kc in range(4):
                P.op("tensor", lambda e: e.transpose(out=p6[:, kc * 128:kc * 128 + nq], in_=attn_tok.t[0:nq, kc * 128:(kc + 1) * 128],
                                                     identity=ident.t[0:nq, 0:nq]),
                     reads=[attn_tok, ident], writes=[psb[7]], signal=(kc == 3))
            P.op("scalar", lambda e: e.activation(out=attnT.t[:, :, 0:nq], in_=p6[:, 0:512].rearrange("p (k n) -> p k n", k=4)[:, :, 0:nq], func=AF.Copy),
                 reads=[psb[7]], writes=[attnT])
            for b4 in range(2):
                for n in range(b4 * 4, b4 * 4 + 4):
                    for kc in range(4):
                        P.op("tensor", lambda e: e.matmul(psb[7].t[:, (n % 4) * 128:(n % 4) * 128 + nq], lhsT=wao.t[:, kc, n * 128:(n + 1) * 128],
                                                          rhs=attnT.t[:, kc, 0:nq], start=(kc == 0), stop=(kc == 3)),
                             reads=[wao, attnT], writes=[psb[7]], signal=(kc == 3))
                P.op("vector", lambda e: e.tensor_tensor(out=t1.t[:, b4 * 4:(b4 + 1) * 4, 0:nq],
                                                         in0=psb[7].t[:, 0:512].rearrange("p (k n) -> p k n", k=4)[:, :, 0:nq],
                                                         in1=gateu.t[:, b4 * 4:(b4 + 1) * 4, 0:nq], op=ALU.mult),
                     reads=[psb[7], gateu], writes=[t1])
                yield 0.8
            pst = psb[7]
            for cc in range(4):
                P.op("tensor", lambda e: e.matmul(pst.t[:, 0:nq], lhsT=onesM.t[:], rhs=cacc.t[:, cc, 0:nq], start=(cc == 0), stop=(cc == 3)),
                     reads=[onesM, cacc], writes=[pst], signal=(cc == 3))
            for cc in range(4):
                P.op("vector", lambda e: e.tensor_tensor(out=dcen.t[:, cc, 0:nq], in0=cacc.t[:, cc, 0:nq], in1=pst.t[:, 0:nq], op=ALU.subtract),
                     reads=[cacc, pst], writes=[dcen])
            P.op("scalar", lambda e: e.activation(out=dsq.t[:, :, 0:nq], in_=dcen.t[:, :, 0:nq], func=AF.Square), reads=[dcen], writes=[dsq])
            for cc in range(4):
                P.op("tensor", lambda e: e.matmul(pst.t[:, 128:128 + nq], lhsT=onesM.t[:], rhs=dsq.t[:, cc, 0:nq], start=(cc == 0), stop=(cc == 3)),
                     reads=[onesM, dsq], writes=[pst], signal=(cc == 3))
            P.op("scalar", lambda e: e.activation(out=rsb.t[:, 0:nq], in_=pst.t[:, 128:128 + nq], func=AF.Sqrt, bias=epsc.t[:, 0:1], scale=1.0),
                 reads=[pst, epsc], writes=[rsb])
            P.op("vector", lambda e: e.reciprocal(out=rsb.t[:, 0:nq], in_=rsb.t[:, 0:nq]), reads=[rsb], writes=[rsb])
            P.op("vector", lambda e: e.tensor_tensor(out=dcen.t[:, :, 0:nq], in0=dcen.t[:, :, 0:nq],
                                                     in1=rsb.t[:, 0:nq].unsqueeze(1).broadcast_to([128, 4, nq]), op=ALU.mult),
                 reads=[dcen, rsb], writes=[dcen])
            for cc in range(4):
                P.op("scalar", lambda e: e.activation(out=actT.t[:, cc, 0:nq], in_=dcen.t[:, cc, 0:nq], func=AF.Silu,
                                                      bias=lnbT.t[:, cc:cc + 1], scale=lngT.t[:, cc:cc + 1]),
                     reads=[dcen, lnbT, lngT], writes=[actT])
            yield 2.0
            mTu = mTu_r.next()
            for b4 in range(2):
                for n in range(b4 * 4, b4 * 4 + 4):
                    for kc in range(4):
                        P.op("tensor", lambda e: e.matmul(psb[7].t[:, (n % 4) * 128:(n % 4) * 128 + nq], lhsT=wco.t[:, kc, n * 128:(n + 1) * 128],
                                                          rhs=actT.t[:, kc, 0:nq], start=(kc == 0), stop=(kc == 3)),
                             reads=[wco, actT], writes=[psb[7]], signal=(kc == 3))
                P.op("vector", lambda e: e.tensor_tensor(out=t2.t[:, b4 * 4:(b4 + 1) * 4, 0:nq],
                                                         in0=psb[7].t[:, 0:512].rearrange("p (k n) -> p k n", k=4)[:, :, 0:nq],
                                                         in1=gateu.t[:, 8 + b4 * 4:8 + (b4 + 1) * 4, 0:nq], op=ALU.mult),
                     reads=[psb[7], gateu], writes=[t2])
                yield 0.8
            P.op("gpsimd", lambda e: e.tensor_tensor(out=mTu.t[:, :, 0:nq], in0=t1.t[:, :, 0:nq], in1=t2.t[:, :, 0:nq], op=ALU.add),
                 reads=[t1, t2], writes=[mTu])
            P.dma("sync", mT_s[:, :, tcol], mTu.t[:, :, 0:nq], reads=[mTu], writes=[D_("mT_s")])
            if seq is not None or prompt_slot == NPT - 1:
                for cc in range(4):
                    P.op("tensor", lambda e: e.transpose(out=psb[7].t[0:30, cc * 128:(cc + 1) * 128], in_=cpad.t[:, cc, nq:nq + 30], identity=identF.t[:]),
                         reads=[cpad, identF], writes=[psb[7]], signal=(cc == 3))
                P.op("vector", lambda e: e.tensor_copy(out=ncv.t[:], in_=psb[7].t[0:30, :]), reads=[psb[7]], writes=[ncv])
                if seq is not None:
                    P.dma("sync", ncs_d[seq], ncv.t[:], reads=[ncv], writes=[D_("ncs")])
                else:
                    P.dma("sync", ncp_d, ncv.t[:], reads=[ncv], writes=[D_("ncp")])
            yield 0.5

        def run_until(gen, marker):
            for v in gen:
                if v == marker:
                    return

        def drain(gen):
            for _ in gen:
                pass

        def interleave(ga, gb):
            ta = tb = 0.0
            a_alive = b_alive = True
            while a_alive or b_alive:
                if a_alive and (not b_alive or ta <= tb):
                    try:
                        ta += next(ga)
                    except StopIteration:
                        a_alive = False
                else:
                    try:
                        tb += next(gb)
                    except StopIteration:
                        b_alive = False

        def until(gen, marker):
            for v in gen:
                if v == marker:
                    return
                yield v if not isinstance(v, str) else 0.0

        def run_pipeline(units, overlap_tail=True):
            cxs = []
            for i, a in enumerate(units):
                cx_ = {"args": a[0:6], "idx": i}
                if len(a) > 6:
                    cx_.update(a[6])
                cxs.append(cx_)
            n = len(cxs)
            fgs = {0: front(cxs[0])}
            drain(until(fgs[0], "PHASE_B"))
            drain(fgs[0])
            tail = None
            for u in range(n):
                bg = back(cxs[u])
                if u + 1 < n:
                    fg = front(cxs[u + 1])
                    s_phase = until(fg, "PHASE_B")
                    if tail is not None and overlap_tail:
                        interleave(s_phase, tail)
                    else:
                        if tail is not None:
                            drain(tail)
                        drain(s_phase)
                    interleave(fg, until(bg, "TAIL"))
                else:
                    if tail is not None:
                        drain(tail)
                    drain(until(bg, "TAIL"))
                tail = bg
            drain(tail)

        run_pipeline([(i, 0, 128, 512 * (i + 1), i, None) for i in range(NPT)])
        P.barrier()
        sunits = []
        regs = [{"kiB": TBuf(f"kiS{r_}"), "kTB": TBuf(f"kTS{r_}"), "v1B": TBuf(f"v1S{r_}"), "kc0": r_ * 2048, "kt0": r_ * 16} for r_ in range(2)]
        for s_ in range(4):
            sunits.append((NPT + s_ // 2, (s_ % 2) * 64, 64, 1088, None, s_, regs[s_ % 2]))
        run_pipeline(sunits, overlap_tail=False)
        P.pop()
        if stage == 2:
            P.finish()
            return nc

        P.push()
        alloc_norm_tmp()
        g2B = P.sb("g2B", [128, D], F32)
        P.dma("sync", g2B.t[:], g2_d, writes=[g2B])
        gfB = P.sb("gfB", [128, D], F32)
        P.dma("sync", gfB.t[:], gf_d, writes=[gfB])
        xnT = P.sb("xnT", [128, 8, TBK], BF16)
        gT = P.sb("gT", [128, 22, TBK], BF16)
        hblk = P.sb("hblk", [128, BLK, D], F32)
        ST = Streamer(2816)
        sA_r = Rot([P.sb(f"sA{i}", [128, SUB], F32) for i in range(2)])
        oT_r = Rot([P.sb(f"oT{i}", [128, SUB], F32) for i in range(2)])
        yt_r = Rot([P.sb(f"yt{i}", [128, D], F32) for i in range(2)])
        for blk in range(2):
            tok0 = blk * TBK
            P.dma("sync", xnT.t[:], mT_s[:, :, tok0:tok0 + TBK], reads=[D_("mT_s")], writes=[xnT])
            for tt in range(BLK):
                P.dma("sync", hblk.t[:, tt, :], h_s[tok0 + tt * 128: tok0 + (tt + 1) * 128, :], reads=[D_("h_s")], writes=[hblk])
            proj_back(ST, xnT, 8, wo_d, hblk, 1.0)
            for tt in range(BLK):
                norm_transpose(hblk.t[:, tt, :], hblk, g2B, xnT, tt * 128, 6 + tt % 2)
            ffn_in(ST, xnT, gT, w2i_d)
            proj_back(ST, gT, 22, w2o_d, hblk, 0.5)
            for tt in range(BLK):
                rstd = rstd_of(hblk.t[:, tt, :], hblk)
                yt = yt_r.next()
                P.op("vector", lambda e: e.scalar_tensor_tensor(out=yt.t[:], in0=hblk.t[:, tt, :], scalar=rstd.t[:, 0:1], in1=gfB.t[:],
                                                                op0=ALU.mult, op1=ALU.mult), reads=[hblk, rstd, gfB], writes=[yt])
                P.dma("sync", y_d[tok0 + tt * 128: tok0 + (tt + 1) * 128, :], yt.t[:], reads=[yt], writes=[D_("y")])
        P.pop()
        P.finish()
    return nc


_NC_CACHE = {}


def _rope_table(pos):
    inv = (10000.0 ** (-np.arange(0, 64, 2, dtype=np.float32) / np.float32(64))).astype(np.float32)
    ang = pos.astype(np.float32)[:, None] * inv[None, :]
    return np.concatenate([np.cos(ang), np.sin(ang)], axis=-1).astype(np.float32)


def kernel(x_prompt, x_sample, cache_k, cache_v, cache_idx_k, state_conv,
           ffn1_norm, ffn1_w_in, ffn1_w_out, mix_norm, w_in, b_gate,
           conv_w, conv_b, conv_ln_g, conv_ln_b, conv_w_out, attn_w_out, w_out,
           ffn2_norm, ffn2_w_in, ffn2_w_out, final_norm):
    f = lambda a: np.ascontiguousarray(np.asarray(a, dtype=np.float32))
    x_prompt, x_sample = f(x_prompt), f(x_sample)
    cache_k, cache_v, cache_idx_k, state_conv = f(cache_k), f(cache_v), f(cache_idx_k), f(state_conv)
    if "nc" not in _NC_CACHE:
        _NC_CACHE["nc"] = build_program()
    nc = _NC_CACHE["nc"]

    def bc(v):
        return np.ascontiguousarray(np.broadcast_to(f(v).reshape(1, D), (128, D)))

    def colT(v, n):
        return np.ascontiguousarray(f(v).reshape(n, 128).T)

    shared = {
        "g1": bc(ffn1_norm[0]), "gm": bc(mix_norm[0]), "g2": bc(ffn2_norm[0]), "gf": bc(final_norm),
        "w1i": f(ffn1_w_in[0]), "w1o": f(ffn1_w_out[0]), "win": f(w_in[0]),
        "bgT": colT(b_gate[0], 16),
        "cwT": np.ascontiguousarray(f(conv_w[0]).reshape(31, 4, 128).transpose(2, 1, 0)),
        "cbT": colT(conv_b[0], 4), "lngT": colT(conv_ln_g[0], 4), "lnbT": colT(conv_ln_b[0], 4),
        "wco": f(conv_w_out[0]), "wao": f(attn_w_out[0]), "wo": f(w_out[0]),
        "w2i": f(ffn2_w_in[0]), "w2o": f(ffn2_w_out[0]),
    }
    p = np.arange(128)
    in_maps = []
    for c in range(8):
        b, j = c // 4, c % 4
        xs = [x_prompt[b, (4 * i + j) * 128:(4 * i + j + 1) * 128] for i in range(NPT)]
        xs += [x_sample[4 * c + s] for s in range(4)]
        cs = np.zeros((128, NT, 64), np.float32)
        for i in range(NPT):
            cs[:, i, :] = _rope_table((4 * i + j) * 128 + p)
        for t in range(2):
            cs[:, NPT + t, :] = _rope_table(1024 + (p % 64))
        lim = (128 * j + 64 + 64 * (p >= 64)).astype(np.float32).reshape(128, 1)
        sel = np.zeros((128, 5), np.float32)
        if j >= 1:
            sel[:, j - 1] = 1.0
        else:
            sel[:, 4] = 1.0
        m = dict(shared)
        m.update({
            "x": np.ascontiguousarray(np.concatenate(xs, axis=0)),
            "ck": np.ascontiguousarray(cache_k[0, 4 * c:4 * c + 4].reshape(4, 1024, 128)),
            "cv": np.ascontiguousarray(cache_v[0, 4 * c:4 * c + 4].reshape(4, 1024, 128)),
            "cik": np.ascontiguousarray(cache_idx_k[0, 4 * c:4 * c + 4]),
            "sconv": np.ascontiguousarray(state_conv[0, 4 * c:4 * c + 4]),
            "cs": cs, "limrel": lim, "sel": sel,
        })
        in_maps.append(m)

    res = run_bass_kernel_spmd(nc, in_maps, core_ids=list(range(8)))
    R = res.results
    y_p = np.zeros((2, 8192, D), np.float32)
    y_s = np.zeros((32, 64, D), np.float32)
    nk_p = np.zeros((1, 2, 8192, 2, 64), np.float32)
    nv_p = np.zeros((1, 2, 8192, 2, 64), np.float32)
    ni_p = np.zeros((1, 2, 8192, 64), np.float32)
    nc_p = np.zeros((1, 2, 30, 512), np.float32)
    nk_s = np.zeros((1, 32, 64, 2, 64), np.float32)
    nv_s = np.zeros((1, 32, 64, 2, 64), np.float32)
    ni_s = np.zeros((1, 32, 64, 64), np.float32)
    nc_s = np.zeros((1, 32, 30, 512), np.float32)
    for c in range(8):
        b, j = c // 4, c % 4
        r = R[c]
        for i in range(NPT):
            g = slice((4 * i + j) * 128, (4 * i + j + 1) * 128)
            l = slice(i * 128, (i + 1) * 128)
            y_p[b, g] = r["y"][l]
            nk_p[0, b, g] = r["nk"][l].reshape(128, 2, 64)
            nv_p[0, b, g] = r["nv"][l].reshape(128, 2, 64)
            ni_p[0, b, g] = r["nki"][l]
        for s in range(4):
            l = slice(NPT * 128 + s * 64, NPT * 128 + (s + 1) * 64)
            y_s[4 * c + s] = r["y"][l]
            nk_s[0, 4 * c + s] = r["nk"][l].reshape(64, 2, 64)
            nv_s[0, 4 * c + s] = r["nv"][l].reshape(64, 2, 64)
            ni_s[0, 4 * c + s] = r["nki"][l]
            nc_s[0, 4 * c + s] = r["ncs"][s]
        if j == 3:
            nc_p[0, b] = r["ncp"]
    return (y_p, y_s, nk_p, nv_p, ni_p, nc_p, nk_s, nv_s, ni_s, nc_s)
```
